# Optimizing a Trainium2 kernel written in Bass

```python
import math
import jax, jax.numpy as jnp
from jax import lax
import numpy as np

D_MODEL = 2048
BATCH = 2
SEQ = 4096
DEPTH = 1

MIX_WIDTH = D_MODEL
HGRN_WIDTH = MIX_WIDTH // 2
HGRN_HEAD_DIM = 128
HGRN_HEADS = HGRN_WIDTH // HGRN_HEAD_DIM
MLSTM_WIDTH = MIX_WIDTH - HGRN_WIDTH
MLSTM_HEADS = 4
MLSTM_HEAD_DIM = MLSTM_WIDTH // MLSTM_HEADS
CHUNK = 64
CONV_WIDTH = 5
D_FF = 256 * ((8 * D_MODEL // 3 + 255) // 256)
DN_ALPHA = (2.0 * DEPTH) ** 0.25
DN_BETA = (8.0 * DEPTH) ** -0.25
LN_EPS = 1e-5
NORM_EPS = 1e-6
M_INIT = -1e30
IN_SPLITS = (HGRN_WIDTH, HGRN_WIDTH, HGRN_WIDTH, HGRN_WIDTH, HGRN_WIDTH,
             MLSTM_WIDTH, MLSTM_WIDTH, MLSTM_WIDTH, MLSTM_WIDTH,
             MLSTM_HEADS, MLSTM_HEADS, MLSTM_HEADS, MLSTM_HEADS)
IN_COLS = 5 * HGRN_WIDTH + 4 * MLSTM_WIDTH + 4 * MLSTM_HEADS

kernel_name = 'bidir_hgrn2_mlstm_macaron_deepnorm'


def _layer_norm(x, g, b):
    xf = x.astype(jnp.float32)
    mu = jnp.mean(xf, axis=-1, keepdims=True)
    var = jnp.mean(jnp.square(xf - mu), axis=-1, keepdims=True)
    y = (xf - mu) * lax.rsqrt(var + LN_EPS)
    return (y * g.astype(jnp.float32) + b.astype(jnp.float32)).astype(x.dtype)


def _swiglu(x, w1, w3, w2):
    return (jax.nn.silu(x @ w1) * (x @ w3)) @ w2


def _to_heads(t, n):
    b_, t_, w = t.shape
    return t.reshape(b_, t_, n, w // n).transpose(0, 2, 1, 3)


def _merge_heads(t):
    b_, h_, t_, d = t.shape
    return t.transpose(0, 2, 1, 3).reshape(b_, t_, h_ * d)


def _chunk(t):
    b_, h_, t_ = t.shape[:3]
    return jnp.moveaxis(t.reshape(b_, h_, t_ // CHUNK, CHUNK, *t.shape[3:]), 2, 0)


def _unchunk(t):
    t = jnp.moveaxis(t, 0, 2)
    return t.reshape(t.shape[0], t.shape[1], t.shape[2] * t.shape[3], *t.shape[4:])


def _flip(t):
    return jnp.flip(t, axis=2)


def _hgrn2_scan(q, k, v, logf):
    b_, h_, _, dk = q.shape
    dv = v.shape[-1]
    mask = jnp.tril(jnp.ones((CHUNK, CHUNK), dtype=bool))

    def step(state, inp):
        qc, kc, vc, gc = inp
        bcum = jnp.cumsum(gc, axis=-2)
        o_inter = jnp.einsum('bhtd,bhde->bhte', qc * jnp.exp(bcum), state)
        rel = bcum[..., :, None, :] - bcum[..., None, :, :]
        decay = jnp.exp(jnp.where(mask[:, :, None], rel, -jnp.inf))
        attn = jnp.einsum('bhtd,bhsd,bhtsd->bhts', qc, kc, decay)
        o = o_inter + jnp.einsum('bhts,bhse->bhte', attn, vc)
        b_last = bcum[..., -1:, :]
        state = (jnp.exp(b_last[..., 0, :])[..., None] * state
                 + jnp.einsum('bhsd,bhse->bhde', kc * jnp.exp(b_last - bcum), vc))
        return state, o

    s0 = jnp.zeros((b_, h_, dk, dv), jnp.float32)
    _, o = lax.scan(step, s0, (_chunk(q), _chunk(k), _chunk(v), _chunk(logf)))
    return _unchunk(o)


def _mlstm_scan(q, k, v, ig, lf):
    b_, h_, _, dk = q.shape
    dv = v.shape[-1]
    mask = jnp.tril(jnp.ones((CHUNK, CHUNK), dtype=bool))

    def step(carry, inp):
        c_st, n_st, m_st = carry
        qc, kc, vc, igc, lfc = inp
        bcum = jnp.cumsum(lfc, axis=-1)
        dmat = jnp.where(mask, bcum[..., :, None] - bcum[..., None, :] + igc[..., None, :], -jnp.inf)
        m_inter = bcum + m_st[..., None]
        m_t = jnp.maximum(m_inter, jnp.max(dmat, axis=-1))
        inter_scale = jnp.exp(m_inter - m_t)
        sc = jnp.einsum('bhtd,bhsd->bhts', qc, kc) * jnp.exp(dmat - m_t[..., None])
        num = jnp.einsum('bhts,bhse->bhte', sc, vc) + inter_scale[..., None] * jnp.einsum('bhtd,bhde->bhte', qc, c_st)
        den = jnp.sum(sc, axis=-1) + inter_scale * jnp.einsum('bhtd,bhd->bht', qc, n_st)
        h = num / jnp.maximum(jnp.abs(den), jnp.exp(-m_t))[..., None]
        b_last = bcum[..., -1]
        w = b_last[..., None] - bcum + igc
        m_new = jnp.maximum(b_last + m_st, jnp.max(w, axis=-1))
        carry_scale = jnp.exp(b_last + m_st - m_new)
        kw = kc * jnp.exp(w - m_new[..., None])[..., None]
        c_new = carry_scale[..., None, None] * c_st + jnp.einsum('bhsd,bhse->bhde', kw, vc)
        n_new = carry_scale[..., None] * n_st + jnp.sum(kw, axis=-2)
        return (c_new, n_new, m_new), h

    carry0 = (jnp.zeros((b_, h_, dk, dv), jnp.float32),
              jnp.zeros((b_, h_, dk), jnp.float32),
              jnp.full((b_, h_), M_INIT, jnp.float32))
    _, h = lax.scan(step, carry0, (_chunk(q), _chunk(k), _chunk(v), _chunk(ig), _chunk(lf)))
    return _unchunk(h)


def _centred_dwconv(t, w, b):
    y = lax.conv_general_dilated(t, w[:, None, :], window_strides=(1,),
                                 padding=[(CONV_WIDTH // 2, CONV_WIDTH // 2)],
                                 dimension_numbers=('NWC', 'WIO', 'NWC'),
                                 feature_group_count=t.shape[-1])
    return y + b


def _mixer(x, layer, w_in, hgrn_lb, hgrn_norm_g, conv_w, conv_b, ig_b, fg_b, mlstm_norm_g, w_out):
    f32 = jnp.float32
    z = x @ w_in
    offsets = [int(o) for o in np.cumsum(IN_SPLITS)[:-1]]
    (hq, hi, hg, hf_fw, hf_bw, mq, mk, mv, mo, mi_fw, mi_bw, mf_fw, mf_bw) = jnp.split(z, offsets, axis=-1)

    lb = jnp.cumsum(jax.nn.softmax(hgrn_lb.astype(f32), axis=1), axis=1)[:, layer]
    q_h = _to_heads(jax.nn.silu(hq.astype(f32)) * (HGRN_HEAD_DIM ** -0.5), HGRN_HEADS)
    v_h = _to_heads(hi.astype(f32), HGRN_HEADS)

    def _forget(zf, lbd):
        f = lbd + (1.0 - lbd) * jax.nn.sigmoid(zf.astype(f32))
        return _to_heads(jnp.log(f), HGRN_HEADS), _to_heads(1.0 - f, HGRN_HEADS)

    logf_fw, k_fw = _forget(hf_fw, lb[0])
    logf_bw, k_bw = _forget(hf_bw, lb[1])
    o_h = (_hgrn2_scan(q_h, k_fw, v_h, logf_fw)
           + _flip(_hgrn2_scan(_flip(q_h), _flip(k_bw), _flip(v_h), _flip(logf_bw))))
    o_h = o_h * lax.rsqrt(jnp.mean(jnp.square(o_h), axis=-1, keepdims=True) + NORM_EPS)
    y_h = _merge_heads(o_h) * hgrn_norm_g.astype(f32) * jax.nn.silu(hg.astype(f32))

    qk = jax.nn.silu(_centred_dwconv(jnp.concatenate([mq, mk], axis=-1), conv_w, conv_b))
    mq_c, mk_c = jnp.split(qk, 2, axis=-1)
    q_m = _to_heads(mq_c.astype(f32), MLSTM_HEADS) * (MLSTM_HEAD_DIM ** -0.5)
    k_m = _to_heads(mk_c.astype(f32), MLSTM_HEADS)
    v_m = _to_heads(mv.astype(f32), MLSTM_HEADS)
    ig_fw = (mi_fw.astype(f32) + ig_b[0].astype(f32)).transpose(0, 2, 1)
    ig_bw = (mi_bw.astype(f32) + ig_b[1].astype(f32)).transpose(0, 2, 1)
    lf_fw = jax.nn.log_sigmoid(mf_fw.astype(f32) + fg_b[0].astype(f32)).transpose(0, 2, 1)
    lf_bw = jax.nn.log_sigmoid(mf_bw.astype(f32) + fg_b[1].astype(f32)).transpose(0, 2, 1)
    h_m = (_mlstm_scan(q_m, k_m, v_m, ig_fw, lf_fw)
           + _flip(_mlstm_scan(_flip(q_m), _flip(k_m), _flip(v_m), _flip(ig_bw), _flip(lf_bw))))
    mu = jnp.mean(h_m, axis=-1, keepdims=True)
    var = jnp.mean(jnp.square(h_m - mu), axis=-1, keepdims=True)
    h_m = (h_m - mu) * lax.rsqrt(var + NORM_EPS)
    y_m = _merge_heads(h_m) * mlstm_norm_g.astype(f32) * jax.nn.sigmoid(mo.astype(f32))

    y = jnp.concatenate([y_h, y_m], axis=-1).astype(x.dtype)
    return y @ w_out


def setup_inputs(seed: int = 0) -> dict:
    key = jax.random.key(seed)
    ks = jax.random.split(key, 24)
    f32 = jnp.float32
    d_sc = D_MODEL ** -0.5
    ff_sc = D_FF ** -0.5

    def nrm(k, shape, scale):
        return jax.random.normal(k, shape, f32) * scale

    col_scale = jnp.concatenate([
        jnp.ones((HGRN_WIDTH,), f32),
        jnp.full((HGRN_WIDTH,), DN_BETA, f32),
        jnp.ones((3 * HGRN_WIDTH,), f32),
        jnp.ones((2 * MLSTM_WIDTH,), f32),
        jnp.full((MLSTM_WIDTH,), DN_BETA, f32),
        jnp.ones((MLSTM_WIDTH,), f32),
        jnp.full((4 * MLSTM_HEADS,), 0.1, f32),
    ])
    fg_bias = jnp.broadcast_to(jnp.linspace(3.0, 6.0, MLSTM_HEADS, dtype=f32), (DEPTH, 2, MLSTM_HEADS))
    return {
        'x': jax.random.normal(ks[0], (BATCH, SEQ, D_MODEL), f32),
        'ffn1_w1': nrm(ks[1], (DEPTH, D_MODEL, D_FF), d_sc),
        'ffn1_w3': nrm(ks[2], (DEPTH, D_MODEL, D_FF), d_sc),
        'ffn1_w2': nrm(ks[3], (DEPTH, D_FF, D_MODEL), ff_sc * DN_BETA),
        'ln1_g': 1.0 + nrm(ks[4], (DEPTH, D_MODEL), 0.02),
        'ln1_b': nrm(ks[5], (DEPTH, D_MODEL), 0.02),
        'w_in': nrm(ks[6], (DEPTH, D_MODEL, IN_COLS), d_sc) * col_scale,
        'hgrn_lb': nrm(ks[7], (2, DEPTH + 1, HGRN_WIDTH), 0.1),
        'hgrn_norm_g': 1.0 + nrm(ks[8], (DEPTH, HGRN_WIDTH), 0.02),
        'mlstm_conv_w': nrm(ks[9], (DEPTH, CONV_WIDTH, 2 * MLSTM_WIDTH), CONV_WIDTH ** -0.5),
        'mlstm_conv_b': nrm(ks[10], (DEPTH, 2 * MLSTM_WIDTH), 0.02),
        'mlstm_ig_b': nrm(ks[11], (DEPTH, 2, MLSTM_HEADS), 0.1),
        'mlstm_fg_b': fg_bias + nrm(ks[12], (DEPTH, 2, MLSTM_HEADS), 0.1),
        'mlstm_norm_g': 1.0 + nrm(ks[13], (DEPTH, MLSTM_WIDTH), 0.02),
        'w_out': nrm(ks[14], (DEPTH, MIX_WIDTH, D_MODEL), (MIX_WIDTH ** -0.5) * DN_BETA),
        'ln2_g': 1.0 + nrm(ks[15], (DEPTH, D_MODEL), 0.02),
        'ln2_b': nrm(ks[16], (DEPTH, D_MODEL), 0.02),
        'ffn2_w1': nrm(ks[17], (DEPTH, D_MODEL, D_FF), d_sc),
        'ffn2_w3': nrm(ks[18], (DEPTH, D_MODEL, D_FF), d_sc),
        'ffn2_w2': nrm(ks[19], (DEPTH, D_FF, D_MODEL), ff_sc * DN_BETA),
        'ln3_g': 1.0 + nrm(ks[20], (DEPTH, D_MODEL), 0.02),
        'ln3_b': nrm(ks[21], (DEPTH, D_MODEL), 0.02),
    }


def reference(x, ffn1_w1, ffn1_w3, ffn1_w2, ln1_g, ln1_b, w_in, hgrn_lb, hgrn_norm_g,
              mlstm_conv_w, mlstm_conv_b, mlstm_ig_b, mlstm_fg_b, mlstm_norm_g, w_out,
              ln2_g, ln2_b, ffn2_w1, ffn2_w3, ffn2_w2, ln3_g, ln3_b):
    for l in range(DEPTH):
        x = _layer_norm(x * DN_ALPHA + 0.5 * _swiglu(x, ffn1_w1[l], ffn1_w3[l], ffn1_w2[l]), ln1_g[l], ln1_b[l])
        y = _mixer(x, l, w_in[l], hgrn_lb, hgrn_norm_g[l], mlstm_conv_w[l], mlstm_conv_b[l],
                   mlstm_ig_b[l], mlstm_fg_b[l], mlstm_norm_g[l], w_out[l])
        x = _layer_norm(x * DN_ALPHA + y, ln2_g[l], ln2_b[l])
        x = _layer_norm(x * DN_ALPHA + 0.5 * _swiglu(x, ffn2_w1[l], ffn2_w3[l], ffn2_w2[l]), ln3_g[l], ln3_b[l])
    return x
```

```python
import numpy as np
import concourse.bass as bass
import concourse.mybir as mybir
from concourse.bass_utils import run_bass_kernel_spmd

F32 = mybir.dt.float32
BF16 = mybir.dt.bfloat16
AF = mybir.ActivationFunctionType
ALU = mybir.AluOpType
AX = mybir.AxisListType

D = 2048
DFF = 5632
NT = 1024
NTT = 8
KT = 16
INC = 9232
ALPHA = 2.0 ** 0.25
LN_EPS = 1e-5
NORM_EPS = 1e-6
NCH = DFF // 512
HG_OFF = 0
ML_OFF = 16 * 129
ST_COLS = 16 * 129 + 8 * 516


class Buf:
    __slots__ = ("name", "w", "r")

    def __init__(self, name):
        self.name = name
        self.w = None
        self.r = {}


class Sched:
    def __init__(self, nc):
        self.nc = nc
        self.E = {"pe": nc.tensor, "act": nc.scalar, "dve": nc.vector, "pool": nc.gpsimd, "sp": nc.sync}
        self.sem = {e: nc.alloc_semaphore("e_" + e) for e in self.E}
        self.cnt = {e: 0 for e in self.E}
        self.seen = {e: {} for e in self.E}
        self.dsem = {}
        self.nbuf = 0

    def buf(self, name=None):
        self.nbuf += 1
        return Buf(name or ("b%d" % self.nbuf))

    def _deps(self, reads, writes):
        d = []
        for b in reads:
            if b.w is not None:
                d.append(b.w)
        for b in writes:
            if b.w is not None:
                d.append(b.w)
            d.extend(b.r.values())
        return d

    def _wait(self, e, toks):
        best = {}
        for t in toks:
            k = t[2]
            if self.seen[e].get(k, 0) >= t[1]:
                continue
            if k not in best or best[k][1] < t[1]:
                best[k] = t
        for k, t in best.items():
            self.E[e].wait_ge(t[0], t[1])
            self.seen[e][k] = t[1]

    def _commit(self, tok, reads, writes):
        for b in reads:
            b.r[tok[2]] = tok
        for b in writes:
            b.w = tok
            b.r = {}

    def op(self, e, fn, reads=(), writes=()):
        self._wait(e, self._deps(reads, writes))
        ins = fn(self.E[e])
        self.cnt[e] += 1
        ins.then_inc(self.sem[e], 1)
        tok = (self.sem[e], self.cnt[e], e)
        self._commit(tok, reads, writes)
        return tok

    def dma(self, q, out, in_, reads=(), writes=(), key=None, slow=False):
        self._wait(q, self._deps(reads, writes))
        if key not in self.dsem:
            self.dsem[key] = [self.nc.alloc_semaphore("d_" + key), 0]
        s = self.dsem[key]
        s[1] += 16
        if slow:
            self.E[q].dma_start(out=out, in_=in_, allow_slow_non_contiguous=True).then_inc(s[0], 16)
        else:
            self.E[q].dma_start(out=out, in_=in_).then_inc(s[0], 16)
        tok = (s[0], s[1], "d_" + key)
        self._commit(tok, reads, writes)
        return tok

    def cc(self, src, dst, groups, reads=(), writes=(), key="cc"):
        q = "pool"
        self._wait(q, self._deps(reads, writes))
        sem = self.nc.alloc_semaphore("c_" + key)
        self.E[q].collective_compute("AllGather", ALU.bypass, replica_groups=groups,
                                     ins=[src], outs=[dst]).then_inc(sem, 1)
        tok = (sem, 1, "c_" + key)
        self._commit(tok, reads, writes)
        return tok

    def all_tokens(self):
        toks = [(self.sem[e], self.cnt[e], e) for e in self.E if self.cnt[e] > 0]
        toks += [(s[0], s[1], "d_" + k) for k, s in self.dsem.items() if s[1] > 0]
        return toks

    def barrier(self):
        toks = self.all_tokens()
        for e in self.E:
            self._wait(e, toks)


def build(debug=False, stop=None, skip_ffn=False):
    nc = bass.Bass("TRN2", target_bir_lowering=False)
    S = Sched(nc)

    def din(name, shape):
        return nc.dram_tensor(name, shape, F32, kind="ExternalInput").ap()

    x_d = din("x", [NT, D])
    if not skip_ffn:
        w1 = [din("ffn1_w1", [D, DFF]), din("ffn2_w1", [D, DFF])]
        w3 = [din("ffn1_w3", [D, DFF]), din("ffn2_w3", [D, DFF])]
        w2 = [din("ffn1_w2", [DFF, D]), din("ffn2_w2", [DFF, D])]
    win_d = din("w_in", [D, INC])
    wout_d = din("w_out", [D, D])
    lng_d = [din("ln%d_g" % i, [128, D]) for i in (1, 2, 3)]
    lnb_d = [din("ln%d_b" % i, [128, D]) for i in (1, 2, 3)]
    NSM = 192
    small_d = din("small", [128, NSM])
    out_d = nc.dram_tensor("out", [NT, D], F32, kind="ExternalOutput").ap()
    halo_src = nc.dram_tensor("halo_src", [128, 64], F32, kind="Internal", addr_space="Local").ap()
    halo_dst = nc.dram_tensor("halo_dst", [4 * 128, 64], F32, kind="Internal", addr_space="Local").ap()
    st_src = [nc.dram_tensor("st_src%d" % k, [128, 1032], F32, kind="Internal", addr_space="Local").ap() for k in range(6)]
    st_dst = [nc.dram_tensor("st_dst%d" % k, [4 * 128, 1032], F32, kind="Internal", addr_space="Local").ap() for k in range(6)]
    dbg = {}
    if debug:
        dbg["x1"] = nc.dram_tensor("dbg_x1", [NT, D], F32, kind="ExternalOutput").ap()
        dbg["yT"] = nc.dram_tensor("dbg_yT", [128, KT * NT], F32, kind="ExternalOutput").ap()
        dbg["x2"] = nc.dram_tensor("dbg_x2", [NT, D], F32, kind="ExternalOutput").ap()
    groups = [[0, 1, 2, 3], [4, 5, 6, 7]]

    X = nc.alloc_sbuf_tensor("X", [128, NTT, D], F32)
    XT = nc.alloc_sbuf_tensor("XT", [128, KT, NT], BF16)
    A = nc.alloc_sbuf_tensor("A", [128, 8192], F32)
    B = nc.alloc_sbuf_tensor("B", [128, 18432], F32)
    ident = nc.alloc_sbuf_tensor("ident", [128, 128], BF16)
    tri = [nc.alloc_sbuf_tensor("tri_fw", [128, 128], F32), nc.alloc_sbuf_tensor("tri_bw", [128, 128], F32)]
    ones = nc.alloc_sbuf_tensor("ones", [128, 128], F32)
    SM = nc.alloc_sbuf_tensor("SM", [128, NSM], F32)
    ST8 = nc.alloc_sbuf_tensor("ST8", [128, 64], F32)
    Xb = [S.buf("X%d" % t) for t in range(NTT)]
    XTb = [S.buf("XT%d" % t) for t in range(NTT)]
    CONST = S.buf("CONST")
    ST8b = S.buf("ST8")

    PS = [nc.alloc_psum_tensor("ps%d" % i, [128, 512], F32) for i in range(8)]
    PSb = [S.buf("ps%d" % i) for i in range(8)]
    ps_next = [0]
    ps_pool = [8]

    def psum():
        i = ps_next[0] % ps_pool[0]
        ps_next[0] += 1
        return PS[i], PSb[i]

    o_lb = 0
    o_hg = 32
    o_mg = 40
    o_cb = 48
    o_cw = 64
    o_gb = 144
    o_mk = 160

    S.dma("sp", SM[:], small_d, writes=[CONST], key="CONST")
    S.op("dve", lambda e: e.memset(ident[:], 1.0), writes=[CONST])
    S.op("dve", lambda e: e.memset(tri[0][:], 1.0), writes=[CONST])
    S.op("dve", lambda e: e.memset(tri[1][:], 1.0), writes=[CONST])
    S.op("dve", lambda e: e.memset(ones[:], 1.0), writes=[CONST])
    S.op("pool", lambda e: e.affine_select(out=ident[:], in_=ident[:], pattern=[[-1, 128]], compare_op=ALU.is_equal,
                                           fill=0.0, base=0, channel_multiplier=1), writes=[CONST])
    S.op("pool", lambda e: e.affine_select(out=tri[0][:], in_=tri[0][:], pattern=[[1, 128]], compare_op=ALU.is_ge,
                                           fill=0.0, base=0, channel_multiplier=-1), writes=[CONST])
    S.op("pool", lambda e: e.affine_select(out=tri[1][:], in_=tri[1][:], pattern=[[-1, 128]], compare_op=ALU.is_ge,
                                           fill=0.0, base=0, channel_multiplier=1), writes=[CONST])

    xv = x_d.rearrange("(t p) d -> p t d", p=128)
    for t in range(NTT):
        S.dma("sp", X[:, t, :], xv[:, t, :], writes=[Xb[t]], key="X%d" % t)

    Ab = A[:].bitcast(BF16)
    Bb = B[:].bitcast(BF16)
    HT = [Ab[:, 0:4096].rearrange("p (j n) -> p j n", j=4), Ab[:, 4096:8192].rearrange("p (j n) -> p j n", j=4)]
    HTb = [[S.buf("HT%d_%d" % (i, h)) for h in range(2)] for i in range(2)]
    SS = [A[:, 4096:4608], A[:, 4608:5120]]
    SSb = [S.buf("SS0"), S.buf("SS1")]
    XBc = Ab[:, 10240:12288]
    XBb = S.buf("XB")
    G = A[:, 6144:8192]
    Gb = S.buf("G")
    RING = [Bb[:, i * 8192:(i + 1) * 8192] for i in range(4)]
    RINGb = [S.buf("R%d" % i) for i in range(4)]
    Bt = B[:, 16384:18432]
    Btb = S.buf("Bt")
    ring_next = [0]

    def ring():
        i = ring_next[0] % 4
        ring_next[0] += 1
        return RING[i], RINGb[i]

    def make_xt(t):
        S.op("act", lambda e: e.activation(out=XBc, in_=X[:, t, :], func=AF.Copy), reads=[Xb[t]], writes=[XBb])
        for half in range(2):
            p, pb = psum()
            pv = p[:].bitcast(BF16).rearrange("p (k n) -> p k n", k=8)

            def tr(e, half=half, pv=pv):
                ins = None
                for k in range(8):
                    kt = half * 8 + k
                    ins = e.transpose(out=pv[:, k, :], in_=XBc[:, kt * 128:(kt + 1) * 128], identity=ident[:])
                return ins
            S.op("pe", tr, reads=[XBb, CONST], writes=[pb])
            S.op("dve", lambda e, half=half, pv=pv: e.tensor_copy(out=XT[:, half * 8:(half + 1) * 8, t * 128:(t + 1) * 128],
                                                                   in_=pv), reads=[pb], writes=[XTb[t]])

    def layer_norm(t, g_d, b_d, first):
        if first:
            S.dma("sp", G, g_d, writes=[Gb], key="G")
            S.dma("sp", Bt, b_d, writes=[Btb], key="Bt")
        xt_ = X[:, t, :]
        c = t * 4
        S.op("dve", lambda e: e.reduce_sum(out=ST8[:, c:c + 1], in_=xt_, axis=AX.X), reads=[Xb[t]], writes=[ST8b])
        S.op("dve", lambda e: e.tensor_scalar_mul(out=ST8[:, c:c + 1], in0=ST8[:, c:c + 1], scalar1=-1.0 / D),
             reads=[ST8b], writes=[ST8b])
        S.op("dve", lambda e: e.tensor_scalar_add(out=xt_, in0=xt_, scalar1=ST8[:, c:c + 1]), reads=[ST8b], writes=[Xb[t]])
        S.op("act", lambda e: e.activation(out=XBc, in_=xt_, func=AF.Square, accum_out=ST8[:, c + 1:c + 2]),
             reads=[Xb[t]], writes=[XBb, ST8b])
        S.op("dve", lambda e: e.tensor_scalar(out=ST8[:, c + 1:c + 2], in0=ST8[:, c + 1:c + 2], scalar1=1.0 / D,
                                              scalar2=LN_EPS, op0=ALU.mult, op1=ALU.add), reads=[ST8b], writes=[ST8b])
        S.op("act", lambda e: e.activation(out=ST8[:, c + 2:c + 3], in_=ST8[:, c + 1:c + 2], func=AF.Sqrt),
             reads=[ST8b], writes=[ST8b])
        S.op("dve", lambda e: e.reciprocal(out=ST8[:, c + 3:c + 4], in_=ST8[:, c + 2:c + 3]), reads=[ST8b], writes=[ST8b])
        S.op("dve", lambda e: e.scalar_tensor_tensor(out=xt_, in0=xt_, scalar=ST8[:, c + 3:c + 4], in1=G,
                                                     op0=ALU.mult, op1=ALU.mult), reads=[ST8b, Gb], writes=[Xb[t]])
        S.op("dve", lambda e: e.tensor_tensor(out=xt_, in0=xt_, in1=Bt, op=ALU.add), reads=[Btb], writes=[Xb[t]])

    def ffn(l, post=None):
        for t in range(NTT):
            S.op("dve", lambda e, t=t: e.tensor_scalar_mul(out=X[:, t, :], in0=X[:, t, :], scalar1=ALPHA),
                 reads=[XBb], writes=[Xb[t]])
        w1v = w1[l].rearrange("(kt p) n -> p kt n", p=128)
        w3v = w3[l].rearrange("(kt p) n -> p kt n", p=128)
        w2v = w2[l].rearrange("(j p) n -> p j n", p=128)

        def h_stage(c):
            r1, r1b = ring()
            S.dma("pool", r1.rearrange("p (k n) -> p k n", k=16), w1v[:, :, c * 512:(c + 1) * 512], writes=[r1b], key=r1b.name)
            r3, r3b = ring()
            S.dma("pool", r3.rearrange("p (k n) -> p k n", k=16), w3v[:, :, c * 512:(c + 1) * 512], writes=[r3b], key=r3b.name)
            r1v = r1.rearrange("p (k n) -> p k n", k=16)
            r3v = r3.rearrange("p (k n) -> p k n", k=16)
            hp = c % 2
            for j in range(4):
                for half in range(2):
                    p1, p1b = psum()
                    p3, p3b = psum()

                    def mm(e, rv, p):
                        ins = None
                        for kt in range(KT):
                            ins = e.matmul(p[:], lhsT=rv[:, kt, j * 128:(j + 1) * 128],
                                           rhs=XT[:, kt, half * 512:(half + 1) * 512], start=(kt == 0), stop=(kt == KT - 1))
                        return ins
                    xr = XTb[half * 4:(half + 1) * 4]
                    S.op("pe", lambda e: mm(e, r1v, p1), reads=[r1b] + xr, writes=[p1b])
                    S.op("pe", lambda e: mm(e, r3v, p3), reads=[r3b] + xr, writes=[p3b])
                    si = (j * 2 + half) % 2
                    S.op("act", lambda e: e.activation(out=SS[si], in_=p1[:], func=AF.Silu), reads=[p1b], writes=[SSb[si]])
                    S.op("dve", lambda e: e.tensor_tensor(out=HT[hp][:, j, half * 512:(half + 1) * 512], in0=p3[:], in1=SS[si],
                                                          op=ALU.mult), reads=[p3b, SSb[si]], writes=[HTb[hp][half]])

        def o_stage(c):
            r2, r2b = ring()
            r2v = r2.rearrange("p (j n) -> p j n", j=4)
            S.dma("pool", r2v, w2v[:, c * 4:(c + 1) * 4, :], writes=[r2b], key=r2b.name)
            hp = c % 2
            for t in range(NTT):
                for cb in range(4):
                    po, pob = psum()

                    def mm(e):
                        ins = None
                        for j in range(4):
                            ins = e.matmul(po[:], lhsT=HT[hp][:, j, t * 128:(t + 1) * 128], rhs=r2v[:, j, cb * 512:(cb + 1) * 512],
                                           start=(j == 0), stop=(j == 3))
                        return ins
                    S.op("pe", mm, reads=[r2b, HTb[hp][t // 4]], writes=[pob])
                    xs = X[:, t, cb * 512:(cb + 1) * 512]
                    S.op("dve", lambda e: e.scalar_tensor_tensor(out=xs, in0=po[:], scalar=0.5, in1=xs, op0=ALU.mult, op1=ALU.add),
                         reads=[pob], writes=[Xb[t]])
                if post is not None and c == NCH - 1:
                    post(t)

        h_stage(0)
        for c in range(NCH):
            if c + 1 < NCH:
                h_stage(c + 1)
            o_stage(c)

    def ln_stage_major(g_d, b_d, tag, do_xt, store=None):
        XB8 = Ab.rearrange("p (t n) -> p t n", t=NTT)
        XB8b = [S.buf("XB8%s_%d" % (tag, t)) for t in range(NTT)]
        STt = [S.buf("STt%s%d" % (tag, t)) for t in range(NTT)]
        gi = (ring_next[0] + 3) % 4
        G2 = RING[gi].bitcast(F32)[:, 0:D]
        S.dma("sp", G2, g_d, writes=[RINGb[gi]], key="G2")
        S.dma("sp", Bt, b_d, writes=[Btb], key="Bt")
        TT = range(NTT)
        for t in TT:
            S.op("dve", lambda e, t=t: e.reduce_sum(out=ST8[:, t * 4:t * 4 + 1], in_=X[:, t, :], axis=AX.X), reads=[Xb[t]],
                 writes=[STt[t]])
        for t in TT:
            S.op("dve", lambda e, t=t: e.tensor_scalar_mul(out=ST8[:, t * 4:t * 4 + 1], in0=ST8[:, t * 4:t * 4 + 1],
                                                           scalar1=-1.0 / D), reads=[STt[t]], writes=[STt[t]])
        for t in TT:
            S.op("dve", lambda e, t=t: e.tensor_scalar_add(out=X[:, t, :], in0=X[:, t, :], scalar1=ST8[:, t * 4:t * 4 + 1]),
                 reads=[STt[t]], writes=[Xb[t]])
        for t in TT:
            S.op("act", lambda e, t=t: e.activation(out=XB8[:, t, :], in_=X[:, t, :], func=AF.Square,
                                                    accum_out=ST8[:, t * 4 + 1:t * 4 + 2]), reads=[Xb[t]], writes=[XB8b[t], STt[t]])
        for t in TT:
            S.op("dve", lambda e, t=t: e.tensor_scalar(out=ST8[:, t * 4 + 1:t * 4 + 2], in0=ST8[:, t * 4 + 1:t * 4 + 2],
                                                       scalar1=1.0 / D, scalar2=LN_EPS, op0=ALU.mult, op1=ALU.add),
                 reads=[STt[t]], writes=[STt[t]])
        for t in TT:
            S.op("act", lambda e, t=t: e.activation(out=ST8[:, t * 4 + 2:t * 4 + 3], in_=ST8[:, t * 4 + 1:t * 4 + 2], func=AF.Sqrt),
                 reads=[STt[t]], writes=[STt[t]])
        for t in TT:
            S.op("dve", lambda e, t=t: e.reciprocal(out=ST8[:, t * 4 + 3:t * 4 + 4], in_=ST8[:, t * 4 + 2:t * 4 + 3]),
                 reads=[STt[t]], writes=[STt[t]])
        for t in TT:
            S.op("dve", lambda e, t=t: e.scalar_tensor_tensor(out=X[:, t, :], in0=X[:, t, :], scalar=ST8[:, t * 4 + 3:t * 4 + 4],
                                                              in1=G2, op0=ALU.mult, op1=ALU.mult), reads=[STt[t], RINGb[gi]],
                 writes=[Xb[t]])
            S.op("dve", lambda e, t=t: e.tensor_tensor(out=X[:, t, :], in0=X[:, t, :], in1=Bt, op=ALU.add), reads=[Btb],
                 writes=[Xb[t]])
            if store is not None:
                store(t)
            if do_xt:
                S.op("act", lambda e, t=t: e.activation(out=XB8[:, t, :], in_=X[:, t, :], func=AF.Copy), reads=[Xb[t]],
                     writes=[XB8b[t]])
        if do_xt:
            for t in TT:
                for half in range(2):
                    p, pb = psum()
                    pv = p[:].bitcast(BF16).rearrange("p (k n) -> p k n", k=8)

                    def tr(e, t=t, half=half, pv=pv):
                        ins = None
                        for k in range(8):
                            kt = half * 8 + k
                            ins = e.transpose(out=pv[:, k, :], in_=XB8[:, t, kt * 128:(kt + 1) * 128], identity=ident[:])
                        return ins
                    S.op("pe", tr, reads=[XB8b[t], CONST], writes=[pb])
                    S.op("dve", lambda e, t=t, half=half, pv=pv: e.tensor_copy(
                        out=XT[:, half * 8:(half + 1) * 8, t * 128:(t + 1) * 128], in_=pv), reads=[pb], writes=[XTb[t]])
        S.barrier()

    for t in range(NTT):
        make_xt(t)
    if not skip_ffn:
        ffn(0)
        S.barrier()
        ln_stage_major(lng_d[0], lnb_d[0], "a", True)
    if debug:
        for t in range(NTT):
            S.dma("sp", dbg["x1"].rearrange("(t p) d -> p t d", p=128)[:, t, :], X[:, t, :], reads=[Xb[t]], key="dbgx1")
    S.barrier()
    if stop == "A":
        S._wait("sp", S.all_tokens())
        return nc

    mixer(nc, S, locals())

    S.barrier()
    if stop is not None and stop.startswith("B"):
        S._wait("sp", S.all_tokens())
        return nc
    YT = Ab.rearrange("p (k n) -> p k n", k=KT)
    YTb = S.buf("YTall")
    wov = wout_d.rearrange("(kt p) n -> p kt n", p=128)
    rs = []
    for i in range(4):
        r, rb = ring()
        rv = r.rearrange("p (k n) -> p k n", k=4)
        S.dma("pool", rv, wov[:, i * 4:(i + 1) * 4, :], writes=[rb], key=rb.name)
        rs.append((rv, rb, r))
    for t in range(NTT):
        S.op("dve", lambda e, t=t: e.tensor_scalar_mul(out=X[:, t, :], in0=X[:, t, :], scalar1=ALPHA), writes=[Xb[t]])
    for i in range(4):
        rv, rb, _ = rs[i]
        for t in range(NTT):
            for cb in range(4):
                po, pob = psum()

                def mm(e):
                    ins = None
                    for k in range(4):
                        ins = e.matmul(po[:], lhsT=YT[:, i * 4 + k, t * 128:(t + 1) * 128], rhs=rv[:, k, cb * 512:(cb + 1) * 512],
                                       start=(k == 0), stop=(k == 3))
                    return ins
                S.op("pe", mm, reads=[rb, YTb], writes=[pob])
                xs = X[:, t, cb * 512:(cb + 1) * 512]
                S.op("dve", lambda e: e.tensor_tensor(out=xs, in0=po[:], in1=xs, op=ALU.add), reads=[pob], writes=[Xb[t]])
    S.barrier()
    ln_stage_major(lng_d[1], lnb_d[1], "b", True)
    if debug:
        for t in range(NTT):
            S.dma("sp", dbg["x2"].rearrange("(t p) d -> p t d", p=128)[:, t, :], X[:, t, :], reads=[Xb[t]], key="dbgx2")

    ov = out_d.rearrange("(t p) d -> p t d", p=128)

    ffn(1)
    S.barrier()
    ln_stage_major(lng_d[2], lnb_d[2], "c", False,
                   store=lambda t: S.dma("sp", ov[:, t, :], X[:, t, :], reads=[Xb[t]], key="OUT"))
    S._wait("sp", S.all_tokens())
    return nc


def mixer(nc, S, L):
    X, XT, A, B, SM, ident, tri, ones, ST8 = (L[k] for k in ("X", "XT", "A", "B", "SM", "ident", "tri", "ones", "ST8"))
    XTb, CONST, psum, win_d, dbg, debug = (L[k] for k in ("XTb", "CONST", "psum", "win_d", "dbg", "debug"))
    halo_src, halo_dst, st_src, st_dst, groups = (L[k] for k in ("halo_src", "halo_dst", "st_src", "st_dst", "groups"))
    o_lb, o_hg, o_mg, o_cb, o_cw, o_gb, o_mk = (L[k] for k in ("o_lb", "o_hg", "o_mg", "o_cb", "o_cw", "o_gb", "o_mk"))
    Ab = A[:].bitcast(BF16)
    Bb = B[:].bitcast(BF16)
    YT = Ab.rearrange("p (k n) -> p k n", k=KT)
    YTb = S.buf("YT")
    wv = win_d.rearrange("(kt p) n -> p kt n", p=128)
    XTall = list(XTb)

    off = [0]

    def carve(n32):
        a = off[0]
        off[0] += n32
        assert off[0] <= 18432, off[0]
        return a
    WR = []
    for i in range(3):
        a = carve(1024)
        WR.append(B[:, a:a + 1024].bitcast(BF16).rearrange("p (k n) -> p k n", k=KT))
    WRb = [S.buf("W%d" % i) for i in range(3)]
    wr_next = [0]

    wr_n = [3]

    def wblock(col0, ncols=128):
        i = wr_next[0] % wr_n[0]
        wr_next[0] += 1
        S.dma("pool", WR[i][:, :, 0:ncols], wv[:, :, col0:col0 + ncols], writes=[WRb[i]], key=WRb[i].name)
        return WR[i], WRb[i]

    def f32(n):
        a = carve(n)
        return B[:, a:a + n]

    def b16(n):
        a = carve((n + 1) // 2)
        return B[:, a:a + (n + 1) // 2].bitcast(BF16)[:, 0:n]

    def proj_fm(col0, consume):
        w, wb = wblock(col0)
        for half in range(2):
            p, pb = psum()

            def mm(e):
                ins = None
                for kt in range(KT):
                    ins = e.matmul(p[:], lhsT=w[:, kt, :], rhs=XT[:, kt, half * 512:(half + 1) * 512],
                                   start=(kt == 0), stop=(kt == KT - 1))
                return ins
            S.op("pe", mm, reads=[wb] + XTall[half * 4:(half + 1) * 4], writes=[pb])
            consume(half, p, pb)

    BND = f32(64).rearrange("p (j b) -> p j b", j=16)
    XTB = b16(16 * 4).rearrange("p (k b) -> p k b", k=16)
    mark = off[0]

    FK = f32(1024); LOGF = f32(1024); BC = f32(1024); EE = f32(1024); QT = f32(1024)
    SQ16 = B[:, mark:mark + 2048].rearrange("p (c n) -> p c n", c=16)
    ON16 = B[:, mark + 2048:mark + 3072].bitcast(BF16).rearrange("p (c n) -> p c n", c=16)
    QA = [b16(1024), b16(1024)]; KA = [b16(1024), b16(1024)]; X3 = [b16(1024), b16(1024)]
    GG = b16(1024)
    VTb = b16(1024)
    bVT = [S.buf(), S.buf()]
    Vt = b16(16 * 128)
    OACC = f32(16 * 128)
    KALL = [EE.bitcast(BF16).rearrange("p (c n) -> p c n", c=16), QT.bitcast(BF16).rearrange("p (c n) -> p c n", c=16)]
    ATTA = [b16(16 * 64).rearrange("p (c n) -> p c n", c=16), b16(16 * 64).rearrange("p (c n) -> p c n", c=16)]
    SD = [f32(129), f32(129)]
    Sbf = [[b16(128), b16(128)] for _ in range(2)]
    EBL = [f32(16), f32(16)]
    RMASK = b16(1024); GIN = f32(4 * 129); DSEG = f32(4); RS = f32(64)
    Vv = Vt.rearrange("p (c n) -> p c n", c=16)
    Ov = OACC.rearrange("p (c n) -> p c n", c=16)
    GINv = GIN.rearrange("p (r n) -> p r n", r=4)
    bFK, bLOGF, bBC, bEE, bQT, bGG, bV, bO, bRM, bGIN, bDS, bRS = (S.buf() for _ in range(12))
    bQA = [S.buf(), S.buf()]; bKA = [S.buf(), S.buf()]; bX3 = [S.buf(), S.buf()]; bEBL = [S.buf(), S.buf()]
    bKALL = [[S.buf() for _ in range(4)] for _ in range(2)]; bATTA = [S.buf(), S.buf()]
    bSD = [S.buf(), S.buf()]; bSbf = [[S.buf(), S.buf()] for _ in range(2)]
    PSKV = [[L["PS"][4], L["PS"][5]], [L["PS"][6], L["PS"][7]]]
    bPSKV = [[L["PSb"][4], L["PSb"][5]], [L["PSb"][6], L["PSb"][7]]]
    ps_pool = L["ps_pool"]

    def hgrn_init():
        S.op("dve", lambda e: e.memset(Ov[0:64], 0.0), writes=[bO])
        S.op("dve", lambda e: e.memset(RMASK, 1.0), writes=[bRM])
        S.op("dve", lambda e: e.memset(RMASK.rearrange("p (c n) -> p c n", n=64)[:, :, 0:1], 0.0), writes=[bRM])
    S.op("dve", lambda e: e.tensor_tensor(out=SM[:, o_lb:o_lb + 16], in0=SM[:, o_lb:o_lb + 16], in1=SM[:, o_lb + 16:o_lb + 32],
                                          op=ALU.subtract), reads=[CONST], writes=[CONST])
    S.op("act", lambda e: e.activation(out=SM[:, o_lb:o_lb + 16], in_=SM[:, o_lb:o_lb + 16], func=AF.Sigmoid),
         reads=[CONST], writes=[CONST])
    S.op("dve", lambda e: e.tensor_scalar(out=SM[:, o_lb + 16:o_lb + 32], in0=SM[:, o_lb:o_lb + 16], scalar1=-1.0, scalar2=1.0,
                                          op0=ALU.mult, op1=ALU.add), reads=[CONST], writes=[CONST])

    def c3(ap):
        return ap.rearrange("p (c n) -> p c n", n=64)

    deferred = []

    def run_deferred():
        while deferred:
            deferred.pop(0)()

    def hgrn_head(h, mode):
        HW = 1024
        cq, cv, cg, cf = h * 128, HW + h * 128, 2 * HW + h * 128, [3 * HW + h * 128, 4 * HW + h * 128]
        vw = {}
        vstate = {"next": 0, "pend": None}
        NV = 6

        def v_evac():
            if vstate["pend"] is not None:
                k, p, pb = vstate["pend"]
                if k < 2:
                    S.op("act", lambda e: e.activation(out=VTb[:, k * 512:(k + 1) * 512], in_=p[:], func=AF.Copy), reads=[pb],
                         writes=[bVT[k]])
                else:
                    g = k - 2
                    pT = p[:].bitcast(BF16)
                    S.op("act", lambda e: e.activation(out=Vv[0:64, g * 4:(g + 1) * 4, :],
                                                       in_=pT[0:64, 0:512].rearrange("p (j n) -> p j n", j=4), func=AF.Copy),
                         reads=[pb], writes=[bV])
                vstate["pend"] = None

        def tick():
            if "w" not in vw:
                return
            v_evac()
            k = vstate["next"]
            if k >= NV:
                return
            vstate["next"] = k + 1
            p, pb = L["PS"][4 + k % 4], L["PSb"][4 + k % 4]
            if k < 2:
                w, wb = vw["w"]

                def mm(e):
                    ins = None
                    for kt in range(KT):
                        ins = e.matmul(p[:], lhsT=w[:, kt, :], rhs=XT[:, kt, k * 512:(k + 1) * 512], start=(kt == 0),
                                       stop=(kt == KT - 1))
                    return ins
                S.op("pe", mm, reads=[wb] + XTall[k * 4:(k + 1) * 4], writes=[pb])
            else:
                g = k - 2
                pT = p[:].bitcast(BF16)

                def trn(e):
                    ins = None
                    for j in range(4):
                        c = g * 4 + j
                        ins = e.transpose(out=pT[0:64, j * 128:(j + 1) * 128], in_=VTb[:, c * 64:(c + 1) * 64], identity=ident[:])
                    return ins
                S.op("pe", trn, reads=[bVT[g // 2], CONST], writes=[pb])
            vstate["pend"] = (k, p, pb)
        qg_pending = []
        for di in range(2):
            col = di * 8 + h
            proj_fm(cf[di], lambda half, p, pb: S.op("act", lambda e: e.activation(out=FK[:, half * 512:(half + 1) * 512], in_=p[:],
                                                                                    func=AF.Sigmoid), reads=[pb], writes=[bFK]))
            if di == 0:
                if mode == 2:
                    for (c0, dstT, dstb, bank0) in ((cq, QT, bQT, 4), (cg, GG, bGG, 6)):
                        wq, wqb = wblock(c0)
                        for half in range(2):
                            p, pb = L["PS"][bank0 + half], L["PSb"][bank0 + half]

                            def mm(e):
                                ins = None
                                for kt in range(KT):
                                    ins = e.matmul(p[:], lhsT=wq[:, kt, :], rhs=XT[:, kt, half * 512:(half + 1) * 512],
                                                   start=(kt == 0), stop=(kt == KT - 1))
                                return ins
                            S.op("pe", mm, reads=[wqb] + XTall[half * 4:(half + 1) * 4], writes=[pb])
                            qg_pending.append((p, pb, dstT, dstb, half))
                    run_deferred()
                else:
                    vw["w"] = wblock(cv)
            elif mode == 2:
                vw["w"] = wblock(cv)
            S.op("dve", lambda e: e.tensor_scalar(out=FK, in0=FK, scalar1=SM[:, o_lb + 16 + col:o_lb + 17 + col],
                                                  scalar2=SM[:, o_lb + col:o_lb + col + 1], op0=ALU.mult, op1=ALU.add),
                 reads=[CONST], writes=[bFK])
            S.op("act", lambda e: e.activation(out=LOGF, in_=FK, func=AF.Ln, accum_out=DSEG[:, di:di + 1]),
                 reads=[bFK], writes=[bLOGF, bDS])
            tick()
            S.op("dve", lambda e: e.tensor_scalar(out=FK, in0=FK, scalar1=-1.0, scalar2=1.0, op0=ALU.mult, op1=ALU.add),
                 reads=[bLOGF], writes=[bFK])
            S.op("dve", lambda e: e.tensor_tensor_scan(out=BC, data0=RMASK, data1=LOGF, initial=0.0, op0=ALU.mult, op1=ALU.add),
                 reads=[bRM, bLOGF], writes=[bBC])
            tick()
            S.op("act", lambda e: e.activation(out=EBL[di], in_=c3(BC)[:, :, 63], func=AF.Exp), reads=[bBC], writes=[bEBL[di]])
            tick()
            if di == 1:
                S.op("dve", lambda e: e.tensor_tensor(out=BC, in0=LOGF, in1=BC, op=ALU.subtract), reads=[bLOGF], writes=[bBC])
            ebl_b = EBL[di].unsqueeze(2).to_broadcast([128, 16, 64])
            S.op("act", lambda e: e.activation(out=EE, in_=BC, func=AF.Exp, scale=-1.0), reads=[bBC], writes=[bEE])
            S.op("dve", lambda e: e.tensor_tensor(out=KA[di], in0=FK, in1=EE, op=ALU.mult), reads=[bFK, bEE], writes=[bKA[di]])
            tick()
            if di == 0:
                S.op("dve", lambda e: e.tensor_tensor(out=c3(X3[0]), in0=c3(KA[0]), in1=ebl_b, op=ALU.mult),
                     reads=[bKA[0], bEBL[0]], writes=[bX3[0]])
            if mode == 2:
                S.op("act", lambda e: e.activation(out=LOGF, in_=BC, func=AF.Exp), reads=[bBC], writes=[bLOGF])
                if di == 0:
                    for (p, pb, dstT, dstb, half) in qg_pending:
                        S.op("act", lambda e: e.activation(out=dstT[:, half * 512:(half + 1) * 512], in_=p[:], func=AF.Silu),
                             reads=[pb], writes=[dstb])
                S.op("dve", lambda e: e.scalar_tensor_tensor(out=QA[di], in0=QT, scalar=128.0 ** -0.5, in1=LOGF, op0=ALU.mult,
                                                             op1=ALU.mult), reads=[bQT, bLOGF], writes=[bQA[di]])
                if di == 1:
                    S.op("dve", lambda e: e.tensor_tensor(out=c3(X3[1]), in0=c3(QA[1]), in1=ebl_b, op=ALU.mult),
                         reads=[bQA[1], bEBL[1]], writes=[bX3[1]])
            sk, so = di, h * 129
            Sst = SD[di][:, 0:128]
            if mode == 1:
                S.op("dve", lambda e: e.memset(Sst, 0.0), writes=[bSD[di]])
            else:
                S.dma("sp", GINv, st_dst[sk].rearrange("(r p) c -> p r c", p=128)[:, :, so:so + 129], reads=[STD[sk]], writes=[bGIN],
                      key="GIN")
                combine(GINv, 128, Sst, bSD[di], bGIN, di)
                S.op("act", lambda e: e.activation(out=Sbf[di][0], in_=Sst, func=AF.Copy), reads=[bSD[di]], writes=[bSbf[di][0]])
        KS = [X3[0], KA[1]]
        bKS = [bX3[0], bKA[1]]
        QI = [QA[0], X3[1]]
        bQI = [bQA[0], bX3[1]]
        pkv = [[None] * 16, [None] * 16]

        def transposes(di):
            alias_b = bEE if di == 0 else bQT
            gs = range(4) if di == 0 else range(3, -1, -1)
            for g in gs:
                p, pb = psum()
                pT = p[:].bitcast(BF16)

                def trn(e):
                    ins = None
                    for j in range(4):
                        c = g * 4 + j
                        ins = e.transpose(out=pT[0:64, j * 128:(j + 1) * 128], in_=KS[di][:, c * 64:(c + 1) * 64], identity=ident[:])
                    return ins
                S.op("pe", trn, reads=[bKS[di], CONST], writes=[pb])
                S.op("act", lambda e: e.activation(out=KALL[di][0:64, g * 4:(g + 1) * 4, :],
                                                   in_=pT[0:64, 0:512].rearrange("p (j n) -> p j n", j=4), func=AF.Copy),
                     reads=[pb], writes=[bKALL[di][g], alias_b])

        def stage_a(di, i):
            c = i if di == 0 else 15 - i
            pk = PSKV[di][i % 2][:, 0:128]
            pkb = bPSKV[di][i % 2]
            S.op("pe", lambda e: e.matmul(pk, lhsT=KALL[di][0:64, c, :], rhs=Vv[0:64, c, :], start=True, stop=True),
                 reads=[bKALL[di][c // 4], bV, (bEE if di == 0 else bQT)], writes=[pkb])
            pkv[di][i] = (pk, pkb)

        def attn_all(di):
            for g in range(2):
                p, pb = psum()

                def mm(e):
                    ins = None
                    for j in range(8):
                        c = g * 8 + j
                        cs = slice(c * 64, (c + 1) * 64)
                        ins = e.matmul(p[0:64, j * 64:(j + 1) * 64], lhsT=KA[di][:, cs], rhs=QA[di][:, cs], start=True, stop=True)
                    return ins
                S.op("pe", mm, reads=[bKA[di], bQA[di]], writes=[pb])
                S.op("dve", lambda e: e.tensor_tensor(out=ATTA[di][0:64, g * 8:(g + 1) * 8, :],
                                                      in0=p[0:64, 0:512].rearrange("p (j n) -> p j n", j=8),
                                                      in1=tri[di][0:64, 0:64].unsqueeze(1).to_broadcast([64, 8, 64]), op=ALU.mult),
                     reads=[pb, CONST], writes=[bATTA[di]])

        pend = [None, None]

        def flush_o(di):
            if pend[di] is not None:
                po, pob, c = pend[di]
                S.op("dve", lambda e: e.tensor_tensor(out=Ov[0:64, c, :], in0=po[0:64, 0:128], in1=Ov[0:64, c, :], op=ALU.add),
                     reads=[pob], writes=[bO])
                pend[di] = None

        def stage_b(di, i):
            c = i if di == 0 else 15 - i
            cs = slice(c * 64, (c + 1) * 64)
            Sst = SD[di][:, 0:128]
            sb_, sbb_ = Sbf[di][i % 2], bSbf[di][i % 2]
            pk, pkb = pkv[di][i]
            S.op("dve", lambda e: e.scalar_tensor_tensor(out=Sst, in0=Sst, scalar=EBL[di][:, c:c + 1], in1=pk,
                                                         op0=ALU.mult, op1=ALU.add), reads=[pkb, bEBL[di]], writes=[bSD[di]])
            if mode == 2:
                if i < 15:
                    nb_, nbb_ = Sbf[di][(i + 1) % 2], bSbf[di][(i + 1) % 2]
                    S.op("act", lambda e: e.activation(out=nb_, in_=Sst, func=AF.Copy), reads=[bSD[di]], writes=[nbb_])
                flush_o(di)
                po, pob = psum()

                def mm(e):
                    e.matmul(po[0:64, 0:128], lhsT=ATTA[di][0:64, c, :], rhs=Vv[0:64, c, :], start=True, stop=False)
                    return e.matmul(po[0:64, 0:128], lhsT=QI[di][:, cs], rhs=sb_, start=False, stop=True)
                S.op("pe", mm, reads=[bATTA[di], bV, bQI[di], sbb_], writes=[pob])
                pend[di] = (po, pob, c)

        while vstate["next"] < NV or vstate["pend"] is not None:
            tick()
        for di in range(2):
            transposes(di)
        if mode == 2:
            for di in range(2):
                attn_all(di)
        for di in range(2):
            stage_a(di, 0)
        for i in range(16):
            for di in range(2):
                if i + 1 < 16:
                    stage_a(di, i + 1)
                stage_b(di, i)
        for di in range(2):
            flush_o(di)
        if mode == 1:
            for di in range(2):
                sk, so = di, h * 129
                S.op("act", lambda e: e.activation(out=SD[di][:, 128:129], in_=DSEG[:, di:di + 1], func=AF.Exp), reads=[bDS],
                     writes=[bSD[di]])
                S.dma("sp", st_src[sk][:, so:so + 129], SD[di], reads=[bSD[di]], writes=[STS[sk]], key="STS%d" % sk)
        if mode == 2:
            S.op("dve", lambda e: e.tensor_tensor(out=SQ16[0:64], in0=Ov[0:64], in1=Ov[0:64], op=ALU.mult), reads=[bO, bEE],
                 writes=[bFK, bLOGF])
            S.op("dve", lambda e: e.tensor_reduce(out=RS[0:64, 0:16], in_=SQ16[0:64], axis=AX.X, op=ALU.add), reads=[bFK, bLOGF],
                 writes=[bRS])
            S.op("dve", lambda e: e.tensor_scalar(out=RS[0:64, 16:32], in0=RS[0:64, 0:16], scalar1=1.0 / 128, scalar2=NORM_EPS,
                                                  op0=ALU.mult, op1=ALU.add), reads=[bRS], writes=[bRS])
            S.op("act", lambda e: e.activation(out=RS[0:64, 32:48], in_=RS[0:64, 16:32], func=AF.Sqrt), reads=[bRS], writes=[bRS])
            S.op("dve", lambda e: e.reciprocal(out=RS[0:64, 48:64], in_=RS[0:64, 32:48]), reads=[bRS], writes=[bRS])
            S.op("dve", lambda e: e.tensor_tensor(out=ON16[0:64], in0=Ov[0:64],
                                                  in1=RS[0:64, 48:64].unsqueeze(2).to_broadcast([64, 16, 128]), op=ALU.mult),
                 reads=[bRS, bO], writes=[bBC])
            def epi_pe(h=h):
                p, pb = psum()
                pT = p[:].bitcast(BF16)

                def trn(e):
                    ins = None
                    for c in range(16):
                        ins = e.transpose(out=pT[:, c * 64:(c + 1) * 64], in_=ON16[0:64, c, :], identity=ident[0:64, 0:64])
                    return ins
                S.op("pe", trn, reads=[bBC, CONST], writes=[pb])
                S.op("dve", lambda e: e.scalar_tensor_tensor(out=YT[:, h, :], in0=pT[:, 0:1024], scalar=SM[:, o_hg + h:o_hg + h + 1],
                                                             in1=GG, op0=ALU.mult, op1=ALU.mult), reads=[pb, bGG, CONST], writes=[YTb])
            deferred.append(epi_pe)
            S.op("dve", lambda e: e.memset(Ov[0:64], 0.0), writes=[bO])

    def combine(Gv, n, dst, dstb, gb, di):
        mk = o_mk + (0 if di == 0 else 4)
        idx = [0, 1, 2] if di == 0 else [3, 2, 1]
        first = True
        for i in idx:
            m = SM[:, mk + i:mk + i + 1]
            if first:
                S.op("dve", lambda e: e.tensor_scalar_mul(out=dst, in0=Gv[:, i, 0:n], scalar1=m), reads=[gb, CONST], writes=[dstb])
                first = False
                continue
            om = SM[:, mk + 16 + i:mk + 17 + i]
            S.op("dve", lambda e: e.tensor_scalar(out=ST8[:, 41:42], in0=Gv[:, i, n:n + 1], scalar1=m, scalar2=om, op0=ALU.mult,
                                                  op1=ALU.add), reads=[gb, CONST], writes=[ST8b_])
            S.op("dve", lambda e: e.tensor_scalar_mul(out=dst, in0=dst, scalar1=ST8[:, 41:42]), reads=[ST8b_], writes=[dstb])
            S.op("dve", lambda e: e.scalar_tensor_tensor(out=dst, in0=Gv[:, i, 0:n], scalar=m, in1=dst, op0=ALU.mult, op1=ALU.add),
                 reads=[gb, CONST], writes=[dstb])

    ST8b_ = L["ST8b"]
    STS = [S.buf("STS%d" % k) for k in range(6)]
    STD = [S.buf("STD%d" % k) for k in range(6)]
    HLS = S.buf("HLS")
    HLD = S.buf("HLD")
    hg_end = off[0]

    off[0] = mark
    MW = 1024
    c_mq, c_mk, c_mv, c_mo, c_g = 5 * 1024, 5 * 1024 + MW, 5 * 1024 + 2 * MW, 5 * 1024 + 3 * MW, 5 * 1024 + 4 * MW
    WR.append(b16(KT * 128).rearrange("p (k n) -> p k n", k=KT))
    WRb.append(S.buf("W3"))
    mQT = b16(2 * 1024).rearrange("p (d n) -> p d n", d=2)
    mKTs = [YT[:, 2 * hh:2 * hh + 2, :] for hh in range(4)]
    bmKTs = [S.buf("mKT%d" % hh) for hh in range(4)]
    ZC = f32(1028); ACC = f32(1024)
    VE = b16(8 * 258).rearrange("p (t n) -> p t n", t=8)
    OT = b16(2 * 1024).rearrange("p (d n) -> p d n", d=2)
    NUM = [f32(8 * 257).rearrange("p (t n) -> p t n", t=8), f32(8 * 257).rearrange("p (t n) -> p t n", t=8)]
    HNb = NUM[1].rearrange("p t n -> p (t n)").bitcast(BF16)[:, 0:2048].rearrange("p (t n) -> p t n", t=8)
    RD = f32(64)
    GZ = f32(128).rearrange("p (t n) -> p t n", t=8)
    GB_ = f32(128).rearrange("p (t n) -> p t n", t=8)
    GD = f32(4 * 64).rearrange("p (k t n) -> p k t n", k=4, t=8)
    SCb = [b16(128), b16(128)]; KWb = [b16(256), b16(256)]
    Cst = f32(2 * 258).rearrange("p (d n) -> p d n", d=2)
    Cbf = [b16(2 * 258).rearrange("p (d n) -> p d n", d=2), b16(2 * 258).rearrange("p (d n) -> p d n", d=2)]
    MG = f32(4 * 258).rearrange("p (r n) -> p r n", r=4)
    HALO = f32(4 * 64).rearrange("p (r j b) -> p r j b", r=4, j=16)
    HLR = f32(64).rearrange("p (j b) -> p j b", j=16)
    (bmQT, bZC, bACC, bVE, bOT, bGZ, bGB, bGD, bC, bMG, bHALO, bHLR, bBND, bXTB, bRD) = (S.buf() for _ in range(15))
    bNUM = [S.buf(), S.buf()]; bSCb = [S.buf(), S.buf()]; bKWb = [S.buf(), S.buf()]; bCbf = [S.buf(), S.buf()]
    PCB = [[L["PS"][4], L["PS"][5]], [L["PS"][6], L["PS"][7]]]
    bPCB = [[L["PSb"][4], L["PSb"][5]], [L["PSb"][6], L["PSb"][7]]]
    assert off[0] <= 18432, off[0]

    def halo_exchange():
        S.op("dve", lambda e: e.tensor_copy(out=XTB[:, :, 0:2], in_=XT[:, :, 0:2]), reads=[XTall[0]], writes=[bXTB])
        S.op("dve", lambda e: e.tensor_copy(out=XTB[:, :, 2:4], in_=XT[:, :, 1022:1024]), reads=[XTall[7]], writes=[bXTB])
        for j in range(16):
            w, wb = wblock(c_mq + j * 128)
            p, pb = psum()

            def mm(e):
                ins = None
                for kt in range(KT):
                    ins = e.matmul(p[:, 0:4], lhsT=w[:, kt, :], rhs=XTB[:, kt, :], start=(kt == 0), stop=(kt == KT - 1))
                return ins
            S.op("pe", mm, reads=[wb, bXTB], writes=[pb])
            S.op("act", lambda e: e.activation(out=BND[:, j, :], in_=p[:, 0:4], func=AF.Copy), reads=[pb], writes=[bBND])
        S.dma("sp", halo_src.rearrange("p (j b) -> p j b", j=16), BND, reads=[bBND], writes=[HLS], key="HLS")
        S.cc(halo_src, halo_dst, groups, reads=[HLS], writes=[HLD], key="halo")

    def halo_select():
        S.dma("sp", HALO, halo_dst.rearrange("(r p) (j b) -> p r j b", p=128, j=16), reads=[HLD], writes=[bHALO], key="HALO")
        for side in range(2):
            mk = o_mk + 8 + side * 4
            src_lo = 2 if side == 0 else 0
            dst = HLR[:, :, side * 2:side * 2 + 2]
            for i in range(4):
                m = SM[:, mk + i:mk + i + 1]
                src = HALO[:, i, :, src_lo:src_lo + 2]
                if i == 0:
                    S.op("dve", lambda e: e.tensor_scalar_mul(out=dst, in0=src, scalar1=m), reads=[bHALO, CONST], writes=[bHLR])
                else:
                    S.op("dve", lambda e: e.scalar_tensor_tensor(out=dst, in0=src, scalar=m, in1=dst, op0=ALU.mult, op1=ALU.add),
                         reads=[bHALO, CONST], writes=[bHLR])

    def mlstm_gates():
        w, wb = wblock(c_g, 16)
        for t in range(NTT):
            p, pb = psum()

            def mm(e):
                ins = None
                for kt in range(KT):
                    ins = e.matmul(p[:, 0:16], lhsT=XT[:, kt, t * 128:(t + 1) * 128], rhs=w[:, kt, 0:16], start=(kt == 0),
                                   stop=(kt == KT - 1))
                return ins
            S.op("pe", mm, reads=[wb, XTall[t]], writes=[pb])
            S.op("dve", lambda e: e.tensor_tensor(out=GZ[:, t, :], in0=p[:, 0:16], in1=SM[:, o_gb:o_gb + 16], op=ALU.add),
                 reads=[pb, CONST], writes=[bGZ])
        S.op("act", lambda e: e.activation(out=GZ[:, :, 8:16], in_=GZ[:, :, 8:16], func=AF.Exp, scale=-1.0), reads=[bGZ], writes=[bGZ])
        S.op("act", lambda e: e.activation(out=GZ[:, :, 8:16], in_=GZ[:, :, 8:16], func=AF.Ln, bias=1.0), reads=[bGZ], writes=[bGZ])
        S.op("dve", lambda e: e.tensor_scalar_mul(out=GZ[:, :, 8:16], in0=GZ[:, :, 8:16], scalar1=-1.0), reads=[bGZ], writes=[bGZ])
        for t in range(NTT):
            p, pb = psum()

            def mm(e):
                e.matmul(p[:, 0:4], lhsT=tri[0][:], rhs=GZ[:, t, 8:12], start=True, stop=True)
                e.matmul(p[:, 4:8], lhsT=tri[1][:], rhs=GZ[:, t, 12:16], start=True, stop=True)
                return e.matmul(p[:, 8:16], lhsT=ones[:], rhs=GZ[:, t, 8:16], start=True, stop=True)
            S.op("pe", mm, reads=[bGZ, CONST], writes=[pb])
            S.op("act", lambda e: e.activation(out=GB_[:, t, :], in_=p[:, 0:16], func=AF.Copy), reads=[pb], writes=[bGB])
        S.op("dve", lambda e: e.tensor_tensor(out=GD[:, 0], in0=GZ[:, :, 0:8], in1=GB_[:, :, 0:8], op=ALU.subtract), reads=[bGZ, bGB],
             writes=[bGD])
        S.op("act", lambda e: e.activation(out=GD[:, 1], in_=GB_[:, :, 0:8], func=AF.Exp), reads=[bGB], writes=[bGD])
        S.op("dve", lambda e: e.tensor_tensor(out=GD[:, 2], in0=GD[:, 0], in1=GB_[:, :, 8:16], op=ALU.add), reads=[bGB], writes=[bGD])
        S.op("act", lambda e: e.activation(out=GD[:, 2], in_=GD[:, 2], func=AF.Exp), reads=[bGD], writes=[bGD])
        S.op("act", lambda e: e.activation(out=GD[:, 0], in_=GD[:, 0], func=AF.Exp), reads=[bGD], writes=[bGD])
        S.op("act", lambda e: e.activation(out=GD[:, 3], in_=GB_[:, :, 8:16], func=AF.Exp), reads=[bGB], writes=[bGD])

    conv_tick = [lambda: None]

    def conv_tile(col0, j, dstT, dstb, scale):
        proj_fm(col0, lambda half, p, pb: S.op("act", lambda e: e.activation(out=ZC[:, 2 + half * 512:2 + (half + 1) * 512], in_=p[:],
                                                                              func=AF.Copy), reads=[pb], writes=[bZC]))
        S.op("dve", lambda e: e.tensor_copy(out=ZC[:, 0:2], in_=HLR[:, j, 0:2]), reads=[bHLR], writes=[bZC])
        S.op("dve", lambda e: e.tensor_copy(out=ZC[:, 1026:1028], in_=HLR[:, j, 2:4]), reads=[bHLR], writes=[bZC])
        cw = o_cw + j * 5
        S.op("dve", lambda e: e.tensor_scalar(out=ACC, in0=ZC[:, 0:1024], scalar1=SM[:, cw:cw + 1], scalar2=SM[:, o_cb + j:o_cb + j + 1],
                                              op0=ALU.mult, op1=ALU.add), reads=[bZC, CONST], writes=[bACC])
        conv_tick[0]()
        for k in range(1, 5):
            S.op("dve", lambda e, k=k: e.scalar_tensor_tensor(out=ACC, in0=ZC[:, k:k + 1024], scalar=SM[:, cw + k:cw + k + 1], in1=ACC,
                                                              op0=ALU.mult, op1=ALU.add), reads=[bZC, CONST], writes=[bACC])
            conv_tick[0]()
        if scale == 1.0:
            S.op("act", lambda e: e.activation(out=dstT, in_=ACC, func=AF.Silu), reads=[bACC], writes=[dstb])
        else:
            S.op("act", lambda e: e.activation(out=ACC, in_=ACC, func=AF.Silu), reads=[bACC], writes=[bACC])
            S.op("dve", lambda e: e.tensor_scalar_mul(out=dstT, in0=ACC, scalar1=scale), reads=[bACC], writes=[dstb])

    def mlstm_head(h, mode):
        wa, wab = wblock(c_mv + h * 256)
        wb_, wbb = wblock(c_mv + h * 256 + 128)
        S.op("dve", lambda e: e.memset(VE[:, :, 256:257], 1.0), writes=[bVE])
        vstate = {"next": 0, "pend": None}

        def v_evac():
            if vstate["pend"] is not None:
                t, p, pb = vstate["pend"]
                S.op("act", lambda e: e.activation(out=VE[:, t, 0:256], in_=p[:, 0:256], func=AF.Copy), reads=[pb], writes=[bVE])
                vstate["pend"] = None

        def tick():
            v_evac()
            t = vstate["next"]
            if t >= NTT:
                return
            vstate["next"] = t + 1
            p, pb = L["PS"][4 + t % 4], L["PSb"][4 + t % 4]

            def mm(e):
                ins = None
                for (w, c0) in ((wa, 0), (wb_, 128)):
                    for kt in range(KT):
                        ins = e.matmul(p[:, c0:c0 + 128], lhsT=XT[:, kt, t * 128:(t + 1) * 128], rhs=w[:, kt, :], start=(kt == 0),
                                       stop=(kt == KT - 1))
                return ins
            S.op("pe", mm, reads=[wab, wbb, XTall[t]], writes=[pb])
            vstate["pend"] = (t, p, pb)
        conv_tick[0] = tick
        mKT, bmKT = mKTs[h], bmKTs[h]
        if mode == 1:
            for dt in range(2):
                conv_tile(c_mk + h * 256 + dt * 128, 8 + h * 2 + dt, mKT[:, dt, :], bmKT, 1.0)
        if mode == 2:
            for dt in range(2):
                conv_tile(c_mq + h * 256 + dt * 128, h * 2 + dt, mQT[:, dt, :], bmQT, 1.0 / 16)
            while vstate["next"] < NTT or vstate["pend"] is not None:
                tick()
            run_deferred()
            for dt in range(2):
                proj_fm(c_mo + h * 256 + dt * 128,
                        lambda half, p, pb: S.op("act", lambda e: e.activation(out=OT[:, dt, half * 512:(half + 1) * 512], in_=p[:],
                                                                                func=AF.Sigmoid), reads=[pb], writes=[bOT]))
                S.op("dve", lambda e: e.tensor_scalar_mul(out=OT[:, dt, :], in0=OT[:, dt, :],
                                                          scalar1=SM[:, o_mg + h * 2 + dt:o_mg + h * 2 + dt + 1]),
                     reads=[CONST], writes=[bOT])
        while vstate["next"] < NTT or vstate["pend"] is not None:
            tick()
        conv_tick[0] = lambda: None
        for di in range(2):
            g = di * 4 + h
            sk, so = 2 + di * 2 + h // 2, (h % 2) * 516
            if mode == 1:
                S.op("dve", lambda e: e.memset(Cst, 0.0), writes=[bC])
            else:
                for dt in range(2):
                    S.dma("sp", MG, st_dst[sk].rearrange("(r p) c -> p r c", p=128)[:, :, so + dt * 258:so + dt * 258 + 258],
                          reads=[STD[sk]], writes=[bMG], key="MG")
                    combine(MG, 257, Cst[:, dt, 0:257], bC, bMG, di)
                S.op("act", lambda e: e.activation(out=Cbf[0][:, :, 0:257], in_=Cst[:, :, 0:257], func=AF.Copy), reads=[bC],
                     writes=[bCbf[0]])
            order = list(range(NTT)) if di == 0 else list(range(NTT - 1, -1, -1))

            def stA(i):
                t = order[i]
                ts = slice(t * 128, (t + 1) * 128)
                if mode == 2:
                    ps_, psb = psum()

                    def mm(e):
                        e.matmul(ps_[:, 0:128], lhsT=mKT[:, 0, ts], rhs=mQT[:, 0, ts], start=True, stop=False)
                        return e.matmul(ps_[:, 0:128], lhsT=mKT[:, 1, ts], rhs=mQT[:, 1, ts], start=False, stop=True)
                    S.op("pe", mm, reads=[bmKT, bmQT], writes=[psb])
                    S.op("dve", lambda e: e.scalar_tensor_tensor(out=SCb[i % 2], in0=ps_[:, 0:128], scalar=GD[:, 0, t, g:g + 1],
                                                                 in1=tri[di][:], op0=ALU.mult, op1=ALU.mult),
                         reads=[psb, bGD, CONST], writes=[bSCb[i % 2]])
                pt, ptb = psum()
                pT = pt[:].bitcast(BF16)

                def trn(e):
                    e.transpose(out=pT[:, 0:128], in_=mKT[:, 0, ts], identity=ident[:])
                    return e.transpose(out=pT[:, 128:256], in_=mKT[:, 1, ts], identity=ident[:])
                S.op("pe", trn, reads=[bmKT, CONST], writes=[ptb])
                S.op("act", lambda e: e.activation(out=KWb[i % 2], in_=pT[:, 0:256], func=AF.Copy, scale=GD[:, 2, t, g:g + 1]),
                     reads=[ptb, bGD], writes=[bKWb[i % 2]])
                for dt in range(2):
                    S.op("pe", lambda e: e.matmul(PCB[i % 2][dt][:, 0:257], lhsT=KWb[i % 2][:, dt * 128:(dt + 1) * 128],
                                                  rhs=VE[:, t, 0:257], start=True, stop=True), reads=[bKWb[i % 2], bVE],
                         writes=[bPCB[i % 2][dt]])

            def stB(i):
                t = order[i]
                ts = slice(t * 128, (t + 1) * 128)
                for dt in range(2):
                    S.op("dve", lambda e: e.scalar_tensor_tensor(out=Cst[:, dt, 0:257], in0=Cst[:, dt, 0:257],
                                                                 scalar=GD[:, 3, t, g:g + 1], in1=PCB[i % 2][dt][:, 0:257],
                                                                 op0=ALU.mult, op1=ALU.add), reads=[bPCB[i % 2][dt], bGD],
                         writes=[bC])
                if mode == 2:
                    if i < NTT - 1:
                        S.op("act", lambda e: e.activation(out=Cbf[(i + 1) % 2][:, :, 0:257], in_=Cst[:, :, 0:257], func=AF.Copy),
                             reads=[bC], writes=[bCbf[(i + 1) % 2]])
                    po, pob = psum()
                    cb = Cbf[i % 2]

                    def mm2(e):
                        e.matmul(po[:, 0:257], lhsT=SCb[i % 2], rhs=VE[:, t, 0:257], start=True, stop=False)
                        e.matmul(po[:, 0:257], lhsT=mQT[:, 0, ts], rhs=cb[:, 0, 0:257], start=False, stop=False)
                        return e.matmul(po[:, 0:257], lhsT=mQT[:, 1, ts], rhs=cb[:, 1, 0:257], start=False, stop=True)
                    S.op("pe", mm2, reads=[bSCb[i % 2], bVE, bmQT, bCbf[i % 2]], writes=[pob])
                    S.op("act", lambda e: e.activation(out=NUM[di][:, t, :], in_=po[:, 0:257], func=AF.Copy,
                                                       scale=GD[:, 1, t, g:g + 1]), reads=[pob, bGD], writes=[bNUM[di]])

            stA(0)
            for i in range(NTT):
                if i + 1 < NTT:
                    stA(i + 1)
                stB(i)
            if mode == 1:
                S.op("dve", lambda e: e.tensor_reduce(out=ST8[:, 48:49], in_=GB_[:, :, 8 + g], axis=AX.X, op=ALU.add), reads=[bGB],
                     writes=[ST8b_])
                for dt in range(2):
                    S.op("act", lambda e: e.activation(out=Cst[:, dt, 257:258], in_=ST8[:, 48:49], func=AF.Exp), reads=[ST8b_],
                         writes=[bC])
                S.dma("sp", st_src[sk][:, so:so + 516].rearrange("p (d n) -> p d n", d=2), Cst, reads=[bC], writes=[STS[sk]],
                      key="STS%d" % sk)
        if mode == 2:
            for di in range(2):
                c0 = di * 8
                S.op("act", lambda e: e.activation(out=RD[:, c0:c0 + 8], in_=NUM[di][:, :, 256], func=AF.Abs), reads=[bNUM[di]],
                     writes=[bRD])
                S.op("dve", lambda e: e.tensor_scalar_max(out=RD[:, c0:c0 + 8], in0=RD[:, c0:c0 + 8], scalar1=1.0), reads=[bRD],
                     writes=[bRD])
                S.op("dve", lambda e: e.reciprocal(out=RD[:, c0:c0 + 8], in_=RD[:, c0:c0 + 8]), reads=[bRD], writes=[bRD])
                S.op("dve", lambda e: e.tensor_tensor(out=NUM[di][:, :, 0:256], in0=NUM[di][:, :, 0:256],
                                                      in1=RD[:, c0:c0 + 8].unsqueeze(2).to_broadcast([128, 8, 256]), op=ALU.mult),
                     reads=[bRD], writes=[bNUM[di]])
            Hv = NUM[0][:, :, 0:256]
            S.op("dve", lambda e: e.tensor_tensor(out=Hv, in0=Hv, in1=NUM[1][:, :, 0:256], op=ALU.add), reads=[bNUM[1]],
                 writes=[bNUM[0]])
            S.op("dve", lambda e: e.tensor_reduce(out=RD[:, 16:24], in_=Hv, axis=AX.X, op=ALU.add), reads=[bNUM[0]], writes=[bRD])
            S.op("dve", lambda e: e.tensor_scalar_mul(out=RD[:, 16:24], in0=RD[:, 16:24], scalar1=-1.0 / 256), reads=[bRD],
                 writes=[bRD])
            S.op("dve", lambda e: e.tensor_tensor(out=Hv, in0=Hv, in1=RD[:, 16:24].unsqueeze(2).to_broadcast([128, 8, 256]),
                                                  op=ALU.add), reads=[bRD], writes=[bNUM[0]])
            S.op("dve", lambda e: e.tensor_tensor(out=NUM[1][:, :, 0:256], in0=Hv, in1=Hv, op=ALU.mult), reads=[bNUM[0]],
                 writes=[bNUM[1]])
            S.op("dve", lambda e: e.tensor_reduce(out=RD[:, 24:32], in_=NUM[1][:, :, 0:256], axis=AX.X, op=ALU.add), reads=[bNUM[1]],
                 writes=[bRD])
            S.op("dve", lambda e: e.tensor_scalar(out=RD[:, 24:32], in0=RD[:, 24:32], scalar1=1.0 / 256, scalar2=NORM_EPS,
                                                  op0=ALU.mult, op1=ALU.add), reads=[bRD], writes=[bRD])
            S.op("act", lambda e: e.activation(out=RD[:, 32:40], in_=RD[:, 24:32], func=AF.Sqrt), reads=[bRD], writes=[bRD])
            S.op("dve", lambda e: e.reciprocal(out=RD[:, 40:48], in_=RD[:, 32:40]), reads=[bRD], writes=[bRD])
            S.op("dve", lambda e: e.tensor_tensor(out=HNb, in0=Hv, in1=RD[:, 40:48].unsqueeze(2).to_broadcast([128, 8, 256]),
                                                  op=ALU.mult), reads=[bRD, bNUM[0]], writes=[bNUM[1]])
            def epi_pe(h=h):
                for dt in range(2):
                    p, pb = psum()
                    pT = p[:].bitcast(BF16)

                    def trn(e):
                        ins = None
                        for t in range(NTT):
                            ins = e.transpose(out=pT[:, t * 128:(t + 1) * 128], in_=HNb[:, t, dt * 128:(dt + 1) * 128],
                                              identity=ident[:])
                        return ins
                    S.op("pe", trn, reads=[bNUM[1], CONST], writes=[pb])
                    S.op("dve", lambda e: e.tensor_tensor(out=YT[:, 8 + h * 2 + dt, :], in0=pT[:, 0:1024], in1=OT[:, dt, :],
                                                          op=ALU.mult), reads=[pb, bOT], writes=[YTb])
            deferred.append(epi_pe)

    stop = L["stop"]
    if stop != "B0":
        halo_exchange()
    if stop == "B1":
        return
    hgrn_init()
    ps_pool[0] = 4
    for h in range(8):
        hgrn_head(h, 1)
    ps_pool[0] = 8
    for k in range(2):
        S.cc(st_src[k], st_dst[k], groups, reads=[STS[k]], writes=[STD[k]], key="st%d" % k)
    S.barrier()
    if stop in ("B2", "B0"):
        return
    halo_select()
    mlstm_gates()
    if stop == "B3":
        return
    ps_pool[0] = 4
    wr_n[0] = 4
    for h in range(4):
        mlstm_head(h, 1)
        if h % 2 == 1:
            for di in range(2):
                k = 2 + di * 2 + h // 2
                S.cc(st_src[k], st_dst[k], groups, reads=[STS[k]], writes=[STD[k]], key="st%d" % k)
    if stop == "B5":
        return
    for h in range(4):
        mlstm_head(h, 2)
    run_deferred()
    wr_n[0] = 3
    S.barrier()
    hgrn_init()
    ps_pool[0] = 4
    for h in range(8):
        hgrn_head(h, 2)
    run_deferred()
    ps_pool[0] = 8
    if debug:
        S.barrier()
        DBG = B[:, 0:16384]
        for half in range(2):
            S.op("dve", lambda e: e.tensor_copy(out=DBG[:, 0:8192], in_=Ab[:, half * 8192:(half + 1) * 8192]), writes=[bFK])
            S.dma("sp", dbg["yT"][:, half * 8192:(half + 1) * 8192], DBG[:, 0:8192], reads=[bFK], writes=[bFK], key="dbgy")


def _small(inputs, core):
    r = core % 4
    sm = np.zeros((128, 192), np.float32)
    lb = np.asarray(inputs["hgrn_lb"], np.float32)
    for di in range(2):
        for h in range(8):
            sm[:, 0 + di * 8 + h] = lb[di, 0, h * 128:(h + 1) * 128]
            sm[:, 16 + di * 8 + h] = lb[di, 1, h * 128:(h + 1) * 128]
    sm[:, 32:40] = np.asarray(inputs["hgrn_norm_g"], np.float32).reshape(8, 128).T
    sm[:, 40:48] = np.asarray(inputs["mlstm_norm_g"], np.float32).reshape(8, 128).T
    sm[:, 48:64] = np.asarray(inputs["mlstm_conv_b"], np.float32).reshape(16, 128).T
    cw = np.asarray(inputs["mlstm_conv_w"], np.float32).reshape(5, 16, 128)
    sm[:, 64:144] = cw.transpose(2, 1, 0).reshape(128, 80)
    ig = np.asarray(inputs["mlstm_ig_b"], np.float32).reshape(2, 4)
    fg = np.asarray(inputs["mlstm_fg_b"], np.float32).reshape(2, 4)
    sm[:, 144:160] = np.concatenate([ig[0], ig[1], fg[0], fg[1]])[None, :]
    for i in range(4):
        sm[:, 160 + i] = 1.0 if i < r else 0.0
        sm[:, 164 + i] = 1.0 if i > r else 0.0
        sm[:, 168 + i] = 1.0 if i == r - 1 else 0.0
        sm[:, 172 + i] = 1.0 if i == r + 1 else 0.0
        sm[:, 176 + i] = 0.0 if i < r else 1.0
        sm[:, 180 + i] = 0.0 if i > r else 1.0
    return sm


def _in_maps(inputs):
    f = lambda a: np.ascontiguousarray(np.asarray(a, np.float32))
    x = f(inputs["x"]).reshape(8, NT, D)
    shared = {
        "ffn1_w1": f(inputs["ffn1_w1"]).reshape(D, DFF), "ffn1_w3": f(inputs["ffn1_w3"]).reshape(D, DFF),
        "ffn1_w2": f(inputs["ffn1_w2"]).reshape(DFF, D), "ffn2_w1": f(inputs["ffn2_w1"]).reshape(D, DFF),
        "ffn2_w3": f(inputs["ffn2_w3"]).reshape(D, DFF), "ffn2_w2": f(inputs["ffn2_w2"]).reshape(DFF, D),
        "w_in": f(inputs["w_in"]).reshape(D, INC), "w_out": f(inputs["w_out"]).reshape(D, D),
    }
    lnp = {"ln1_g": inputs["ln1_g"], "ln1_b": inputs["ln1_b"], "ln2_g": inputs["ln2_g"], "ln2_b": inputs["ln2_b"],
           "ln3_g": inputs["ln3_g"], "ln3_b": inputs["ln3_b"]}
    for k, v in lnp.items():
        shared[k] = np.ascontiguousarray(np.broadcast_to(f(v).reshape(1, D), (128, D)))
    maps = []
    for c in range(8):
        m = dict(shared)
        m["x"] = x[c]
        m["small"] = _small(inputs, c)
        maps.append(m)
    return maps


_NC_CACHE = {}


def kernel(**inputs):
    if "nc" not in _NC_CACHE:
        _NC_CACHE["nc"] = build(False)
    nc = _NC_CACHE["nc"]
    res = run_bass_kernel_spmd(nc, _in_maps(inputs), core_ids=list(range(8)))
    out = np.stack([np.asarray(r["out"], np.float32) for r in res.results], axis=0)
    return out.reshape(2, 4096, D)
```

```python
import numpy as np
import concourse.bass as bass
import concourse.mybir as mybir
from concourse.bass_utils import run_bass_kernel_spmd

F32 = mybir.dt.float32
BF16 = mybir.dt.bfloat16
AF = mybir.ActivationFunctionType
ALU = mybir.AluOpType
AX = mybir.AxisListType

D = 2048
DFF = 5632
NT = 1024
NTT = 8
KT = 16
INC = 9232
ALPHA = 2.0 ** 0.25
LN_EPS = 1e-5
NORM_EPS = 1e-6
NCH = DFF // 512
HG_OFF = 0
ML_OFF = 16 * 129
ST_COLS = 16 * 129 + 8 * 516


class Buf:
    __slots__ = ("name", "w", "r")

    def __init__(self, name):
        self.name = name
        self.w = None
        self.r = {}


class Sched:
    def __init__(self, nc):
        self.nc = nc
        self.E = {"pe": nc.tensor, "act": nc.scalar, "dve": nc.vector, "pool": nc.gpsimd, "sp": nc.sync}
        self.sem = {e: nc.alloc_semaphore("e_" + e) for e in self.E}
        self.cnt = {e: 0 for e in self.E}
        self.seen = {e: {} for e in self.E}
        self.dsem = {}
        self.nbuf = 0

    def buf(self, name=None):
        self.nbuf += 1
        return Buf(name or ("b%d" % self.nbuf))

    def _deps(self, reads, writes):
        d = []
        for b in reads:
            if b.w is not None:
                d.append(b.w)
        for b in writes:
            if b.w is not None:
                d.append(b.w)
            d.extend(b.r.values())
        return d

    def _wait(self, e, toks):
        best = {}
        for t in toks:
            k = t[2]
            if self.seen[e].get(k, 0) >= t[1]:
                continue
            if k not in best or best[k][1] < t[1]:
                best[k] = t
        for k, t in best.items():
            self.E[e].wait_ge(t[0], t[1])
            self.seen[e][k] = t[1]

    def _commit(self, tok, reads, writes):
        for b in reads:
            b.r[tok[2]] = tok
        for b in writes:
            b.w = tok
            b.r = {}

    def op(self, e, fn, reads=(), writes=()):
        self._wait(e, self._deps(reads, writes))
        ins = fn(self.E[e])
        self.cnt[e] += 1
        ins.then_inc(self.sem[e], 1)
        tok = (self.sem[e], self.cnt[e], e)
        self._commit(tok, reads, writes)
        return tok

    def dma(self, q, out, in_, reads=(), writes=(), key=None, slow=False):
        self._wait(q, self._deps(reads, writes))
        if key not in self.dsem:
            self.dsem[key] = [self.nc.alloc_semaphore("d_" + key), 0]
        s = self.dsem[key]
        s[1] += 16
        if slow:
            self.E[q].dma_start(out=out, in_=in_, allow_slow_non_contiguous=True).then_inc(s[0], 16)
        else:
            self.E[q].dma_start(out=out, in_=in_).then_inc(s[0], 16)
        tok = (s[0], s[1], "d_" + key)
        self._commit(tok, reads, writes)
        return tok

    def cc(self, src, dst, groups, reads=(), writes=(), key="cc"):
        q = "pool"
        self._wait(q, self._deps(reads, writes))
        sem = self.nc.alloc_semaphore("c_" + key)
        self.E[q].collective_compute("AllGather", ALU.bypass, replica_groups=groups,
                                     ins=[src], outs=[dst]).then_inc(sem, 1)
        tok = (sem, 1, "c_" + key)
        self._commit(tok, reads, writes)
        return tok

    def all_tokens(self):
        toks = [(self.sem[e], self.cnt[e], e) for e in self.E if self.cnt[e] > 0]
        toks += [(s[0], s[1], "d_" + k) for k, s in self.dsem.items() if s[1] > 0]
        return toks

    def barrier(self):
        toks = self.all_tokens()
        for e in self.E:
            self._wait(e, toks)


def build(debug=False, stop=None, skip_ffn=False):
    nc = bass.Bass("TRN2", target_bir_lowering=False)
    S = Sched(nc)

    def din(name, shape):
        return nc.dram_tensor(name, shape, F32, kind="ExternalInput").ap()

    x_d = din("x", [NT, D])
    if not skip_ffn:
        w1 = [din("ffn1_w1", [D, DFF]), din("ffn2_w1", [D, DFF])]
        w3 = [din("ffn1_w3", [D, DFF]), din("ffn2_w3", [D, DFF])]
        w2 = [din("ffn1_w2", [DFF, D]), din("ffn2_w2", [DFF, D])]
    win_d = din("w_in", [D, INC])
    wout_d = din("w_out", [D, D])
    lng_d = [din("ln%d_g" % i, [128, D]) for i in (1, 2, 3)]
    lnb_d = [din("ln%d_b" % i, [128, D]) for i in (1, 2, 3)]
    NSM = 192
    small_d = din("small", [128, NSM])
    out_d = nc.dram_tensor("out", [NT, D], F32, kind="ExternalOutput").ap()
    halo_src = nc.dram_tensor("halo_src", [128, 64], F32, kind="Internal", addr_space="Local").ap()
    halo_dst = nc.dram_tensor("halo_dst", [4 * 128, 64], F32, kind="Internal", addr_space="Local").ap()
    st_src = [nc.dram_tensor("st_src%d" % k, [128, 1032], F32, kind="Internal", addr_space="Local").ap() for k in range(6)]
    st_dst = [nc.dram_tensor("st_dst%d" % k, [4 * 128, 1032], F32, kind="Internal", addr_space="Local").ap() for k in range(6)]
    dbg = {}
    if debug:
        dbg["x1"] = nc.dram_tensor("dbg_x1", [NT, D], F32, kind="ExternalOutput").ap()
        dbg["yT"] = nc.dram_tensor("dbg_yT", [128, KT * NT], F32, kind="ExternalOutput").ap()
        dbg["x2"] = nc.dram_tensor("dbg_x2", [NT, D], F32, kind="ExternalOutput").ap()
    groups = [[0, 1, 2, 3], [4, 5, 6, 7]]

    X = nc.alloc_sbuf_tensor("X", [128, NTT, D], F32)
    XT = nc.alloc_sbuf_tensor("XT", [128, KT, NT], BF16)
    A = nc.alloc_sbuf_tensor("A", [128, 8192], F32)
    B = nc.alloc_sbuf_tensor("B", [128, 18432], F32)
    ident = nc.alloc_sbuf_tensor("ident", [128, 128], BF16)
    tri = [nc.alloc_sbuf_tensor("tri_fw", [128, 128], F32), nc.alloc_sbuf_tensor("tri_bw", [128, 128], F32)]
    ones = nc.alloc_sbuf_tensor("ones", [128, 128], F32)
    SM = nc.alloc_sbuf_tensor("SM", [128, NSM], F32)
    ST8 = nc.alloc_sbuf_tensor("ST8", [128, 64], F32)
    Xb = [S.buf("X%d" % t) for t in range(NTT)]
    XTb = [S.buf("XT%d" % t) for t in range(NTT)]
    CONST = S.buf("CONST")
    ST8b = S.buf("ST8")

    PS = [nc.alloc_psum_tensor("ps%d" % i, [128, 512], F32) for i in range(8)]
    PSb = [S.buf("ps%d" % i) for i in range(8)]
    ps_next = [0]
    ps_pool = [8]

    def psum():
        i = ps_next[0] % ps_pool[0]
        ps_next[0] += 1
        return PS[i], PSb[i]

    o_lb = 0
    o_hg = 32
    o_mg = 40
    o_cb = 48
    o_cw = 64
    o_gb = 144
    o_mk = 160

    S.dma("sp", SM[:], small_d, writes=[CONST], key="CONST")
    S.op("dve", lambda e: e.memset(ident[:], 1.0), writes=[CONST])
    S.op("dve", lambda e: e.memset(tri[0][:], 1.0), writes=[CONST])
    S.op("dve", lambda e: e.memset(tri[1][:], 1.0), writes=[CONST])
    S.op("dve", lambda e: e.memset(ones[:], 1.0), writes=[CONST])
    S.op("pool", lambda e: e.affine_select(out=ident[:], in_=ident[:], pattern=[[-1, 128]], compare_op=ALU.is_equal,
                                           fill=0.0, base=0, channel_multiplier=1), writes=[CONST])
    S.op("pool", lambda e: e.affine_select(out=tri[0][:], in_=tri[0][:], pattern=[[1, 128]], compare_op=ALU.is_ge,
                                           fill=0.0, base=0, channel_multiplier=-1), writes=[CONST])
    S.op("pool", lambda e: e.affine_select(out=tri[1][:], in_=tri[1][:], pattern=[[-1, 128]], compare_op=ALU.is_ge,
                                           fill=0.0, base=0, channel_multiplier=1), writes=[CONST])

    xv = x_d.rearrange("(t p) d -> p t d", p=128)
    for t in range(NTT):
        S.dma("sp", X[:, t, :], xv[:, t, :], writes=[Xb[t]], key="X%d" % t)

    Ab = A[:].bitcast(BF16)
    Bb = B[:].bitcast(BF16)
    HT = [Ab[:, 0:4096].rearrange("p (j n) -> p j n", j=4), Ab[:, 4096:8192].rearrange("p (j n) -> p j n", j=4)]
    HTb = [[S.buf("HT%d_%d" % (i, h)) for h in range(2)] for i in range(2)]
    SS = [A[:, 4096:4608], A[:, 4608:5120]]
    SSb = [S.buf("SS0"), S.buf("SS1")]
    XBc = Ab[:, 10240:12288]
    XBb = S.buf("XB")
    G = A[:, 6144:8192]
    Gb = S.buf("G")
    RING = [Bb[:, i * 8192:(i + 1) * 8192] for i in range(4)]
    RINGb = [S.buf("R%d" % i) for i in range(4)]
    Bt = B[:, 16384:18432]
    Btb = S.buf("Bt")
    ring_next = [0]

    def ring():
        i = ring_next[0] % 4
        ring_next[0] += 1
        return RING[i], RINGb[i]

    def make_xt(t):
        S.op("act", lambda e: e.activation(out=XBc, in_=X[:, t, :], func=AF.Copy), reads=[Xb[t]], writes=[XBb])
        for half in range(2):
            p, pb = psum()
            pv = p[:].bitcast(BF16).rearrange("p (k n) -> p k n", k=8)

            def tr(e, half=half, pv=pv):
                ins = None
                for k in range(8):
                    kt = half * 8 + k
                    ins = e.transpose(out=pv[:, k, :], in_=XBc[:, kt * 128:(kt + 1) * 128], identity=ident[:])
                return ins
            S.op("pe", tr, reads=[XBb, CONST], writes=[pb])
            S.op("dve", lambda e, half=half, pv=pv: e.tensor_copy(out=XT[:, half * 8:(half + 1) * 8, t * 128:(t + 1) * 128],
                                                                   in_=pv), reads=[pb], writes=[XTb[t]])

    def layer_norm(t, g_d, b_d, first):
        if first:
            S.dma("sp", G, g_d, writes=[Gb], key="G")
            S.dma("sp", Bt, b_d, writes=[Btb], key="Bt")
        xt_ = X[:, t, :]
        c = t * 4
        S.op("dve", lambda e: e.reduce_sum(out=ST8[:, c:c + 1], in_=xt_, axis=AX.X), reads=[Xb[t]], writes=[ST8b])
        S.op("dve", lambda e: e.tensor_scalar_mul(out=ST8[:, c:c + 1], in0=ST8[:, c:c + 1], scalar1=-1.0 / D),
             reads=[ST8b], writes=[ST8b])
        S.op("dve", lambda e: e.tensor_scalar_add(out=xt_, in0=xt_, scalar1=ST8[:, c:c + 1]), reads=[ST8b], writes=[Xb[t]])
        S.op("act", lambda e: e.activation(out=XBc, in_=xt_, func=AF.Square, accum_out=ST8[:, c + 1:c + 2]),
             reads=[Xb[t]], writes=[XBb, ST8b])
        S.op("dve", lambda e: e.tensor_scalar(out=ST8[:, c + 1:c + 2], in0=ST8[:, c + 1:c + 2], scalar1=1.0 / D,
                                              scalar2=LN_EPS, op0=ALU.mult, op1=ALU.add), reads=[ST8b], writes=[ST8b])
        S.op("act", lambda e: e.activation(out=ST8[:, c + 2:c + 3], in_=ST8[:, c + 1:c + 2], func=AF.Sqrt),
             reads=[ST8b], writes=[ST8b])
        S.op("dve", lambda e: e.reciprocal(out=ST8[:, c + 3:c + 4], in_=ST8[:, c + 2:c + 3]), reads=[ST8b], writes=[ST8b])
        S.op("dve", lambda e: e.scalar_tensor_tensor(out=xt_, in0=xt_, scalar=ST8[:, c + 3:c + 4], in1=G,
                                                     op0=ALU.mult, op1=ALU.mult), reads=[ST8b, Gb], writes=[Xb[t]])
        S.op("dve", lambda e: e.tensor_tensor(out=xt_, in0=xt_, in1=Bt, op=ALU.add), reads=[Btb], writes=[Xb[t]])

    def ffn(l, post=None):
        for t in range(NTT):
            S.op("dve", lambda e, t=t: e.tensor_scalar_mul(out=X[:, t, :], in0=X[:, t, :], scalar1=ALPHA),
                 reads=[XBb], writes=[Xb[t]])
        w1v = w1[l].rearrange("(kt p) n -> p kt n", p=128)
        w3v = w3[l].rearrange("(kt p) n -> p kt n", p=128)
        w2v = w2[l].rearrange("(j p) n -> p j n", p=128)

        def h_stage(c):
            r1, r1b = ring()
            S.dma("pool", r1.rearrange("p (k n) -> p k n", k=16), w1v[:, :, c * 512:(c + 1) * 512], writes=[r1b], key=r1b.name)
            r3, r3b = ring()
            S.dma("pool", r3.rearrange("p (k n) -> p k n", k=16), w3v[:, :, c * 512:(c + 1) * 512], writes=[r3b], key=r3b.name)
            r1v = r1.rearrange("p (k n) -> p k n", k=16)
            r3v = r3.rearrange("p (k n) -> p k n", k=16)
            hp = c % 2
            for j in range(4):
                for half in range(2):
                    p1, p1b = psum()
                    p3, p3b = psum()

                    def mm(e, rv, p):
                        ins = None
                        for kt in range(KT):
                            ins = e.matmul(p[:], lhsT=rv[:, kt, j * 128:(j + 1) * 128],
                                           rhs=XT[:, kt, half * 512:(half + 1) * 512], start=(kt == 0), stop=(kt == KT - 1))
                        return ins
                    xr = XTb[half * 4:(half + 1) * 4]
                    S.op("pe", lambda e: mm(e, r1v, p1), reads=[r1b] + xr, writes=[p1b])
                    S.op("pe", lambda e: mm(e, r3v, p3), reads=[r3b] + xr, writes=[p3b])
                    si = (j * 2 + half) % 2
                    S.op("act", lambda e: e.activation(out=SS[si], in_=p1[:], func=AF.Silu), reads=[p1b], writes=[SSb[si]])
                    S.op("dve", lambda e: e.tensor_tensor(out=HT[hp][:, j, half * 512:(half + 1) * 512], in0=p3[:], in1=SS[si],
                                                          op=ALU.mult), reads=[p3b, SSb[si]], writes=[HTb[hp][half]])

        def o_stage(c):
            r2, r2b = ring()
            r2v = r2.rearrange("p (j n) -> p j n", j=4)
            S.dma("pool", r2v, w2v[:, c * 4:(c + 1) * 4, :], writes=[r2b], key=r2b.name)
            hp = c % 2
            for t in range(NTT):
                for cb in range(4):
                    po, pob = psum()

                    def mm(e):
                        ins = None
                        for j in range(4):
                            ins = e.matmul(po[:], lhsT=HT[hp][:, j, t * 128:(t + 1) * 128], rhs=r2v[:, j, cb * 512:(cb + 1) * 512],
                                           start=(j == 0), stop=(j == 3))
                        return ins
                    S.op("pe", mm, reads=[r2b, HTb[hp][t // 4]], writes=[pob])
                    xs = X[:, t, cb * 512:(cb + 1) * 512]
                    S.op("dve", lambda e: e.scalar_tensor_tensor(out=xs, in0=po[:], scalar=0.5, in1=xs, op0=ALU.mult, op1=ALU.add),
                         reads=[pob], writes=[Xb[t]])
                if post is not None and c == NCH - 1:
                    post(t)

        h_stage(0)
        for c in range(NCH):
            if c + 1 < NCH:
                h_stage(c + 1)
            o_stage(c)

    def ln_stage_major(g_d, b_d, tag, do_xt, store=None):
        XB8 = Ab.rearrange("p (t n) -> p t n", t=NTT)
        XB8b = [S.buf("XB8%s_%d" % (tag, t)) for t in range(NTT)]
        STt = [S.buf("STt%s%d" % (tag, t)) for t in range(NTT)]
        gi = (ring_next[0] + 3) % 4
        G2 = RING[gi].bitcast(F32)[:, 0:D]
        S.dma("sp", G2, g_d, writes=[RINGb[gi]], key="G2")
        S.dma("sp", Bt, b_d, writes=[Btb], key="Bt")
        TT = range(NTT)
        for t in TT:
            S.op("dve", lambda e, t=t: e.reduce_sum(out=ST8[:, t * 4:t * 4 + 1], in_=X[:, t, :], axis=AX.X), reads=[Xb[t]],
                 writes=[STt[t]])
        for t in TT:
            S.op("dve", lambda e, t=t: e.tensor_scalar_mul(out=ST8[:, t * 4:t * 4 + 1], in0=ST8[:, t * 4:t * 4 + 1],
                                                           scalar1=-1.0 / D), reads=[STt[t]], writes=[STt[t]])
        for t in TT:
            S.op("dve", lambda e, t=t: e.tensor_scalar_add(out=X[:, t, :], in0=X[:, t, :], scalar1=ST8[:, t * 4:t * 4 + 1]),
                 reads=[STt[t]], writes=[Xb[t]])
        for t in TT:
            S.op("act", lambda e, t=t: e.activation(out=XB8[:, t, :], in_=X[:, t, :], func=AF.Square,
                                                    accum_out=ST8[:, t * 4 + 1:t * 4 + 2]), reads=[Xb[t]], writes=[XB8b[t], STt[t]])
        for t in TT:
            S.op("dve", lambda e, t=t: e.tensor_scalar(out=ST8[:, t * 4 + 1:t * 4 + 2], in0=ST8[:, t * 4 + 1:t * 4 + 2],
                                                       scalar1=1.0 / D, scalar2=LN_EPS, op0=ALU.mult, op1=ALU.add),
                 reads=[STt[t]], writes=[STt[t]])
        for t in TT:
            S.op("act", lambda e, t=t: e.activation(out=ST8[:, t * 4 + 2:t * 4 + 3], in_=ST8[:, t * 4 + 1:t * 4 + 2], func=AF.Sqrt),
                 reads=[STt[t]], writes=[STt[t]])
        for t in TT:
            S.op("dve", lambda e, t=t: e.reciprocal(out=ST8[:, t * 4 + 3:t * 4 + 4], in_=ST8[:, t * 4 + 2:t * 4 + 3]),
                 reads=[STt[t]], writes=[STt[t]])
        pend_xt = []
        for t in TT:
            S.op("dve", lambda e, t=t: e.scalar_tensor_tensor(out=X[:, t, :], in0=X[:, t, :], scalar=ST8[:, t * 4 + 3:t * 4 + 4],
                                                              in1=G2, op0=ALU.mult, op1=ALU.mult), reads=[STt[t], RINGb[gi]],
                 writes=[Xb[t]])
            S.op("dve", lambda e, t=t: e.tensor_tensor(out=X[:, t, :], in0=X[:, t, :], in1=Bt, op=ALU.add), reads=[Btb],
                 writes=[Xb[t]])
            if store is not None:
                store(t)
            if do_xt:
                S.op("act", lambda e, t=t: e.activation(out=XB8[:, t, :], in_=X[:, t, :], func=AF.Copy), reads=[Xb[t]],
                     writes=[XB8b[t]])
                for (tp, half, pv, pb) in pend_xt:
                    S.op("dve", lambda e, tp=tp, half=half, pv=pv: e.tensor_copy(
                        out=XT[:, half * 8:(half + 1) * 8, tp * 128:(tp + 1) * 128], in_=pv), reads=[pb], writes=[XTb[tp]])
                del pend_xt[:]
                for half in range(2):
                    p, pb = psum()
                    pv = p[:].bitcast(BF16).rearrange("p (k n) -> p k n", k=8)

                    def tr(e, t=t, half=half, pv=pv):
                        ins = None
                        for k in range(8):
                            kt = half * 8 + k
                            ins = e.transpose(out=pv[:, k, :], in_=XB8[:, t, kt * 128:(kt + 1) * 128], identity=ident[:])
                        return ins
                    S.op("pe", tr, reads=[XB8b[t], CONST], writes=[pb])
                    pend_xt.append((t, half, pv, pb))
        for (tp, half, pv, pb) in pend_xt:
            S.op("dve", lambda e, tp=tp, half=half, pv=pv: e.tensor_copy(
                out=XT[:, half * 8:(half + 1) * 8, tp * 128:(tp + 1) * 128], in_=pv), reads=[pb], writes=[XTb[tp]])
        S.barrier()

    for t in range(NTT):
        make_xt(t)
    if not skip_ffn:
        ffn(0)
        S.barrier()
        ln_stage_major(lng_d[0], lnb_d[0], "a", True)
    if debug:
        for t in range(NTT):
            S.dma("sp", dbg["x1"].rearrange("(t p) d -> p t d", p=128)[:, t, :], X[:, t, :], reads=[Xb[t]], key="dbgx1")
    S.barrier()
    if stop == "A":
        S._wait("sp", S.all_tokens())
        return nc

    mixer(nc, S, locals())

    S.barrier()
    if stop is not None and stop.startswith("B"):
        S._wait("sp", S.all_tokens())
        return nc
    YT = Ab.rearrange("p (k n) -> p k n", k=KT)
    YTb = S.buf("YTall")
    wov = wout_d.rearrange("(kt p) n -> p kt n", p=128)
    rs = []
    for i in range(4):
        r, rb = ring()
        rv = r.rearrange("p (k n) -> p k n", k=4)
        S.dma("pool", rv, wov[:, i * 4:(i + 1) * 4, :], writes=[rb], key=rb.name)
        rs.append((rv, rb, r))
    for t in range(NTT):
        S.op("dve", lambda e, t=t: e.tensor_scalar_mul(out=X[:, t, :], in0=X[:, t, :], scalar1=ALPHA), writes=[Xb[t]])
    for i in range(4):
        rv, rb, _ = rs[i]
        for t in range(NTT):
            for cb in range(4):
                po, pob = psum()

                def mm(e):
                    ins = None
                    for k in range(4):
                        ins = e.matmul(po[:], lhsT=YT[:, i * 4 + k, t * 128:(t + 1) * 128], rhs=rv[:, k, cb * 512:(cb + 1) * 512],
                                       start=(k == 0), stop=(k == 3))
                    return ins
                S.op("pe", mm, reads=[rb, YTb], writes=[pob])
                xs = X[:, t, cb * 512:(cb + 1) * 512]
                S.op("dve", lambda e: e.tensor_tensor(out=xs, in0=po[:], in1=xs, op=ALU.add), reads=[pob], writes=[Xb[t]])
    S.barrier()
    ln_stage_major(lng_d[1], lnb_d[1], "b", True)
    if debug:
        for t in range(NTT):
            S.dma("sp", dbg["x2"].rearrange("(t p) d -> p t d", p=128)[:, t, :], X[:, t, :], reads=[Xb[t]], key="dbgx2")

    ov = out_d.rearrange("(t p) d -> p t d", p=128)

    ffn(1)
    S.barrier()
    ln_stage_major(lng_d[2], lnb_d[2], "c", False,
                   store=lambda t: S.dma("sp", ov[:, t, :], X[:, t, :], reads=[Xb[t]], key="OUT"))
    S._wait("sp", S.all_tokens())
    return nc


def mixer(nc, S, L):
    X, XT, A, B, SM, ident, tri, ones, ST8 = (L[k] for k in ("X", "XT", "A", "B", "SM", "ident", "tri", "ones", "ST8"))
    XTb, CONST, psum, win_d, dbg, debug = (L[k] for k in ("XTb", "CONST", "psum", "win_d", "dbg", "debug"))
    halo_src, halo_dst, st_src, st_dst, groups = (L[k] for k in ("halo_src", "halo_dst", "st_src", "st_dst", "groups"))
    o_lb, o_hg, o_mg, o_cb, o_cw, o_gb, o_mk = (L[k] for k in ("o_lb", "o_hg", "o_mg", "o_cb", "o_cw", "o_gb", "o_mk"))
    Ab = A[:].bitcast(BF16)
    Bb = B[:].bitcast(BF16)
    YT = Ab.rearrange("p (k n) -> p k n", k=KT)
    YTb = S.buf("YT")
    wv = win_d.rearrange("(kt p) n -> p kt n", p=128)
    XTall = list(XTb)

    off = [0]

    def carve(n32):
        a = off[0]
        off[0] += n32
        assert off[0] <= 18432, off[0]
        return a
    WR = []
    for i in range(3):
        a = carve(1024)
        WR.append(B[:, a:a + 1024].bitcast(BF16).rearrange("p (k n) -> p k n", k=KT))
    WRb = [S.buf("W%d" % i) for i in range(3)]
    wr_next = [0]

    wr_n = [3]

    def wblock(col0, ncols=128):
        i = wr_next[0] % wr_n[0]
        wr_next[0] += 1
        S.dma("pool", WR[i][:, :, 0:ncols], wv[:, :, col0:col0 + ncols], writes=[WRb[i]], key=WRb[i].name)
        return WR[i], WRb[i]

    def f32(n):
        a = carve(n)
        return B[:, a:a + n]

    def b16(n):
        a = carve((n + 1) // 2)
        return B[:, a:a + (n + 1) // 2].bitcast(BF16)[:, 0:n]

    def proj_fm(col0, consume):
        w, wb = wblock(col0)
        for half in range(2):
            p, pb = psum()

            def mm(e):
                ins = None
                for kt in range(KT):
                    ins = e.matmul(p[:], lhsT=w[:, kt, :], rhs=XT[:, kt, half * 512:(half + 1) * 512],
                                   start=(kt == 0), stop=(kt == KT - 1))
                return ins
            S.op("pe", mm, reads=[wb] + XTall[half * 4:(half + 1) * 4], writes=[pb])
            consume(half, p, pb)

    BND = f32(64).rearrange("p (j b) -> p j b", j=16)
    XTB = b16(16 * 4).rearrange("p (k b) -> p k b", k=16)
    mark = off[0]

    FK = f32(1024); LOGF = f32(1024); BC = f32(1024); EE = f32(1024); QT = f32(1024)
    SQ16 = B[:, mark:mark + 2048].rearrange("p (c n) -> p c n", c=16)
    ON16 = B[:, mark + 2048:mark + 3072].bitcast(BF16).rearrange("p (c n) -> p c n", c=16)
    QA = [b16(1024), b16(1024)]; KA = [b16(1024), b16(1024)]; X3 = [b16(1024), b16(1024)]
    GG = b16(1024)
    VTb = b16(1024)
    bVT = [S.buf(), S.buf()]
    Vt = b16(16 * 128)
    OACC = f32(16 * 128)
    KALL = [EE.bitcast(BF16).rearrange("p (c n) -> p c n", c=16), QT.bitcast(BF16).rearrange("p (c n) -> p c n", c=16)]
    ATTA = [b16(16 * 64).rearrange("p (c n) -> p c n", c=16), b16(16 * 64).rearrange("p (c n) -> p c n", c=16)]
    SD = [f32(129), f32(129)]
    Sbf = [[b16(128), b16(128)] for _ in range(2)]
    EBL = [f32(16), f32(16)]
    RMASK = b16(1024); GIN = f32(4 * 129); DSEG = f32(4); RS = f32(64)
    Vv = Vt.rearrange("p (c n) -> p c n", c=16)
    Ov = OACC.rearrange("p (c n) -> p c n", c=16)
    GINv = GIN.rearrange("p (r n) -> p r n", r=4)
    bFK, bLOGF, bBC, bEE, bQT, bGG, bV, bO, bRM, bGIN, bDS, bRS = (S.buf() for _ in range(12))
    bQA = [S.buf(), S.buf()]; bKA = [S.buf(), S.buf()]; bX3 = [S.buf(), S.buf()]; bEBL = [S.buf(), S.buf()]
    bKALL = [[S.buf() for _ in range(4)] for _ in range(2)]; bATTA = [S.buf(), S.buf()]
    bSD = [S.buf(), S.buf()]; bSbf = [[S.buf(), S.buf()] for _ in range(2)]
    PSKV = [[L["PS"][4], L["PS"][5]], [L["PS"][6], L["PS"][7]]]
    bPSKV = [[L["PSb"][4], L["PSb"][5]], [L["PSb"][6], L["PSb"][7]]]
    ps_pool = L["ps_pool"]

    def hgrn_init():
        S.op("dve", lambda e: e.memset(Ov[0:64], 0.0), writes=[bO])
        S.op("dve", lambda e: e.memset(RMASK, 1.0), writes=[bRM])
        S.op("dve", lambda e: e.memset(RMASK.rearrange("p (c n) -> p c n", n=64)[:, :, 0:1], 0.0), writes=[bRM])
    S.op("dve", lambda e: e.tensor_tensor(out=SM[:, o_lb:o_lb + 16], in0=SM[:, o_lb:o_lb + 16], in1=SM[:, o_lb + 16:o_lb + 32],
                                          op=ALU.subtract), reads=[CONST], writes=[CONST])
    S.op("act", lambda e: e.activation(out=SM[:, o_lb:o_lb + 16], in_=SM[:, o_lb:o_lb + 16], func=AF.Sigmoid),
         reads=[CONST], writes=[CONST])
    S.op("dve", lambda e: e.tensor_scalar(out=SM[:, o_lb + 16:o_lb + 32], in0=SM[:, o_lb:o_lb + 16], scalar1=-1.0, scalar2=1.0,
                                          op0=ALU.mult, op1=ALU.add), reads=[CONST], writes=[CONST])

    def c3(ap):
        return ap.rearrange("p (c n) -> p c n", n=64)

    deferred = []

    def run_deferred():
        while deferred:
            deferred.pop(0)()

    def hgrn_head(h, mode):
        HW = 1024
        cq, cv, cg, cf = h * 128, HW + h * 128, 2 * HW + h * 128, [3 * HW + h * 128, 4 * HW + h * 128]
        vw = {}
        vstate = {"next": 0, "pend": None}
        NV = 6

        def v_evac():
            if vstate["pend"] is not None:
                k, p, pb = vstate["pend"]
                if k < 2:
                    S.op("act", lambda e: e.activation(out=VTb[:, k * 512:(k + 1) * 512], in_=p[:], func=AF.Copy), reads=[pb],
                         writes=[bVT[k]])
                else:
                    g = k - 2
                    pT = p[:].bitcast(BF16)
                    S.op("act", lambda e: e.activation(out=Vv[0:64, g * 4:(g + 1) * 4, :],
                                                       in_=pT[0:64, 0:512].rearrange("p (j n) -> p j n", j=4), func=AF.Copy),
                         reads=[pb], writes=[bV])
                vstate["pend"] = None

        def tick():
            if "w" not in vw:
                return
            v_evac()
            k = vstate["next"]
            if k >= NV:
                return
            vstate["next"] = k + 1
            p, pb = L["PS"][4 + k % 4], L["PSb"][4 + k % 4]
            if k < 2:
                w, wb = vw["w"]

                def mm(e):
                    ins = None
                    for kt in range(KT):
                        ins = e.matmul(p[:], lhsT=w[:, kt, :], rhs=XT[:, kt, k * 512:(k + 1) * 512], start=(kt == 0),
                                       stop=(kt == KT - 1))
                    return ins
                S.op("pe", mm, reads=[wb] + XTall[k * 4:(k + 1) * 4], writes=[pb])
            else:
                g = k - 2
                pT = p[:].bitcast(BF16)

                def trn(e):
                    ins = None
                    for j in range(4):
                        c = g * 4 + j
                        ins = e.transpose(out=pT[0:64, j * 128:(j + 1) * 128], in_=VTb[:, c * 64:(c + 1) * 64], identity=ident[:])
                    return ins
                S.op("pe", trn, reads=[bVT[g // 2], CONST], writes=[pb])
            vstate["pend"] = (k, p, pb)
        qg_pending = []
        for di in range(2):
            col = di * 8 + h
            proj_fm(cf[di], lambda half, p, pb: S.op("act", lambda e: e.activation(out=FK[:, half * 512:(half + 1) * 512], in_=p[:],
                                                                                    func=AF.Sigmoid), reads=[pb], writes=[bFK]))
            if di == 0:
                if mode == 2:
                    for (c0, dstT, dstb, bank0) in ((cq, QT, bQT, 4), (cg, GG, bGG, 6)):
                        wq, wqb = wblock(c0)
                        for half in range(2):
                            p, pb = L["PS"][bank0 + half], L["PSb"][bank0 + half]

                            def mm(e):
                                ins = None
                                for kt in range(KT):
                                    ins = e.matmul(p[:], lhsT=wq[:, kt, :], rhs=XT[:, kt, half * 512:(half + 1) * 512],
                                                   start=(kt == 0), stop=(kt == KT - 1))
                                return ins
                            S.op("pe", mm, reads=[wqb] + XTall[half * 4:(half + 1) * 4], writes=[pb])
                            qg_pending.append((p, pb, dstT, dstb, half))
                    run_deferred()
                else:
                    vw["w"] = wblock(cv)
            elif mode == 2:
                vw["w"] = wblock(cv)
            S.op("dve", lambda e: e.tensor_scalar(out=FK, in0=FK, scalar1=SM[:, o_lb + 16 + col:o_lb + 17 + col],
                                                  scalar2=SM[:, o_lb + col:o_lb + col + 1], op0=ALU.mult, op1=ALU.add),
                 reads=[CONST], writes=[bFK])
            S.op("act", lambda e: e.activation(out=LOGF, in_=FK, func=AF.Ln, accum_out=DSEG[:, di:di + 1]),
                 reads=[bFK], writes=[bLOGF, bDS])
            tick()
            S.op("dve", lambda e: e.tensor_scalar(out=FK, in0=FK, scalar1=-1.0, scalar2=1.0, op0=ALU.mult, op1=ALU.add),
                 reads=[bLOGF], writes=[bFK])
            S.op("dve", lambda e: e.tensor_tensor_scan(out=BC, data0=RMASK, data1=LOGF, initial=0.0, op0=ALU.mult, op1=ALU.add),
                 reads=[bRM, bLOGF], writes=[bBC])
            tick()
            S.op("act", lambda e: e.activation(out=EBL[di], in_=c3(BC)[:, :, 63], func=AF.Exp), reads=[bBC], writes=[bEBL[di]])
            tick()
            if di == 1:
                S.op("dve", lambda e: e.tensor_tensor(out=BC, in0=LOGF, in1=BC, op=ALU.subtract), reads=[bLOGF], writes=[bBC])
            ebl_b = EBL[di].unsqueeze(2).to_broadcast([128, 16, 64])
            S.op("act", lambda e: e.activation(out=EE, in_=BC, func=AF.Exp, scale=-1.0), reads=[bBC], writes=[bEE])
            S.op("dve", lambda e: e.tensor_tensor(out=KA[di], in0=FK, in1=EE, op=ALU.mult), reads=[bFK, bEE], writes=[bKA[di]])
            tick()
            if di == 0:
                S.op("dve", lambda e: e.tensor_tensor(out=c3(X3[0]), in0=c3(KA[0]), in1=ebl_b, op=ALU.mult),
                     reads=[bKA[0], bEBL[0]], writes=[bX3[0]])
            if mode == 2:
                S.op("act", lambda e: e.activation(out=LOGF, in_=BC, func=AF.Exp), reads=[bBC], writes=[bLOGF])
                if di == 0:
                    for (p, pb, dstT, dstb, half) in qg_pending:
                        S.op("act", lambda e: e.activation(out=dstT[:, half * 512:(half + 1) * 512], in_=p[:], func=AF.Silu),
                             reads=[pb], writes=[dstb])
                S.op("dve", lambda e: e.scalar_tensor_tensor(out=QA[di], in0=QT, scalar=128.0 ** -0.5, in1=LOGF, op0=ALU.mult,
                                                             op1=ALU.mult), reads=[bQT, bLOGF], writes=[bQA[di]])
                if di == 1:
                    S.op("dve", lambda e: e.tensor_tensor(out=c3(X3[1]), in0=c3(QA[1]), in1=ebl_b, op=ALU.mult),
                         reads=[bQA[1], bEBL[1]], writes=[bX3[1]])
            sk, so = di, h * 129
            Sst = SD[di][:, 0:128]
            if mode == 1:
                S.op("dve", lambda e: e.memset(Sst, 0.0), writes=[bSD[di]])
            else:
                S.dma("sp", GINv, st_dst[sk].rearrange("(r p) c -> p r c", p=128)[:, :, so:so + 129], reads=[STD[sk]], writes=[bGIN],
                      key="GIN")
                combine(GINv, 128, Sst, bSD[di], bGIN, di)
                S.op("act", lambda e: e.activation(out=Sbf[di][0], in_=Sst, func=AF.Copy), reads=[bSD[di]], writes=[bSbf[di][0]])
        KS = [X3[0], KA[1]]
        bKS = [bX3[0], bKA[1]]
        QI = [QA[0], X3[1]]
        bQI = [bQA[0], bX3[1]]
        pkv = [[None] * 16, [None] * 16]

        def transposes(di):
            alias_b = bEE if di == 0 else bQT
            gs = range(4) if di == 0 else range(3, -1, -1)
            for g in gs:
                p, pb = psum()
                pT = p[:].bitcast(BF16)

                def trn(e):
                    ins = None
                    for j in range(4):
                        c = g * 4 + j
                        ins = e.transpose(out=pT[0:64, j * 128:(j + 1) * 128], in_=KS[di][:, c * 64:(c + 1) * 64], identity=ident[:])
                    return ins
                S.op("pe", trn, reads=[bKS[di], CONST], writes=[pb])
                S.op("act", lambda e: e.activation(out=KALL[di][0:64, g * 4:(g + 1) * 4, :],
                                                   in_=pT[0:64, 0:512].rearrange("p (j n) -> p j n", j=4), func=AF.Copy),
                     reads=[pb], writes=[bKALL[di][g], alias_b])

        def stage_a(di, i):
            c = i if di == 0 else 15 - i
            pk = PSKV[di][i % 2][:, 0:128]
            pkb = bPSKV[di][i % 2]
            S.op("pe", lambda e: e.matmul(pk, lhsT=KALL[di][0:64, c, :], rhs=Vv[0:64, c, :], start=True, stop=True),
                 reads=[bKALL[di][c // 4], bV, (bEE if di == 0 else bQT)], writes=[pkb])
            pkv[di][i] = (pk, pkb)

        def attn_all(di):
            for g in range(2):
                p, pb = psum()

                def mm(e):
                    ins = None
                    for j in range(8):
                        c = g * 8 + j
                        cs = slice(c * 64, (c + 1) * 64)
                        ins = e.matmul(p[0:64, j * 64:(j + 1) * 64], lhsT=KA[di][:, cs], rhs=QA[di][:, cs], start=True, stop=True)
                    return ins
                S.op("pe", mm, reads=[bKA[di], bQA[di]], writes=[pb])
                S.op("dve", lambda e: e.tensor_tensor(out=ATTA[di][0:64, g * 8:(g + 1) * 8, :],
                                                      in0=p[0:64, 0:512].rearrange("p (j n) -> p j n", j=8),
                                                      in1=tri[di][0:64, 0:64].unsqueeze(1).to_broadcast([64, 8, 64]), op=ALU.mult),
                     reads=[pb, CONST], writes=[bATTA[di]])

        pend = [None, None]

        def flush_o(di):
            if pend[di] is not None:
                po, pob, c = pend[di]
                S.op("dve", lambda e: e.tensor_tensor(out=Ov[0:64, c, :], in0=po[0:64, 0:128], in1=Ov[0:64, c, :], op=ALU.add),
                     reads=[pob], writes=[bO])
                pend[di] = None

        def stage_b(di, i):
            c = i if di == 0 else 15 - i
            cs = slice(c * 64, (c + 1) * 64)
            Sst = SD[di][:, 0:128]
            sb_, sbb_ = Sbf[di][i % 2], bSbf[di][i % 2]
            pk, pkb = pkv[di][i]
            S.op("dve", lambda e: e.scalar_tensor_tensor(out=Sst, in0=Sst, scalar=EBL[di][:, c:c + 1], in1=pk,
                                                         op0=ALU.mult, op1=ALU.add), reads=[pkb, bEBL[di]], writes=[bSD[di]])
            if mode == 2:
                if i < 15:
                    nb_, nbb_ = Sbf[di][(i + 1) % 2], bSbf[di][(i + 1) % 2]
                    S.op("act", lambda e: e.activation(out=nb_, in_=Sst, func=AF.Copy), reads=[bSD[di]], writes=[nbb_])
                flush_o(di)
                po, pob = psum()

                def mm(e):
                    e.matmul(po[0:64, 0:128], lhsT=ATTA[di][0:64, c, :], rhs=Vv[0:64, c, :], start=True, stop=False)
                    return e.matmul(po[0:64, 0:128], lhsT=QI[di][:, cs], rhs=sb_, start=False, stop=True)
                S.op("pe", mm, reads=[bATTA[di], bV, bQI[di], sbb_], writes=[pob])
                pend[di] = (po, pob, c)

        while vstate["next"] < NV or vstate["pend"] is not None:
            tick()
        for di in range(2):
            transposes(di)
        if mode == 2:
            for di in range(2):
                attn_all(di)
        for di in range(2):
            stage_a(di, 0)
        for i in range(16):
            for di in range(2):
                if i + 1 < 16:
                    stage_a(di, i + 1)
                stage_b(di, i)
        for di in range(2):
            flush_o(di)
        if mode == 1:
            for di in range(2):
                sk, so = di, h * 129
                S.op("act", lambda e: e.activation(out=SD[di][:, 128:129], in_=DSEG[:, di:di + 1], func=AF.Exp), reads=[bDS],
                     writes=[bSD[di]])
                S.dma("sp", st_src[sk][:, so:so + 129], SD[di], reads=[bSD[di]], writes=[STS[sk]], key="STS%d" % sk)
        if mode == 2:
            S.op("dve", lambda e: e.tensor_tensor(out=SQ16[0:64], in0=Ov[0:64], in1=Ov[0:64], op=ALU.mult), reads=[bO, bEE],
                 writes=[bFK, bLOGF])
            S.op("dve", lambda e: e.tensor_reduce(out=RS[0:64, 0:16], in_=SQ16[0:64], axis=AX.X, op=ALU.add), reads=[bFK, bLOGF],
                 writes=[bRS])
            S.op("dve", lambda e: e.tensor_scalar(out=RS[0:64, 16:32], in0=RS[0:64, 0:16], scalar1=1.0 / 128, scalar2=NORM_EPS,
                                                  op0=ALU.mult, op1=ALU.add), reads=[bRS], writes=[bRS])
            S.op("act", lambda e: e.activation(out=RS[0:64, 32:48], in_=RS[0:64, 16:32], func=AF.Sqrt), reads=[bRS], writes=[bRS])
            S.op("dve", lambda e: e.reciprocal(out=RS[0:64, 48:64], in_=RS[0:64, 32:48]), reads=[bRS], writes=[bRS])
            S.op("dve", lambda e: e.tensor_tensor(out=ON16[0:64], in0=Ov[0:64],
                                                  in1=RS[0:64, 48:64].unsqueeze(2).to_broadcast([64, 16, 128]), op=ALU.mult),
                 reads=[bRS, bO], writes=[bBC])
            def epi_pe(h=h):
                p, pb = psum()
                pT = p[:].bitcast(BF16)

                def trn(e):
                    ins = None
                    for c in range(16):
                        ins = e.transpose(out=pT[:, c * 64:(c + 1) * 64], in_=ON16[0:64, c, :], identity=ident[0:64, 0:64])
                    return ins
                S.op("pe", trn, reads=[bBC, CONST], writes=[pb])
                S.op("dve", lambda e: e.scalar_tensor_tensor(out=YT[:, h, :], in0=pT[:, 0:1024], scalar=SM[:, o_hg + h:o_hg + h + 1],
                                                             in1=GG, op0=ALU.mult, op1=ALU.mult), reads=[pb, bGG, CONST], writes=[YTb])
            deferred.append(epi_pe)
            S.op("dve", lambda e: e.memset(Ov[0:64], 0.0), writes=[bO])

    def combine(Gv, n, dst, dstb, gb, di):
        mk = o_mk + (0 if di == 0 else 4)
        idx = [0, 1, 2] if di == 0 else [3, 2, 1]
        first = True
        for i in idx:
            m = SM[:, mk + i:mk + i + 1]
            if first:
                S.op("dve", lambda e: e.tensor_scalar_mul(out=dst, in0=Gv[:, i, 0:n], scalar1=m), reads=[gb, CONST], writes=[dstb])
                first = False
                continue
            om = SM[:, mk + 16 + i:mk + 17 + i]
            S.op("dve", lambda e: e.tensor_scalar(out=ST8[:, 41:42], in0=Gv[:, i, n:n + 1], scalar1=m, scalar2=om, op0=ALU.mult,
                                                  op1=ALU.add), reads=[gb, CONST], writes=[ST8b_])
            S.op("dve", lambda e: e.tensor_scalar_mul(out=dst, in0=dst, scalar1=ST8[:, 41:42]), reads=[ST8b_], writes=[dstb])
            S.op("dve", lambda e: e.scalar_tensor_tensor(out=dst, in0=Gv[:, i, 0:n], scalar=m, in1=dst, op0=ALU.mult, op1=ALU.add),
                 reads=[gb, CONST], writes=[dstb])

    ST8b_ = L["ST8b"]
    STS = [S.buf("STS%d" % k) for k in range(6)]
    STD = [S.buf("STD%d" % k) for k in range(6)]
    HLS = S.buf("HLS")
    HLD = S.buf("HLD")
    hg_end = off[0]

    off[0] = mark
    MW = 1024
    c_mq, c_mk, c_mv, c_mo, c_g = 5 * 1024, 5 * 1024 + MW, 5 * 1024 + 2 * MW, 5 * 1024 + 3 * MW, 5 * 1024 + 4 * MW
    WR.append(b16(KT * 128).rearrange("p (k n) -> p k n", k=KT))
    WRb.append(S.buf("W3"))
    mQT = b16(2 * 1024).rearrange("p (d n) -> p d n", d=2)
    mKTs = [YT[:, 2 * hh:2 * hh + 2, :] for hh in range(4)]
    bmKTs = [S.buf("mKT%d" % hh) for hh in range(4)]
    ZC = f32(1028); ACC = f32(1024)
    VE = b16(8 * 258).rearrange("p (t n) -> p t n", t=8)
    OT = b16(2 * 1024).rearrange("p (d n) -> p d n", d=2)
    NUM = [f32(8 * 257).rearrange("p (t n) -> p t n", t=8), f32(8 * 257).rearrange("p (t n) -> p t n", t=8)]
    HNb = NUM[1].rearrange("p t n -> p (t n)").bitcast(BF16)[:, 0:2048].rearrange("p (t n) -> p t n", t=8)
    RD = f32(64)
    GZ = f32(128).rearrange("p (t n) -> p t n", t=8)
    GB_ = f32(128).rearrange("p (t n) -> p t n", t=8)
    GD = f32(4 * 64).rearrange("p (k t n) -> p k t n", k=4, t=8)
    SCb = [b16(128), b16(128)]; KWb = [b16(256), b16(256)]
    Cst = f32(2 * 258).rearrange("p (d n) -> p d n", d=2)
    Cbf = [b16(2 * 258).rearrange("p (d n) -> p d n", d=2), b16(2 * 258).rearrange("p (d n) -> p d n", d=2)]
    MG = f32(4 * 258).rearrange("p (r n) -> p r n", r=4)
    HALO = f32(4 * 64).rearrange("p (r j b) -> p r j b", r=4, j=16)
    HLR = f32(64).rearrange("p (j b) -> p j b", j=16)
    (bmQT, bZC, bACC, bVE, bOT, bGZ, bGB, bGD, bC, bMG, bHALO, bHLR, bBND, bXTB, bRD) = (S.buf() for _ in range(15))
    bNUM = [S.buf(), S.buf()]; bSCb = [S.buf(), S.buf()]; bKWb = [S.buf(), S.buf()]; bCbf = [S.buf(), S.buf()]
    PCB = [[L["PS"][4], L["PS"][5]], [L["PS"][6], L["PS"][7]]]
    bPCB = [[L["PSb"][4], L["PSb"][5]], [L["PSb"][6], L["PSb"][7]]]
    assert off[0] <= 18432, off[0]

    def halo_exchange():
        S.op("dve", lambda e: e.tensor_copy(out=XTB[:, :, 0:2], in_=XT[:, :, 0:2]), reads=[XTall[0]], writes=[bXTB])
        S.op("dve", lambda e: e.tensor_copy(out=XTB[:, :, 2:4], in_=XT[:, :, 1022:1024]), reads=[XTall[7]], writes=[bXTB])
        for j in range(16):
            w, wb = wblock(c_mq + j * 128)
            p, pb = psum()

            def mm(e):
                ins = None
                for kt in range(KT):
                    ins = e.matmul(p[:, 0:4], lhsT=w[:, kt, :], rhs=XTB[:, kt, :], start=(kt == 0), stop=(kt == KT - 1))
                return ins
            S.op("pe", mm, reads=[wb, bXTB], writes=[pb])
            S.op("act", lambda e: e.activation(out=BND[:, j, :], in_=p[:, 0:4], func=AF.Copy), reads=[pb], writes=[bBND])
        S.dma("sp", halo_src.rearrange("p (j b) -> p j b", j=16), BND, reads=[bBND], writes=[HLS], key="HLS")
        S.cc(halo_src, halo_dst, groups, reads=[HLS], writes=[HLD], key="halo")

    def halo_select():
        S.dma("sp", HALO, halo_dst.rearrange("(r p) (j b) -> p r j b", p=128, j=16), reads=[HLD], writes=[bHALO], key="HALO")
        for side in range(2):
            mk = o_mk + 8 + side * 4
            src_lo = 2 if side == 0 else 0
            dst = HLR[:, :, side * 2:side * 2 + 2]
            for i in range(4):
                m = SM[:, mk + i:mk + i + 1]
                src = HALO[:, i, :, src_lo:src_lo + 2]
                if i == 0:
                    S.op("dve", lambda e: e.tensor_scalar_mul(out=dst, in0=src, scalar1=m), reads=[bHALO, CONST], writes=[bHLR])
                else:
                    S.op("dve", lambda e: e.scalar_tensor_tensor(out=dst, in0=src, scalar=m, in1=dst, op0=ALU.mult, op1=ALU.add),
                         reads=[bHALO, CONST], writes=[bHLR])

    def mlstm_gates():
        w, wb = wblock(c_g, 16)
        for t in range(NTT):
            p, pb = psum()

            def mm(e):
                ins = None
                for kt in range(KT):
                    ins = e.matmul(p[:, 0:16], lhsT=XT[:, kt, t * 128:(t + 1) * 128], rhs=w[:, kt, 0:16], start=(kt == 0),
                                   stop=(kt == KT - 1))
                return ins
            S.op("pe", mm, reads=[wb, XTall[t]], writes=[pb])
            S.op("dve", lambda e: e.tensor_tensor(out=GZ[:, t, :], in0=p[:, 0:16], in1=SM[:, o_gb:o_gb + 16], op=ALU.add),
                 reads=[pb, CONST], writes=[bGZ])
        S.op("act", lambda e: e.activation(out=GZ[:, :, 8:16], in_=GZ[:, :, 8:16], func=AF.Exp, scale=-1.0), reads=[bGZ], writes=[bGZ])
        S.op("act", lambda e: e.activation(out=GZ[:, :, 8:16], in_=GZ[:, :, 8:16], func=AF.Ln, bias=1.0), reads=[bGZ], writes=[bGZ])
        S.op("dve", lambda e: e.tensor_scalar_mul(out=GZ[:, :, 8:16], in0=GZ[:, :, 8:16], scalar1=-1.0), reads=[bGZ], writes=[bGZ])
        for t in range(NTT):
            p, pb = psum()

            def mm(e):
                e.matmul(p[:, 0:4], lhsT=tri[0][:], rhs=GZ[:, t, 8:12], start=True, stop=True)
                e.matmul(p[:, 4:8], lhsT=tri[1][:], rhs=GZ[:, t, 12:16], start=True, stop=True)
                return e.matmul(p[:, 8:16], lhsT=ones[:], rhs=GZ[:, t, 8:16], start=True, stop=True)
            S.op("pe", mm, reads=[bGZ, CONST], writes=[pb])
            S.op("act", lambda e: e.activation(out=GB_[:, t, :], in_=p[:, 0:16], func=AF.Copy), reads=[pb], writes=[bGB])
        S.op("dve", lambda e: e.tensor_tensor(out=GD[:, 0], in0=GZ[:, :, 0:8], in1=GB_[:, :, 0:8], op=ALU.subtract), reads=[bGZ, bGB],
             writes=[bGD])
        S.op("act", lambda e: e.activation(out=GD[:, 1], in_=GB_[:, :, 0:8], func=AF.Exp), reads=[bGB], writes=[bGD])
        S.op("dve", lambda e: e.tensor_tensor(out=GD[:, 2], in0=GD[:, 0], in1=GB_[:, :, 8:16], op=ALU.add), reads=[bGB], writes=[bGD])
        S.op("act", lambda e: e.activation(out=GD[:, 2], in_=GD[:, 2], func=AF.Exp), reads=[bGD], writes=[bGD])
        S.op("act", lambda e: e.activation(out=GD[:, 0], in_=GD[:, 0], func=AF.Exp), reads=[bGD], writes=[bGD])
        S.op("act", lambda e: e.activation(out=GD[:, 3], in_=GB_[:, :, 8:16], func=AF.Exp), reads=[bGB], writes=[bGD])

    conv_tick = [lambda: None]

    def conv_tile(col0, j, dstT, dstb, scale):
        proj_fm(col0, lambda half, p, pb: S.op("act", lambda e: e.activation(out=ZC[:, 2 + half * 512:2 + (half + 1) * 512], in_=p[:],
                                                                              func=AF.Copy), reads=[pb], writes=[bZC]))
        S.op("dve", lambda e: e.tensor_copy(out=ZC[:, 0:2], in_=HLR[:, j, 0:2]), reads=[bHLR], writes=[bZC])
        S.op("dve", lambda e: e.tensor_copy(out=ZC[:, 1026:1028], in_=HLR[:, j, 2:4]), reads=[bHLR], writes=[bZC])
        cw = o_cw + j * 5
        S.op("dve", lambda e: e.tensor_scalar(out=ACC, in0=ZC[:, 0:1024], scalar1=SM[:, cw:cw + 1], scalar2=SM[:, o_cb + j:o_cb + j + 1],
                                              op0=ALU.mult, op1=ALU.add), reads=[bZC, CONST], writes=[bACC])
        conv_tick[0]()
        for k in range(1, 5):
            S.op("dve", lambda e, k=k: e.scalar_tensor_tensor(out=ACC, in0=ZC[:, k:k + 1024], scalar=SM[:, cw + k:cw + k + 1], in1=ACC,
                                                              op0=ALU.mult, op1=ALU.add), reads=[bZC, CONST], writes=[bACC])
            conv_tick[0]()
        if scale == 1.0:
            S.op("act", lambda e: e.activation(out=dstT, in_=ACC, func=AF.Silu), reads=[bACC], writes=[dstb])
        else:
            S.op("act", lambda e: e.activation(out=ACC, in_=ACC, func=AF.Silu), reads=[bACC], writes=[bACC])
            S.op("dve", lambda e: e.tensor_scalar_mul(out=dstT, in0=ACC, scalar1=scale), reads=[bACC], writes=[dstb])

    def mlstm_head(h, mode):
        wa, wab = wblock(c_mv + h * 256)
        wb_, wbb = wblock(c_mv + h * 256 + 128)
        S.op("dve", lambda e: e.memset(VE[:, :, 256:257], 1.0), writes=[bVE])
        vstate = {"next": 0, "pend": None}

        def v_evac():
            if vstate["pend"] is not None:
                t, p, pb = vstate["pend"]
                S.op("act", lambda e: e.activation(out=VE[:, t, 0:256], in_=p[:, 0:256], func=AF.Copy), reads=[pb], writes=[bVE])
                vstate["pend"] = None

        def tick():
            v_evac()
            t = vstate["next"]
            if t >= NTT:
                return
            vstate["next"] = t + 1
            p, pb = L["PS"][4 + t % 4], L["PSb"][4 + t % 4]

            def mm(e):
                ins = None
                for (w, c0) in ((wa, 0), (wb_, 128)):
                    for kt in range(KT):
                        ins = e.matmul(p[:, c0:c0 + 128], lhsT=XT[:, kt, t * 128:(t + 1) * 128], rhs=w[:, kt, :], start=(kt == 0),
                                       stop=(kt == KT - 1))
                return ins
            S.op("pe", mm, reads=[wab, wbb, XTall[t]], writes=[pb])
            vstate["pend"] = (t, p, pb)
        conv_tick[0] = tick
        mKT, bmKT = mKTs[h], bmKTs[h]
        if mode == 1:
            for dt in range(2):
                conv_tile(c_mk + h * 256 + dt * 128, 8 + h * 2 + dt, mKT[:, dt, :], bmKT, 1.0)
        if mode == 2:
            for dt in range(2):
                conv_tile(c_mq + h * 256 + dt * 128, h * 2 + dt, mQT[:, dt, :], bmQT, 1.0 / 16)
            while vstate["next"] < NTT or vstate["pend"] is not None:
                tick()
            run_deferred()
            for dt in range(2):
                proj_fm(c_mo + h * 256 + dt * 128,
                        lambda half, p, pb: S.op("act", lambda e: e.activation(out=OT[:, dt, half * 512:(half + 1) * 512], in_=p[:],
                                                                                func=AF.Sigmoid), reads=[pb], writes=[bOT]))
                S.op("dve", lambda e: e.tensor_scalar_mul(out=OT[:, dt, :], in0=OT[:, dt, :],
                                                          scalar1=SM[:, o_mg + h * 2 + dt:o_mg + h * 2 + dt + 1]),
                     reads=[CONST], writes=[bOT])
        while vstate["next"] < NTT or vstate["pend"] is not None:
            tick()
        conv_tick[0] = lambda: None
        for di in range(2):
            g = di * 4 + h
            sk, so = 2 + di * 2 + h // 2, (h % 2) * 516
            if mode == 1:
                S.op("dve", lambda e: e.memset(Cst, 0.0), writes=[bC])
            else:
                for dt in range(2):
                    S.dma("sp", MG, st_dst[sk].rearrange("(r p) c -> p r c", p=128)[:, :, so + dt * 258:so + dt * 258 + 258],
                          reads=[STD[sk]], writes=[bMG], key="MG")
                    combine(MG, 257, Cst[:, dt, 0:257], bC, bMG, di)
                S.op("act", lambda e: e.activation(out=Cbf[0][:, :, 0:257], in_=Cst[:, :, 0:257], func=AF.Copy), reads=[bC],
                     writes=[bCbf[0]])
            order = list(range(NTT)) if di == 0 else list(range(NTT - 1, -1, -1))

            def stA(i):
                t = order[i]
                ts = slice(t * 128, (t + 1) * 128)
                if mode == 2:
                    ps_, psb = psum()

                    def mm(e):
                        e.matmul(ps_[:, 0:128], lhsT=mKT[:, 0, ts], rhs=mQT[:, 0, ts], start=True, stop=False)
                        return e.matmul(ps_[:, 0:128], lhsT=mKT[:, 1, ts], rhs=mQT[:, 1, ts], start=False, stop=True)
                    S.op("pe", mm, reads=[bmKT, bmQT], writes=[psb])
                    S.op("dve", lambda e: e.scalar_tensor_tensor(out=SCb[i % 2], in0=ps_[:, 0:128], scalar=GD[:, 0, t, g:g + 1],
                                                                 in1=tri[di][:], op0=ALU.mult, op1=ALU.mult),
                         reads=[psb, bGD, CONST], writes=[bSCb[i % 2]])
                pt, ptb = psum()
                pT = pt[:].bitcast(BF16)

                def trn(e):
                    e.transpose(out=pT[:, 0:128], in_=mKT[:, 0, ts], identity=ident[:])
                    return e.transpose(out=pT[:, 128:256], in_=mKT[:, 1, ts], identity=ident[:])
                S.op("pe", trn, reads=[bmKT, CONST], writes=[ptb])
                S.op("act", lambda e: e.activation(out=KWb[i % 2], in_=pT[:, 0:256], func=AF.Copy, scale=GD[:, 2, t, g:g + 1]),
                     reads=[ptb, bGD], writes=[bKWb[i % 2]])
                for dt in range(2):
                    S.op("pe", lambda e: e.matmul(PCB[i % 2][dt][:, 0:257], lhsT=KWb[i % 2][:, dt * 128:(dt + 1) * 128],
                                                  rhs=VE[:, t, 0:257], start=True, stop=True), reads=[bKWb[i % 2], bVE],
                         writes=[bPCB[i % 2][dt]])

            def stB(i):
                t = order[i]
                ts = slice(t * 128, (t + 1) * 128)
                for dt in range(2):
                    S.op("dve", lambda e: e.scalar_tensor_tensor(out=Cst[:, dt, 0:257], in0=Cst[:, dt, 0:257],
                                                                 scalar=GD[:, 3, t, g:g + 1], in1=PCB[i % 2][dt][:, 0:257],
                                                                 op0=ALU.mult, op1=ALU.add), reads=[bPCB[i % 2][dt], bGD],
                         writes=[bC])
                if mode == 2:
                    if i < NTT - 1:
                        S.op("act", lambda e: e.activation(out=Cbf[(i + 1) % 2][:, :, 0:257], in_=Cst[:, :, 0:257], func=AF.Copy),
                             reads=[bC], writes=[bCbf[(i + 1) % 2]])
                    po, pob = psum()
                    cb = Cbf[i % 2]

                    def mm2(e):
                        e.matmul(po[:, 0:257], lhsT=SCb[i % 2], rhs=VE[:, t, 0:257], start=True, stop=False)
                        e.matmul(po[:, 0:257], lhsT=mQT[:, 0, ts], rhs=cb[:, 0, 0:257], start=False, stop=False)
                        return e.matmul(po[:, 0:257], lhsT=mQT[:, 1, ts], rhs=cb[:, 1, 0:257], start=False, stop=True)
                    S.op("pe", mm2, reads=[bSCb[i % 2], bVE, bmQT, bCbf[i % 2]], writes=[pob])
                    S.op("act", lambda e: e.activation(out=NUM[di][:, t, :], in_=po[:, 0:257], func=AF.Copy,
                                                       scale=GD[:, 1, t, g:g + 1]), reads=[pob, bGD], writes=[bNUM[di]])

            stA(0)
            for i in range(NTT):
                if i + 1 < NTT:
                    stA(i + 1)
                stB(i)
            if mode == 1:
                S.op("dve", lambda e: e.tensor_reduce(out=ST8[:, 48:49], in_=GB_[:, :, 8 + g], axis=AX.X, op=ALU.add), reads=[bGB],
                     writes=[ST8b_])
                for dt in range(2):
                    S.op("act", lambda e: e.activation(out=Cst[:, dt, 257:258], in_=ST8[:, 48:49], func=AF.Exp), reads=[ST8b_],
                         writes=[bC])
                S.dma("sp", st_src[sk][:, so:so + 516].rearrange("p (d n) -> p d n", d=2), Cst, reads=[bC], writes=[STS[sk]],
                      key="STS%d" % sk)
        if mode == 2:
            for di in range(2):
                c0 = di * 8
                S.op("act", lambda e: e.activation(out=RD[:, c0:c0 + 8], in_=NUM[di][:, :, 256], func=AF.Abs), reads=[bNUM[di]],
                     writes=[bRD])
                S.op("dve", lambda e: e.tensor_scalar_max(out=RD[:, c0:c0 + 8], in0=RD[:, c0:c0 + 8], scalar1=1.0), reads=[bRD],
                     writes=[bRD])
                S.op("dve", lambda e: e.reciprocal(out=RD[:, c0:c0 + 8], in_=RD[:, c0:c0 + 8]), reads=[bRD], writes=[bRD])
                S.op("dve", lambda e: e.tensor_tensor(out=NUM[di][:, :, 0:256], in0=NUM[di][:, :, 0:256],
                                                      in1=RD[:, c0:c0 + 8].unsqueeze(2).to_broadcast([128, 8, 256]), op=ALU.mult),
                     reads=[bRD], writes=[bNUM[di]])
            Hv = NUM[0][:, :, 0:256]
            S.op("dve", lambda e: e.tensor_tensor(out=Hv, in0=Hv, in1=NUM[1][:, :, 0:256], op=ALU.add), reads=[bNUM[1]],
                 writes=[bNUM[0]])
            S.op("dve", lambda e: e.tensor_reduce(out=RD[:, 16:24], in_=Hv, axis=AX.X, op=ALU.add), reads=[bNUM[0]], writes=[bRD])
            S.op("dve", lambda e: e.tensor_scalar_mul(out=RD[:, 16:24], in0=RD[:, 16:24], scalar1=-1.0 / 256), reads=[bRD],
                 writes=[bRD])
            S.op("dve", lambda e: e.tensor_tensor(out=Hv, in0=Hv, in1=RD[:, 16:24].unsqueeze(2).to_broadcast([128, 8, 256]),
                                                  op=ALU.add), reads=[bRD], writes=[bNUM[0]])
            S.op("dve", lambda e: e.tensor_tensor(out=NUM[1][:, :, 0:256], in0=Hv, in1=Hv, op=ALU.mult), reads=[bNUM[0]],
                 writes=[bNUM[1]])
            S.op("dve", lambda e: e.tensor_reduce(out=RD[:, 24:32], in_=NUM[1][:, :, 0:256], axis=AX.X, op=ALU.add), reads=[bNUM[1]],
                 writes=[bRD])
            S.op("dve", lambda e: e.tensor_scalar(out=RD[:, 24:32], in0=RD[:, 24:32], scalar1=1.0 / 256, scalar2=NORM_EPS,
                                                  op0=ALU.mult, op1=ALU.add), reads=[bRD], writes=[bRD])
            S.op("act", lambda e: e.activation(out=RD[:, 32:40], in_=RD[:, 24:32], func=AF.Sqrt), reads=[bRD], writes=[bRD])
            S.op("dve", lambda e: e.reciprocal(out=RD[:, 40:48], in_=RD[:, 32:40]), reads=[bRD], writes=[bRD])
            S.op("dve", lambda e: e.tensor_tensor(out=HNb, in0=Hv, in1=RD[:, 40:48].unsqueeze(2).to_broadcast([128, 8, 256]),
                                                  op=ALU.mult), reads=[bRD, bNUM[0]], writes=[bNUM[1]])
            def epi_pe(h=h):
                for dt in range(2):
                    p, pb = psum()
                    pT = p[:].bitcast(BF16)

                    def trn(e):
                        ins = None
                        for t in range(NTT):
                            ins = e.transpose(out=pT[:, t * 128:(t + 1) * 128], in_=HNb[:, t, dt * 128:(dt + 1) * 128],
                                              identity=ident[:])
                        return ins
                    S.op("pe", trn, reads=[bNUM[1], CONST], writes=[pb])
                    S.op("dve", lambda e: e.tensor_tensor(out=YT[:, 8 + h * 2 + dt, :], in0=pT[:, 0:1024], in1=OT[:, dt, :],
                                                          op=ALU.mult), reads=[pb, bOT], writes=[YTb])
            deferred.append(epi_pe)

    stop = L["stop"]
    if stop != "B0":
        halo_exchange()
    if stop == "B1":
        return
    hgrn_init()
    ps_pool[0] = 4
    for h in range(8):
        hgrn_head(h, 1)
    ps_pool[0] = 8
    for k in range(2):
        S.cc(st_src[k], st_dst[k], groups, reads=[STS[k]], writes=[STD[k]], key="st%d" % k)
    S.barrier()
    if stop in ("B2", "B0"):
        return
    halo_select()
    mlstm_gates()
    if stop == "B3":
        return
    ps_pool[0] = 4
    wr_n[0] = 4
    for h in range(4):
        mlstm_head(h, 1)
        if h % 2 == 1:
            for di in range(2):
                k = 2 + di * 2 + h // 2
                S.cc(st_src[k], st_dst[k], groups, reads=[STS[k]], writes=[STD[k]], key="st%d" % k)
    if stop == "B5":
        return
    for h in range(4):
        mlstm_head(h, 2)
    run_deferred()
    wr_n[0] = 3
    S.barrier()
    hgrn_init()
    ps_pool[0] = 4
    for h in range(8):
        hgrn_head(h, 2)
    run_deferred()
    ps_pool[0] = 8
    if debug:
        S.barrier()
        DBG = B[:, 0:16384]
        for half in range(2):
            S.op("dve", lambda e: e.tensor_copy(out=DBG[:, 0:8192], in_=Ab[:, half * 8192:(half + 1) * 8192]), writes=[bFK])
            S.dma("sp", dbg["yT"][:, half * 8192:(half + 1) * 8192], DBG[:, 0:8192], reads=[bFK], writes=[bFK], key="dbgy")


def _small(inputs, core):
    r = core % 4
    sm = np.zeros((128, 192), np.float32)
    lb = np.asarray(inputs["hgrn_lb"], np.float32)
    for di in range(2):
        for h in range(8):
            sm[:, 0 + di * 8 + h] = lb[di, 0, h * 128:(h + 1) * 128]
            sm[:, 16 + di * 8 + h] = lb[di, 1, h * 128:(h + 1) * 128]
    sm[:, 32:40] = np.asarray(inputs["hgrn_norm_g"], np.float32).reshape(8, 128).T
    sm[:, 40:48] = np.asarray(inputs["mlstm_norm_g"], np.float32).reshape(8, 128).T
    sm[:, 48:64] = np.asarray(inputs["mlstm_conv_b"], np.float32).reshape(16, 128).T
    cw = np.asarray(inputs["mlstm_conv_w"], np.float32).reshape(5, 16, 128)
    sm[:, 64:144] = cw.transpose(2, 1, 0).reshape(128, 80)
    ig = np.asarray(inputs["mlstm_ig_b"], np.float32).reshape(2, 4)
    fg = np.asarray(inputs["mlstm_fg_b"], np.float32).reshape(2, 4)
    sm[:, 144:160] = np.concatenate([ig[0], ig[1], fg[0], fg[1]])[None, :]
    for i in range(4):
        sm[:, 160 + i] = 1.0 if i < r else 0.0
        sm[:, 164 + i] = 1.0 if i > r else 0.0
        sm[:, 168 + i] = 1.0 if i == r - 1 else 0.0
        sm[:, 172 + i] = 1.0 if i == r + 1 else 0.0
        sm[:, 176 + i] = 0.0 if i < r else 1.0
        sm[:, 180 + i] = 0.0 if i > r else 1.0
    return sm


def _in_maps(inputs):
    f = lambda a: np.ascontiguousarray(np.asarray(a, np.float32))
    x = f(inputs["x"]).reshape(8, NT, D)
    shared = {
        "ffn1_w1": f(inputs["ffn1_w1"]).reshape(D, DFF), "ffn1_w3": f(inputs["ffn1_w3"]).reshape(D, DFF),
        "ffn1_w2": f(inputs["ffn1_w2"]).reshape(DFF, D), "ffn2_w1": f(inputs["ffn2_w1"]).reshape(D, DFF),
        "ffn2_w3": f(inputs["ffn2_w3"]).reshape(D, DFF), "ffn2_w2": f(inputs["ffn2_w2"]).reshape(DFF, D),
        "w_in": f(inputs["w_in"]).reshape(D, INC), "w_out": f(inputs["w_out"]).reshape(D, D),
    }
    lnp = {"ln1_g": inputs["ln1_g"], "ln1_b": inputs["ln1_b"], "ln2_g": inputs["ln2_g"], "ln2_b": inputs["ln2_b"],
           "ln3_g": inputs["ln3_g"], "ln3_b": inputs["ln3_b"]}
    for k, v in lnp.items():
        shared[k] = np.ascontiguousarray(np.broadcast_to(f(v).reshape(1, D), (128, D)))
    maps = []
    for c in range(8):
        m = dict(shared)
        m["x"] = x[c]
        m["small"] = _small(inputs, c)
        maps.append(m)
    return maps


_NC_CACHE = {}


def kernel(**inputs):
    if "nc" not in _NC_CACHE:
        _NC_CACHE["nc"] = build(False)
    nc = _NC_CACHE["nc"]
    res = run_bass_kernel_spmd(nc, _in_maps(inputs), core_ids=list(range(8)))
    out = np.stack([np.asarray(r["out"], np.float32) for r in res.results], axis=0)
    return out.reshape(2, 4096, D)
```

```python
import numpy as np
import concourse.bass as bass
import concourse.mybir as mybir
from concourse.bass_utils import run_bass_kernel_spmd

F32 = mybir.dt.float32
BF16 = mybir.dt.bfloat16
AF = mybir.ActivationFunctionType
ALU = mybir.AluOpType
AX = mybir.AxisListType

D = 2048
DFF = 5632
NT = 1024
NTT = 8
KT = 16
INC = 9232
ALPHA = 2.0 ** 0.25
LN_EPS = 1e-5
NORM_EPS = 1e-6
NCH = DFF // 512
HG_OFF = 0
ML_OFF = 16 * 129
ST_COLS = 16 * 129 + 8 * 516


class Buf:
    __slots__ = ("name", "w", "r")

    def __init__(self, name):
        self.name = name
        self.w = None
        self.r = {}


class Sched:
    def __init__(self, nc):
        self.nc = nc
        self.E = {"pe": nc.tensor, "act": nc.scalar, "dve": nc.vector, "pool": nc.gpsimd, "sp": nc.sync}
        self.sem = {e: nc.alloc_semaphore("e_" + e) for e in self.E}
        self.cnt = {e: 0 for e in self.E}
        self.seen = {e: {} for e in self.E}
        self.dsem = {}
        self.nbuf = 0

    def buf(self, name=None):
        self.nbuf += 1
        return Buf(name or ("b%d" % self.nbuf))

    def _deps(self, reads, writes):
        d = []
        for b in reads:
            if b.w is not None:
                d.append(b.w)
        for b in writes:
            if b.w is not None:
                d.append(b.w)
            d.extend(b.r.values())
        return d

    def _wait(self, e, toks):
        best = {}
        for t in toks:
            k = t[2]
            if self.seen[e].get(k, 0) >= t[1]:
                continue
            if k not in best or best[k][1] < t[1]:
                best[k] = t
        for k, t in best.items():
            self.E[e].wait_ge(t[0], t[1])
            self.seen[e][k] = t[1]

    def _commit(self, tok, reads, writes):
        for b in reads:
            b.r[tok[2]] = tok
        for b in writes:
            b.w = tok
            b.r = {}

    def op(self, e, fn, reads=(), writes=()):
        self._wait(e, self._deps(reads, writes))
        ins = fn(self.E[e])
        self.cnt[e] += 1
        ins.then_inc(self.sem[e], 1)
        tok = (self.sem[e], self.cnt[e], e)
        self._commit(tok, reads, writes)
        return tok

    def dma(self, q, out, in_, reads=(), writes=(), key=None, slow=False):
        self._wait(q, self._deps(reads, writes))
        if key not in self.dsem:
            self.dsem[key] = [self.nc.alloc_semaphore("d_" + key), 0]
        s = self.dsem[key]
        s[1] += 16
        if slow:
            self.E[q].dma_start(out=out, in_=in_, allow_slow_non_contiguous=True).then_inc(s[0], 16)
        else:
            self.E[q].dma_start(out=out, in_=in_).then_inc(s[0], 16)
        tok = (s[0], s[1], "d_" + key)
        self._commit(tok, reads, writes)
        return tok

    def cc(self, src, dst, groups, reads=(), writes=(), key="cc"):
        q = "pool"
        self._wait(q, self._deps(reads, writes))
        sem = self.nc.alloc_semaphore("c_" + key)
        self.E[q].collective_compute("AllGather", ALU.bypass, replica_groups=groups,
                                     ins=[src], outs=[dst]).then_inc(sem, 1)
        tok = (sem, 1, "c_" + key)
        self._commit(tok, reads, writes)
        return tok

    def all_tokens(self):
        toks = [(self.sem[e], self.cnt[e], e) for e in self.E if self.cnt[e] > 0]
        toks += [(s[0], s[1], "d_" + k) for k, s in self.dsem.items() if s[1] > 0]
        return toks

    def barrier(self):
        toks = self.all_tokens()
        for e in self.E:
            self._wait(e, toks)


def build(debug=False, stop=None, skip_ffn=False):
    nc = bass.Bass("TRN2", target_bir_lowering=False)
    S = Sched(nc)

    def din(name, shape):
        return nc.dram_tensor(name, shape, F32, kind="ExternalInput").ap()

    x_d = din("x", [NT, D])
    if not skip_ffn:
        w1 = [din("ffn1_w1", [D, DFF]), din("ffn2_w1", [D, DFF])]
        w3 = [din("ffn1_w3", [D, DFF]), din("ffn2_w3", [D, DFF])]
        w2 = [din("ffn1_w2", [DFF, D]), din("ffn2_w2", [DFF, D])]
    win_d = din("w_in", [D, INC])
    wout_d = din("w_out", [D, D])
    lng_d = [din("ln%d_g" % i, [128, D]) for i in (1, 2, 3)]
    lnb_d = [din("ln%d_b" % i, [128, D]) for i in (1, 2, 3)]
    NSM = 192
    small_d = din("small", [128, NSM])
    out_d = nc.dram_tensor("out", [NT, D], F32, kind="ExternalOutput").ap()
    halo_src = nc.dram_tensor("halo_src", [128, 64], F32, kind="Internal", addr_space="Local").ap()
    halo_dst = nc.dram_tensor("halo_dst", [4 * 128, 64], F32, kind="Internal", addr_space="Local").ap()
    st_src = [nc.dram_tensor("st_src%d" % k, [128, 1032], F32, kind="Internal", addr_space="Local").ap() for k in range(6)]
    st_dst = [nc.dram_tensor("st_dst%d" % k, [4 * 128, 1032], F32, kind="Internal", addr_space="Local").ap() for k in range(6)]
    dbg = {}
    if debug:
        dbg["x1"] = nc.dram_tensor("dbg_x1", [NT, D], F32, kind="ExternalOutput").ap()
        dbg["yT"] = nc.dram_tensor("dbg_yT", [128, KT * NT], F32, kind="ExternalOutput").ap()
        dbg["x2"] = nc.dram_tensor("dbg_x2", [NT, D], F32, kind="ExternalOutput").ap()
    groups = [[0, 1, 2, 3], [4, 5, 6, 7]]

    X = nc.alloc_sbuf_tensor("X", [128, NTT, D], F32)
    XT = nc.alloc_sbuf_tensor("XT", [128, KT, NT], BF16)
    A = nc.alloc_sbuf_tensor("A", [128, 8192], F32)
    B = nc.alloc_sbuf_tensor("B", [128, 18432], F32)
    ident = nc.alloc_sbuf_tensor("ident", [128, 128], BF16)
    tri = [nc.alloc_sbuf_tensor("tri_fw", [128, 128], F32), nc.alloc_sbuf_tensor("tri_bw", [128, 128], F32)]
    ones = nc.alloc_sbuf_tensor("ones", [128, 128], F32)
    SM = nc.alloc_sbuf_tensor("SM", [128, NSM], F32)
    ST8 = nc.alloc_sbuf_tensor("ST8", [128, 64], F32)
    Xb = [S.buf("X%d" % t) for t in range(NTT)]
    XTb = [S.buf("XT%d" % t) for t in range(NTT)]
    CONST = S.buf("CONST")
    ST8b = S.buf("ST8")

    PS = [nc.alloc_psum_tensor("ps%d" % i, [128, 512], F32) for i in range(8)]
    PSb = [S.buf("ps%d" % i) for i in range(8)]
    ps_next = [0]
    ps_pool = [8]

    def psum():
        i = ps_next[0] % ps_pool[0]
        ps_next[0] += 1
        return PS[i], PSb[i]

    o_lb = 0
    o_hg = 32
    o_mg = 40
    o_cb = 48
    o_cw = 64
    o_gb = 144
    o_mk = 160

    S.dma("sp", SM[:], small_d, writes=[CONST], key="CONST")
    S.op("dve", lambda e: e.memset(ident[:], 1.0), writes=[CONST])
    S.op("dve", lambda e: e.memset(tri[0][:], 1.0), writes=[CONST])
    S.op("dve", lambda e: e.memset(tri[1][:], 1.0), writes=[CONST])
    S.op("dve", lambda e: e.memset(ones[:], 1.0), writes=[CONST])
    S.op("pool", lambda e: e.affine_select(out=ident[:], in_=ident[:], pattern=[[-1, 128]], compare_op=ALU.is_equal,
                                           fill=0.0, base=0, channel_multiplier=1), writes=[CONST])
    S.op("pool", lambda e: e.affine_select(out=tri[0][:], in_=tri[0][:], pattern=[[1, 128]], compare_op=ALU.is_ge,
                                           fill=0.0, base=0, channel_multiplier=-1), writes=[CONST])
    S.op("pool", lambda e: e.affine_select(out=tri[1][:], in_=tri[1][:], pattern=[[-1, 128]], compare_op=ALU.is_ge,
                                           fill=0.0, base=0, channel_multiplier=1), writes=[CONST])

    xv = x_d.rearrange("(t p) d -> p t d", p=128)
    for t in range(NTT):
        S.dma("sp", X[:, t, :], xv[:, t, :], writes=[Xb[t]], key="X%d" % t)

    Ab = A[:].bitcast(BF16)
    Bb = B[:].bitcast(BF16)
    HT = [Ab[:, 0:4096].rearrange("p (j n) -> p j n", j=4), Ab[:, 4096:8192].rearrange("p (j n) -> p j n", j=4)]
    HTb = [[S.buf("HT%d_%d" % (i, h)) for h in range(2)] for i in range(2)]
    SS = [A[:, 4096:4608], A[:, 4608:5120]]
    SSb = [S.buf("SS0"), S.buf("SS1")]
    XBc = Ab[:, 10240:12288]
    XBb = S.buf("XB")
    G = A[:, 6144:8192]
    Gb = S.buf("G")
    RING = [Bb[:, i * 8192:(i + 1) * 8192] for i in range(4)]
    RINGb = [S.buf("R%d" % i) for i in range(4)]
    Bt = B[:, 16384:18432]
    Btb = S.buf("Bt")
    ring_next = [0]

    def ring():
        i = ring_next[0] % 4
        ring_next[0] += 1
        return RING[i], RINGb[i]

    def make_xt(t):
        S.op("act", lambda e: e.activation(out=XBc, in_=X[:, t, :], func=AF.Copy), reads=[Xb[t]], writes=[XBb])
        for half in range(2):
            p, pb = psum()
            pv = p[:].bitcast(BF16).rearrange("p (k n) -> p k n", k=8)

            def tr(e, half=half, pv=pv):
                ins = None
                for k in range(8):
                    kt = half * 8 + k
                    ins = e.transpose(out=pv[:, k, :], in_=XBc[:, kt * 128:(kt + 1) * 128], identity=ident[:])
                return ins
            S.op("pe", tr, reads=[XBb, CONST], writes=[pb])
            S.op("dve", lambda e, half=half, pv=pv: e.tensor_copy(out=XT[:, half * 8:(half + 1) * 8, t * 128:(t + 1) * 128],
                                                                   in_=pv), reads=[pb], writes=[XTb[t]])

    def layer_norm(t, g_d, b_d, first):
        if first:
            S.dma("sp", G, g_d, writes=[Gb], key="G")
            S.dma("sp", Bt, b_d, writes=[Btb], key="Bt")
        xt_ = X[:, t, :]
        c = t * 4
        S.op("dve", lambda e: e.reduce_sum(out=ST8[:, c:c + 1], in_=xt_, axis=AX.X), reads=[Xb[t]], writes=[ST8b])
        S.op("dve", lambda e: e.tensor_scalar_mul(out=ST8[:, c:c + 1], in0=ST8[:, c:c + 1], scalar1=-1.0 / D),
             reads=[ST8b], writes=[ST8b])
        S.op("dve", lambda e: e.tensor_scalar_add(out=xt_, in0=xt_, scalar1=ST8[:, c:c + 1]), reads=[ST8b], writes=[Xb[t]])
        S.op("act", lambda e: e.activation(out=XBc, in_=xt_, func=AF.Square, accum_out=ST8[:, c + 1:c + 2]),
             reads=[Xb[t]], writes=[XBb, ST8b])
        S.op("dve", lambda e: e.tensor_scalar(out=ST8[:, c + 1:c + 2], in0=ST8[:, c + 1:c + 2], scalar1=1.0 / D,
                                              scalar2=LN_EPS, op0=ALU.mult, op1=ALU.add), reads=[ST8b], writes=[ST8b])
        S.op("act", lambda e: e.activation(out=ST8[:, c + 2:c + 3], in_=ST8[:, c + 1:c + 2], func=AF.Sqrt),
             reads=[ST8b], writes=[ST8b])
        S.op("dve", lambda e: e.reciprocal(out=ST8[:, c + 3:c + 4], in_=ST8[:, c + 2:c + 3]), reads=[ST8b], writes=[ST8b])
        S.op("dve", lambda e: e.scalar_tensor_tensor(out=xt_, in0=xt_, scalar=ST8[:, c + 3:c + 4], in1=G,
                                                     op0=ALU.mult, op1=ALU.mult), reads=[ST8b, Gb], writes=[Xb[t]])
        S.op("dve", lambda e: e.tensor_tensor(out=xt_, in0=xt_, in1=Bt, op=ALU.add), reads=[Btb], writes=[Xb[t]])

    def ffn(l, post=None):
        for t in range(NTT):
            S.op("dve", lambda e, t=t: e.tensor_scalar_mul(out=X[:, t, :], in0=X[:, t, :], scalar1=ALPHA),
                 reads=[XBb], writes=[Xb[t]])
        w1v = w1[l].rearrange("(kt p) n -> p kt n", p=128)
        w3v = w3[l].rearrange("(kt p) n -> p kt n", p=128)
        w2v = w2[l].rearrange("(j p) n -> p j n", p=128)

        def h_stage(c):
            r1, r1b = ring()
            S.dma("pool", r1.rearrange("p (k n) -> p k n", k=16), w1v[:, :, c * 512:(c + 1) * 512], writes=[r1b], key=r1b.name)
            r3, r3b = ring()
            S.dma("pool", r3.rearrange("p (k n) -> p k n", k=16), w3v[:, :, c * 512:(c + 1) * 512], writes=[r3b], key=r3b.name)
            r1v = r1.rearrange("p (k n) -> p k n", k=16)
            r3v = r3.rearrange("p (k n) -> p k n", k=16)
            hp = c % 2
            for j in range(4):
                for half in range(2):
                    p1, p1b = psum()
                    p3, p3b = psum()

                    def mm(e, rv, p):
                        ins = None
                        for kt in range(KT):
                            ins = e.matmul(p[:], lhsT=rv[:, kt, j * 128:(j + 1) * 128],
                                           rhs=XT[:, kt, half * 512:(half + 1) * 512], start=(kt == 0), stop=(kt == KT - 1))
                        return ins
                    xr = XTb[half * 4:(half + 1) * 4]
                    S.op("pe", lambda e: mm(e, r1v, p1), reads=[r1b] + xr, writes=[p1b])
                    S.op("pe", lambda e: mm(e, r3v, p3), reads=[r3b] + xr, writes=[p3b])
                    si = (j * 2 + half) % 2
                    S.op("act", lambda e: e.activation(out=SS[si], in_=p1[:], func=AF.Silu), reads=[p1b], writes=[SSb[si]])
                    S.op("dve", lambda e: e.tensor_tensor(out=HT[hp][:, j, half * 512:(half + 1) * 512], in0=p3[:], in1=SS[si],
                                                          op=ALU.mult), reads=[p3b, SSb[si]], writes=[HTb[hp][half]])

        def o_stage(c):
            r2, r2b = ring()
            r2v = r2.rearrange("p (j n) -> p j n", j=4)
            S.dma("pool", r2v, w2v[:, c * 4:(c + 1) * 4, :], writes=[r2b], key=r2b.name)
            hp = c % 2
            for t in range(NTT):
                for cb in range(4):
                    po, pob = psum()

                    def mm(e):
                        ins = None
                        for j in range(4):
                            ins = e.matmul(po[:], lhsT=HT[hp][:, j, t * 128:(t + 1) * 128], rhs=r2v[:, j, cb * 512:(cb + 1) * 512],
                                           start=(j == 0), stop=(j == 3))
                        return ins
                    S.op("pe", mm, reads=[r2b, HTb[hp][t // 4]], writes=[pob])
                    xs = X[:, t, cb * 512:(cb + 1) * 512]
                    S.op("dve", lambda e: e.scalar_tensor_tensor(out=xs, in0=po[:], scalar=0.5, in1=xs, op0=ALU.mult, op1=ALU.add),
                         reads=[pob], writes=[Xb[t]])
                if post is not None and c == NCH - 1:
                    post(t)

        h_stage(0)
        for c in range(NCH):
            if c + 1 < NCH:
                h_stage(c + 1)
            o_stage(c)

    def ln_stage_major(g_d, b_d, tag, do_xt, store=None):
        XB8 = Ab.rearrange("p (t n) -> p t n", t=NTT)
        XB8b = [S.buf("XB8%s_%d" % (tag, t)) for t in range(NTT)]
        STt = [S.buf("STt%s%d" % (tag, t)) for t in range(NTT)]
        gi = (ring_next[0] + 3) % 4
        G2 = RING[gi].bitcast(F32)[:, 0:D]
        S.dma("sp", G2, g_d, writes=[RINGb[gi]], key="G2")
        S.dma("sp", Bt, b_d, writes=[Btb], key="Bt")
        TT = range(NTT)
        for t in TT:
            S.op("act", lambda e, t=t: e.activation(out=XB8[:, t, :], in_=X[:, t, :], func=AF.Copy,
                                                    accum_out=ST8[:, t * 4:t * 4 + 1]), reads=[Xb[t]], writes=[XB8b[t], STt[t]])
        for t in TT:
            S.op("dve", lambda e, t=t: e.tensor_scalar_mul(out=ST8[:, t * 4:t * 4 + 1], in0=ST8[:, t * 4:t * 4 + 1],
                                                           scalar1=-1.0 / D), reads=[STt[t]], writes=[STt[t]])
        for t in TT:
            S.op("dve", lambda e, t=t: e.tensor_scalar_add(out=X[:, t, :], in0=X[:, t, :], scalar1=ST8[:, t * 4:t * 4 + 1]),
                 reads=[STt[t]], writes=[Xb[t]])
        for t in TT:
            S.op("act", lambda e, t=t: e.activation(out=XB8[:, t, :], in_=X[:, t, :], func=AF.Square,
                                                    accum_out=ST8[:, t * 4 + 1:t * 4 + 2]), reads=[Xb[t]], writes=[XB8b[t], STt[t]])
        for t in TT:
            S.op("dve", lambda e, t=t: e.tensor_scalar(out=ST8[:, t * 4 + 1:t * 4 + 2], in0=ST8[:, t * 4 + 1:t * 4 + 2],
                                                       scalar1=1.0 / D, scalar2=LN_EPS, op0=ALU.mult, op1=ALU.add),
                 reads=[STt[t]], writes=[STt[t]])
        for t in TT:
            S.op("act", lambda e, t=t: e.activation(out=ST8[:, t * 4 + 2:t * 4 + 3], in_=ST8[:, t * 4 + 1:t * 4 + 2], func=AF.Sqrt),
                 reads=[STt[t]], writes=[STt[t]])
        for t in TT:
            S.op("dve", lambda e, t=t: e.reciprocal(out=ST8[:, t * 4 + 3:t * 4 + 4], in_=ST8[:, t * 4 + 2:t * 4 + 3]),
                 reads=[STt[t]], writes=[STt[t]])
        for t in TT:
            S.op("dve", lambda e, t=t: e.scalar_tensor_tensor(out=X[:, t, :], in0=X[:, t, :], scalar=ST8[:, t * 4 + 3:t * 4 + 4],
                                                              in1=G2, op0=ALU.mult, op1=ALU.mult), reads=[STt[t], RINGb[gi]],
                 writes=[Xb[t]])
            S.op("dve", lambda e, t=t: e.tensor_tensor(out=X[:, t, :], in0=X[:, t, :], in1=Bt, op=ALU.add), reads=[Btb],
                 writes=[Xb[t]])
            if store is not None:
                store(t)
            if do_xt:
                S.op("act", lambda e, t=t: e.activation(out=XB8[:, t, :], in_=X[:, t, :], func=AF.Copy), reads=[Xb[t]],
                     writes=[XB8b[t]])
        if do_xt:
            for t in TT:
                for half in range(2):
                    p, pb = psum()
                    pv = p[:].bitcast(BF16).rearrange("p (k n) -> p k n", k=8)

                    def tr(e, t=t, half=half, pv=pv):
                        ins = None
                        for k in range(8):
                            kt = half * 8 + k
                            ins = e.transpose(out=pv[:, k, :], in_=XB8[:, t, kt * 128:(kt + 1) * 128], identity=ident[:])
                        return ins
                    S.op("pe", tr, reads=[XB8b[t], CONST], writes=[pb])
                    S.op("dve", lambda e, t=t, half=half, pv=pv: e.tensor_copy(
                        out=XT[:, half * 8:(half + 1) * 8, t * 128:(t + 1) * 128], in_=pv), reads=[pb], writes=[XTb[t]])
        S.barrier()

    for t in range(NTT):
        make_xt(t)
    if not skip_ffn:
        ffn(0)
        S.barrier()
        ln_stage_major(lng_d[0], lnb_d[0], "a", True)
    if debug:
        for t in range(NTT):
            S.dma("sp", dbg["x1"].rearrange("(t p) d -> p t d", p=128)[:, t, :], X[:, t, :], reads=[Xb[t]], key="dbgx1")
    S.barrier()
    if stop == "A":
        S._wait("sp", S.all_tokens())
        return nc

    mixer(nc, S, locals())

    S.barrier()
    if stop is not None and stop.startswith("B"):
        S._wait("sp", S.all_tokens())
        return nc
    YT = Ab.rearrange("p (k n) -> p k n", k=KT)
    YTb = S.buf("YTall")
    wov = wout_d.rearrange("(kt p) n -> p kt n", p=128)
    rs = []
    for i in range(4):
        r, rb = ring()
        rv = r.rearrange("p (k n) -> p k n", k=4)
        S.dma("pool", rv, wov[:, i * 4:(i + 1) * 4, :], writes=[rb], key=rb.name)
        rs.append((rv, rb, r))
    for t in range(NTT):
        S.op("dve", lambda e, t=t: e.tensor_scalar_mul(out=X[:, t, :], in0=X[:, t, :], scalar1=ALPHA), writes=[Xb[t]])
    for i in range(4):
        rv, rb, _ = rs[i]
        for t in range(NTT):
            for cb in range(4):
                po, pob = psum()

                def mm(e):
                    ins = None
                    for k in range(4):
                        ins = e.matmul(po[:], lhsT=YT[:, i * 4 + k, t * 128:(t + 1) * 128], rhs=rv[:, k, cb * 512:(cb + 1) * 512],
                                       start=(k == 0), stop=(k == 3))
                    return ins
                S.op("pe", mm, reads=[rb, YTb], writes=[pob])
                xs = X[:, t, cb * 512:(cb + 1) * 512]
                S.op("dve", lambda e: e.tensor_tensor(out=xs, in0=po[:], in1=xs, op=ALU.add), reads=[pob], writes=[Xb[t]])
    S.barrier()
    ln_stage_major(lng_d[1], lnb_d[1], "b", True)
    if debug:
        for t in range(NTT):
            S.dma("sp", dbg["x2"].rearrange("(t p) d -> p t d", p=128)[:, t, :], X[:, t, :], reads=[Xb[t]], key="dbgx2")

    ov = out_d.rearrange("(t p) d -> p t d", p=128)

    ffn(1)
    S.barrier()
    ln_stage_major(lng_d[2], lnb_d[2], "c", False,
                   store=lambda t: S.dma("sp", ov[:, t, :], X[:, t, :], reads=[Xb[t]], key="OUT"))
    S._wait("sp", S.all_tokens())
    return nc


def mixer(nc, S, L):
    X, XT, A, B, SM, ident, tri, ones, ST8 = (L[k] for k in ("X", "XT", "A", "B", "SM", "ident", "tri", "ones", "ST8"))
    XTb, CONST, psum, win_d, dbg, debug = (L[k] for k in ("XTb", "CONST", "psum", "win_d", "dbg", "debug"))
    halo_src, halo_dst, st_src, st_dst, groups = (L[k] for k in ("halo_src", "halo_dst", "st_src", "st_dst", "groups"))
    o_lb, o_hg, o_mg, o_cb, o_cw, o_gb, o_mk = (L[k] for k in ("o_lb", "o_hg", "o_mg", "o_cb", "o_cw", "o_gb", "o_mk"))
    Ab = A[:].bitcast(BF16)
    Bb = B[:].bitcast(BF16)
    YT = Ab.rearrange("p (k n) -> p k n", k=KT)
    YTb = S.buf("YT")
    wv = win_d.rearrange("(kt p) n -> p kt n", p=128)
    XTall = list(XTb)

    off = [0]

    def carve(n32):
        a = off[0]
        off[0] += n32
        assert off[0] <= 18432, off[0]
        return a
    WR = []
    for i in range(3):
        a = carve(1024)
        WR.append(B[:, a:a + 1024].bitcast(BF16).rearrange("p (k n) -> p k n", k=KT))
    WRb = [S.buf("W%d" % i) for i in range(3)]
    wr_next = [0]

    wr_n = [3]

    def wblock(col0, ncols=128):
        i = wr_next[0] % wr_n[0]
        wr_next[0] += 1
        S.dma("pool", WR[i][:, :, 0:ncols], wv[:, :, col0:col0 + ncols], writes=[WRb[i]], key=WRb[i].name)
        return WR[i], WRb[i]

    def f32(n):
        a = carve(n)
        return B[:, a:a + n]

    def b16(n):
        a = carve((n + 1) // 2)
        return B[:, a:a + (n + 1) // 2].bitcast(BF16)[:, 0:n]

    def proj_fm(col0, consume):
        w, wb = wblock(col0)
        for half in range(2):
            p, pb = psum()

            def mm(e):
                ins = None
                for kt in range(KT):
                    ins = e.matmul(p[:], lhsT=w[:, kt, :], rhs=XT[:, kt, half * 512:(half + 1) * 512],
                                   start=(kt == 0), stop=(kt == KT - 1))
                return ins
            S.op("pe", mm, reads=[wb] + XTall[half * 4:(half + 1) * 4], writes=[pb])
            consume(half, p, pb)

    BND = f32(64).rearrange("p (j b) -> p j b", j=16)
    XTB = b16(16 * 4).rearrange("p (k b) -> p k b", k=16)
    mark = off[0]

    FK = f32(1024); LOGF = f32(1024); BC = f32(1024); EE = f32(1024); QT = f32(1024)
    SQ16 = B[:, mark:mark + 2048].rearrange("p (c n) -> p c n", c=16)
    ON16 = B[:, mark + 2048:mark + 3072].bitcast(BF16).rearrange("p (c n) -> p c n", c=16)
    QA = [b16(1024), b16(1024)]; KA = [b16(1024), b16(1024)]; X3 = [b16(1024), b16(1024)]
    GG = b16(1024)
    VTb = b16(1024)
    bVT = [S.buf(), S.buf()]
    Vt = b16(16 * 128)
    OACC = f32(16 * 128)
    KALL = [EE.bitcast(BF16).rearrange("p (c n) -> p c n", c=16), QT.bitcast(BF16).rearrange("p (c n) -> p c n", c=16)]
    ATTA = [b16(16 * 64).rearrange("p (c n) -> p c n", c=16), b16(16 * 64).rearrange("p (c n) -> p c n", c=16)]
    SD = [f32(129), f32(129)]
    Sbf = [[b16(128), b16(128)] for _ in range(2)]
    EBL = [f32(16), f32(16)]
    RMASK = b16(1024); GIN = f32(4 * 129); DSEG = f32(4); RS = f32(64)
    Vv = Vt.rearrange("p (c n) -> p c n", c=16)
    Ov = OACC.rearrange("p (c n) -> p c n", c=16)
    GINv = GIN.rearrange("p (r n) -> p r n", r=4)
    bFK, bLOGF, bBC, bEE, bQT, bGG, bV, bO, bRM, bGIN, bDS, bRS = (S.buf() for _ in range(12))
    bQA = [S.buf(), S.buf()]; bKA = [S.buf(), S.buf()]; bX3 = [S.buf(), S.buf()]; bEBL = [S.buf(), S.buf()]
    bKALL = [[S.buf() for _ in range(4)] for _ in range(2)]; bATTA = [S.buf(), S.buf()]
    bSD = [S.buf(), S.buf()]; bSbf = [[S.buf(), S.buf()] for _ in range(2)]
    PSKV = [[L["PS"][4], L["PS"][5]], [L["PS"][6], L["PS"][7]]]
    bPSKV = [[L["PSb"][4], L["PSb"][5]], [L["PSb"][6], L["PSb"][7]]]
    ps_pool = L["ps_pool"]

    def hgrn_init():
        S.op("dve", lambda e: e.memset(Ov[0:64], 0.0), writes=[bO])
        S.op("dve", lambda e: e.memset(RMASK, 1.0), writes=[bRM])
        S.op("dve", lambda e: e.memset(RMASK.rearrange("p (c n) -> p c n", n=64)[:, :, 0:1], 0.0), writes=[bRM])
    S.op("dve", lambda e: e.tensor_tensor(out=SM[:, o_lb:o_lb + 16], in0=SM[:, o_lb:o_lb + 16], in1=SM[:, o_lb + 16:o_lb + 32],
                                          op=ALU.subtract), reads=[CONST], writes=[CONST])
    S.op("act", lambda e: e.activation(out=SM[:, o_lb:o_lb + 16], in_=SM[:, o_lb:o_lb + 16], func=AF.Sigmoid),
         reads=[CONST], writes=[CONST])
    S.op("dve", lambda e: e.tensor_scalar(out=SM[:, o_lb + 16:o_lb + 32], in0=SM[:, o_lb:o_lb + 16], scalar1=-1.0, scalar2=1.0,
                                          op0=ALU.mult, op1=ALU.add), reads=[CONST], writes=[CONST])

    def c3(ap):
        return ap.rearrange("p (c n) -> p c n", n=64)

    deferred = []

    def run_deferred():
        while deferred:
            deferred.pop(0)()

    def hgrn_head(h, mode):
        HW = 1024
        cq, cv, cg, cf = h * 128, HW + h * 128, 2 * HW + h * 128, [3 * HW + h * 128, 4 * HW + h * 128]
        vw = {}
        vstate = {"next": 0, "pend": None}
        NV = 6

        def v_evac():
            if vstate["pend"] is not None:
                k, p, pb = vstate["pend"]
                if k < 2:
                    S.op("act", lambda e: e.activation(out=VTb[:, k * 512:(k + 1) * 512], in_=p[:], func=AF.Copy), reads=[pb],
                         writes=[bVT[k]])
                else:
                    g = k - 2
                    pT = p[:].bitcast(BF16)
                    S.op("act", lambda e: e.activation(out=Vv[0:64, g * 4:(g + 1) * 4, :],
                                                       in_=pT[0:64, 0:512].rearrange("p (j n) -> p j n", j=4), func=AF.Copy),
                         reads=[pb], writes=[bV])
                vstate["pend"] = None

        def tick():
            if "w" not in vw:
                return
            v_evac()
            k = vstate["next"]
            if k >= NV:
                return
            vstate["next"] = k + 1
            p, pb = L["PS"][4 + k % 4], L["PSb"][4 + k % 4]
            if k < 2:
                w, wb = vw["w"]

                def mm(e):
                    ins = None
                    for kt in range(KT):
                        ins = e.matmul(p[:], lhsT=w[:, kt, :], rhs=XT[:, kt, k * 512:(k + 1) * 512], start=(kt == 0),
                                       stop=(kt == KT - 1))
                    return ins
                S.op("pe", mm, reads=[wb] + XTall[k * 4:(k + 1) * 4], writes=[pb])
            else:
                g = k - 2
                pT = p[:].bitcast(BF16)

                def trn(e):
                    ins = None
                    for j in range(4):
                        c = g * 4 + j
                        ins = e.transpose(out=pT[0:64, j * 128:(j + 1) * 128], in_=VTb[:, c * 64:(c + 1) * 64], identity=ident[:])
                    return ins
                S.op("pe", trn, reads=[bVT[g // 2], CONST], writes=[pb])
            vstate["pend"] = (k, p, pb)
        qg_pending = []
        for di in range(2):
            col = di * 8 + h
            proj_fm(cf[di], lambda half, p, pb: S.op("act", lambda e: e.activation(out=FK[:, half * 512:(half + 1) * 512], in_=p[:],
                                                                                    func=AF.Sigmoid), reads=[pb], writes=[bFK]))
            if di == 0:
                if mode == 2:
                    for (c0, dstT, dstb, bank0) in ((cq, QT, bQT, 4), (cg, GG, bGG, 6)):
                        wq, wqb = wblock(c0)
                        for half in range(2):
                            p, pb = L["PS"][bank0 + half], L["PSb"][bank0 + half]

                            def mm(e):
                                ins = None
                                for kt in range(KT):
                                    ins = e.matmul(p[:], lhsT=wq[:, kt, :], rhs=XT[:, kt, half * 512:(half + 1) * 512],
                                                   start=(kt == 0), stop=(kt == KT - 1))
                                return ins
                            S.op("pe", mm, reads=[wqb] + XTall[half * 4:(half + 1) * 4], writes=[pb])
                            qg_pending.append((p, pb, dstT, dstb, half))
                    run_deferred()
                else:
                    vw["w"] = wblock(cv)
            elif mode == 2:
                vw["w"] = wblock(cv)
            S.op("dve", lambda e: e.tensor_scalar(out=FK, in0=FK, scalar1=SM[:, o_lb + 16 + col:o_lb + 17 + col],
                                                  scalar2=SM[:, o_lb + col:o_lb + col + 1], op0=ALU.mult, op1=ALU.add),
                 reads=[CONST], writes=[bFK])
            S.op("act", lambda e: e.activation(out=LOGF, in_=FK, func=AF.Ln, accum_out=DSEG[:, di:di + 1]),
                 reads=[bFK], writes=[bLOGF, bDS])
            tick()
            S.op("dve", lambda e: e.tensor_scalar(out=FK, in0=FK, scalar1=-1.0, scalar2=1.0, op0=ALU.mult, op1=ALU.add),
                 reads=[bLOGF], writes=[bFK])
            S.op("dve", lambda e: e.tensor_tensor_scan(out=BC, data0=RMASK, data1=LOGF, initial=0.0, op0=ALU.mult, op1=ALU.add),
                 reads=[bRM, bLOGF], writes=[bBC])
            tick()
            S.op("act", lambda e: e.activation(out=EBL[di], in_=c3(BC)[:, :, 63], func=AF.Exp), reads=[bBC], writes=[bEBL[di]])
            tick()
            if di == 1:
                S.op("dve", lambda e: e.tensor_tensor(out=BC, in0=LOGF, in1=BC, op=ALU.subtract), reads=[bLOGF], writes=[bBC])
            ebl_b = EBL[di].unsqueeze(2).to_broadcast([128, 16, 64])
            S.op("act", lambda e: e.activation(out=EE, in_=BC, func=AF.Exp, scale=-1.0), reads=[bBC], writes=[bEE])
            S.op("dve", lambda e: e.tensor_tensor(out=KA[di], in0=FK, in1=EE, op=ALU.mult), reads=[bFK, bEE], writes=[bKA[di]])
            tick()
            if di == 0:
                S.op("dve", lambda e: e.tensor_tensor(out=c3(X3[0]), in0=c3(KA[0]), in1=ebl_b, op=ALU.mult),
                     reads=[bKA[0], bEBL[0]], writes=[bX3[0]])
            if mode == 2:
                S.op("act", lambda e: e.activation(out=LOGF, in_=BC, func=AF.Exp), reads=[bBC], writes=[bLOGF])
                if di == 0:
                    for (p, pb, dstT, dstb, half) in qg_pending:
                        S.op("act", lambda e: e.activation(out=dstT[:, half * 512:(half + 1) * 512], in_=p[:], func=AF.Silu),
                             reads=[pb], writes=[dstb])
                S.op("dve", lambda e: e.scalar_tensor_tensor(out=QA[di], in0=QT, scalar=128.0 ** -0.5, in1=LOGF, op0=ALU.mult,
                                                             op1=ALU.mult), reads=[bQT, bLOGF], writes=[bQA[di]])
                if di == 1:
                    S.op("dve", lambda e: e.tensor_tensor(out=c3(X3[1]), in0=c3(QA[1]), in1=ebl_b, op=ALU.mult),
                         reads=[bQA[1], bEBL[1]], writes=[bX3[1]])
            sk, so = di, h * 129
            Sst = SD[di][:, 0:128]
            if mode == 1:
                S.op("dve", lambda e: e.memset(Sst, 0.0), writes=[bSD[di]])
            else:
                S.dma("sp", GINv, st_dst[sk].rearrange("(r p) c -> p r c", p=128)[:, :, so:so + 129], reads=[STD[sk]], writes=[bGIN],
                      key="GIN")
                combine(GINv, 128, Sst, bSD[di], bGIN, di)
                S.op("act", lambda e: e.activation(out=Sbf[di][0], in_=Sst, func=AF.Copy), reads=[bSD[di]], writes=[bSbf[di][0]])
        KS = [X3[0], KA[1]]
        bKS = [bX3[0], bKA[1]]
        QI = [QA[0], X3[1]]
        bQI = [bQA[0], bX3[1]]
        pkv = [[None] * 16, [None] * 16]

        def transposes(di):
            alias_b = bEE if di == 0 else bQT
            gs = range(4) if di == 0 else range(3, -1, -1)
            for g in gs:
                p, pb = psum()
                pT = p[:].bitcast(BF16)

                def trn(e):
                    ins = None
                    for j in range(4):
                        c = g * 4 + j
                        ins = e.transpose(out=pT[0:64, j * 128:(j + 1) * 128], in_=KS[di][:, c * 64:(c + 1) * 64], identity=ident[:])
                    return ins
                S.op("pe", trn, reads=[bKS[di], CONST], writes=[pb])
                S.op("act", lambda e: e.activation(out=KALL[di][0:64, g * 4:(g + 1) * 4, :],
                                                   in_=pT[0:64, 0:512].rearrange("p (j n) -> p j n", j=4), func=AF.Copy),
                     reads=[pb], writes=[bKALL[di][g], alias_b])

        def stage_a(di, i):
            c = i if di == 0 else 15 - i
            pk = PSKV[di][i % 2][:, 0:128]
            pkb = bPSKV[di][i % 2]
            S.op("pe", lambda e: e.matmul(pk, lhsT=KALL[di][0:64, c, :], rhs=Vv[0:64, c, :], start=True, stop=True),
                 reads=[bKALL[di][c // 4], bV, (bEE if di == 0 else bQT)], writes=[pkb])
            pkv[di][i] = (pk, pkb)

        def attn_all(di):
            for g in range(2):
                p, pb = psum()

                def mm(e):
                    ins = None
                    for j in range(8):
                        c = g * 8 + j
                        cs = slice(c * 64, (c + 1) * 64)
                        ins = e.matmul(p[0:64, j * 64:(j + 1) * 64], lhsT=KA[di][:, cs], rhs=QA[di][:, cs], start=True, stop=True)
                    return ins
                S.op("pe", mm, reads=[bKA[di], bQA[di]], writes=[pb])
                S.op("dve", lambda e: e.tensor_tensor(out=ATTA[di][0:64, g * 8:(g + 1) * 8, :],
                                                      in0=p[0:64, 0:512].rearrange("p (j n) -> p j n", j=8),
                                                      in1=tri[di][0:64, 0:64].unsqueeze(1).to_broadcast([64, 8, 64]), op=ALU.mult),
                     reads=[pb, CONST], writes=[bATTA[di]])

        pend = [None, None]

        def flush_o(di):
            if pend[di] is not None:
                po, pob, c = pend[di]
                S.op("dve", lambda e: e.tensor_tensor(out=Ov[0:64, c, :], in0=po[0:64, 0:128], in1=Ov[0:64, c, :], op=ALU.add),
                     reads=[pob], writes=[bO])
                pend[di] = None

        def stage_b(di, i):
            c = i if di == 0 else 15 - i
            cs = slice(c * 64, (c + 1) * 64)
            Sst = SD[di][:, 0:128]
            sb_, sbb_ = Sbf[di][i % 2], bSbf[di][i % 2]
            pk, pkb = pkv[di][i]
            S.op("dve", lambda e: e.scalar_tensor_tensor(out=Sst, in0=Sst, scalar=EBL[di][:, c:c + 1], in1=pk,
                                                         op0=ALU.mult, op1=ALU.add), reads=[pkb, bEBL[di]], writes=[bSD[di]])
            if mode == 2:
                if i < 15:
                    nb_, nbb_ = Sbf[di][(i + 1) % 2], bSbf[di][(i + 1) % 2]
                    S.op("act", lambda e: e.activation(out=nb_, in_=Sst, func=AF.Copy), reads=[bSD[di]], writes=[nbb_])
                flush_o(di)
                po, pob = psum()

                def mm(e):
                    e.matmul(po[0:64, 0:128], lhsT=ATTA[di][0:64, c, :], rhs=Vv[0:64, c, :], start=True, stop=False)
                    return e.matmul(po[0:64, 0:128], lhsT=QI[di][:, cs], rhs=sb_, start=False, stop=True)
                S.op("pe", mm, reads=[bATTA[di], bV, bQI[di], sbb_], writes=[pob])
                pend[di] = (po, pob, c)

        while vstate["next"] < NV or vstate["pend"] is not None:
            tick()
        for di in range(2):
            transposes(di)
        if mode == 2:
            for di in range(2):
                attn_all(di)
        for di in range(2):
            stage_a(di, 0)
        for i in range(16):
            for di in range(2):
                if i + 1 < 16:
                    stage_a(di, i + 1)
                stage_b(di, i)
        for di in range(2):
            flush_o(di)
        if mode == 1:
            for di in range(2):
                sk, so = di, h * 129
                S.op("act", lambda e: e.activation(out=SD[di][:, 128:129], in_=DSEG[:, di:di + 1], func=AF.Exp), reads=[bDS],
                     writes=[bSD[di]])
                S.dma("sp", st_src[sk][:, so:so + 129], SD[di], reads=[bSD[di]], writes=[STS[sk]], key="STS%d" % sk)
        if mode == 2:
            S.op("dve", lambda e: e.tensor_tensor(out=SQ16[0:64], in0=Ov[0:64], in1=Ov[0:64], op=ALU.mult), reads=[bO, bEE],
                 writes=[bFK, bLOGF])
            S.op("dve", lambda e: e.tensor_reduce(out=RS[0:64, 0:16], in_=SQ16[0:64], axis=AX.X, op=ALU.add), reads=[bFK, bLOGF],
                 writes=[bRS])
            S.op("dve", lambda e: e.tensor_scalar(out=RS[0:64, 16:32], in0=RS[0:64, 0:16], scalar1=1.0 / 128, scalar2=NORM_EPS,
                                                  op0=ALU.mult, op1=ALU.add), reads=[bRS], writes=[bRS])
            S.op("act", lambda e: e.activation(out=RS[0:64, 32:48], in_=RS[0:64, 16:32], func=AF.Sqrt), reads=[bRS], writes=[bRS])
            S.op("dve", lambda e: e.reciprocal(out=RS[0:64, 48:64], in_=RS[0:64, 32:48]), reads=[bRS], writes=[bRS])
            S.op("dve", lambda e: e.tensor_tensor(out=ON16[0:64], in0=Ov[0:64],
                                                  in1=RS[0:64, 48:64].unsqueeze(2).to_broadcast([64, 16, 128]), op=ALU.mult),
                 reads=[bRS, bO], writes=[bBC])
            def epi_pe(h=h):
                p, pb = psum()
                pT = p[:].bitcast(BF16)

                def trn(e):
                    ins = None
                    for c in range(16):
                        ins = e.transpose(out=pT[:, c * 64:(c + 1) * 64], in_=ON16[0:64, c, :], identity=ident[0:64, 0:64])
                    return ins
                S.op("pe", trn, reads=[bBC, CONST], writes=[pb])
                S.op("dve", lambda e: e.scalar_tensor_tensor(out=YT[:, h, :], in0=pT[:, 0:1024], scalar=SM[:, o_hg + h:o_hg + h + 1],
                                                             in1=GG, op0=ALU.mult, op1=ALU.mult), reads=[pb, bGG, CONST], writes=[YTb])
            deferred.append(epi_pe)
            S.op("dve", lambda e: e.memset(Ov[0:64], 0.0), writes=[bO])

    def combine(Gv, n, dst, dstb, gb, di):
        mk = o_mk + (0 if di == 0 else 4)
        idx = [0, 1, 2] if di == 0 else [3, 2, 1]
        first = True
        for i in idx:
            m = SM[:, mk + i:mk + i + 1]
            if first:
                S.op("dve", lambda e: e.tensor_scalar_mul(out=dst, in0=Gv[:, i, 0:n], scalar1=m), reads=[gb, CONST], writes=[dstb])
                first = False
                continue
            om = SM[:, mk + 16 + i:mk + 17 + i]
            S.op("dve", lambda e: e.tensor_scalar(out=ST8[:, 41:42], in0=Gv[:, i, n:n + 1], scalar1=m, scalar2=om, op0=ALU.mult,
                                                  op1=ALU.add), reads=[gb, CONST], writes=[ST8b_])
            S.op("dve", lambda e: e.tensor_scalar_mul(out=dst, in0=dst, scalar1=ST8[:, 41:42]), reads=[ST8b_], writes=[dstb])
            S.op("dve", lambda e: e.scalar_tensor_tensor(out=dst, in0=Gv[:, i, 0:n], scalar=m, in1=dst, op0=ALU.mult, op1=ALU.add),
                 reads=[gb, CONST], writes=[dstb])

    ST8b_ = L["ST8b"]
    STS = [S.buf("STS%d" % k) for k in range(6)]
    STD = [S.buf("STD%d" % k) for k in range(6)]
    HLS = S.buf("HLS")
    HLD = S.buf("HLD")
    hg_end = off[0]

    off[0] = mark
    MW = 1024
    c_mq, c_mk, c_mv, c_mo, c_g = 5 * 1024, 5 * 1024 + MW, 5 * 1024 + 2 * MW, 5 * 1024 + 3 * MW, 5 * 1024 + 4 * MW
    WR.append(b16(KT * 128).rearrange("p (k n) -> p k n", k=KT))
    WRb.append(S.buf("W3"))
    mQT = b16(2 * 1024).rearrange("p (d n) -> p d n", d=2)
    mKTs = [YT[:, 2 * hh:2 * hh + 2, :] for hh in range(4)]
    bmKTs = [S.buf("mKT%d" % hh) for hh in range(4)]
    ZC = f32(1028); ACC = f32(1024)
    VE = b16(8 * 258).rearrange("p (t n) -> p t n", t=8)
    OT = b16(2 * 1024).rearrange("p (d n) -> p d n", d=2)
    NUM = [f32(8 * 257).rearrange("p (t n) -> p t n", t=8), f32(8 * 257).rearrange("p (t n) -> p t n", t=8)]
    HNb = NUM[1].rearrange("p t n -> p (t n)").bitcast(BF16)[:, 0:2048].rearrange("p (t n) -> p t n", t=8)
    RD = f32(64)
    GZ = f32(128).rearrange("p (t n) -> p t n", t=8)
    GB_ = f32(128).rearrange("p (t n) -> p t n", t=8)
    GD = f32(4 * 64).rearrange("p (k t n) -> p k t n", k=4, t=8)
    SCb = [b16(128), b16(128)]; KWb = [b16(256), b16(256)]
    Cst = f32(2 * 258).rearrange("p (d n) -> p d n", d=2)
    Cbf = [b16(2 * 258).rearrange("p (d n) -> p d n", d=2), b16(2 * 258).rearrange("p (d n) -> p d n", d=2)]
    MG = f32(4 * 258).rearrange("p (r n) -> p r n", r=4)
    HALO = f32(4 * 64).rearrange("p (r j b) -> p r j b", r=4, j=16)
    HLR = f32(64).rearrange("p (j b) -> p j b", j=16)
    (bmQT, bZC, bACC, bVE, bOT, bGZ, bGB, bGD, bC, bMG, bHALO, bHLR, bBND, bXTB, bRD) = (S.buf() for _ in range(15))
    bNUM = [S.buf(), S.buf()]; bSCb = [S.buf(), S.buf()]; bKWb = [S.buf(), S.buf()]; bCbf = [S.buf(), S.buf()]
    PCB = [[L["PS"][4], L["PS"][5]], [L["PS"][6], L["PS"][7]]]
    bPCB = [[L["PSb"][4], L["PSb"][5]], [L["PSb"][6], L["PSb"][7]]]
    assert off[0] <= 18432, off[0]

    def halo_exchange():
        S.op("dve", lambda e: e.tensor_copy(out=XTB[:, :, 0:2], in_=XT[:, :, 0:2]), reads=[XTall[0]], writes=[bXTB])
        S.op("dve", lambda e: e.tensor_copy(out=XTB[:, :, 2:4], in_=XT[:, :, 1022:1024]), reads=[XTall[7]], writes=[bXTB])
        for j in range(16):
            w, wb = wblock(c_mq + j * 128)
            p, pb = psum()

            def mm(e):
                ins = None
                for kt in range(KT):
                    ins = e.matmul(p[:, 0:4], lhsT=w[:, kt, :], rhs=XTB[:, kt, :], start=(kt == 0), stop=(kt == KT - 1))
                return ins
            S.op("pe", mm, reads=[wb, bXTB], writes=[pb])
            S.op("act", lambda e: e.activation(out=BND[:, j, :], in_=p[:, 0:4], func=AF.Copy), reads=[pb], writes=[bBND])
        S.dma("sp", halo_src.rearrange("p (j b) -> p j b", j=16), BND, reads=[bBND], writes=[HLS], key="HLS")
        S.cc(halo_src, halo_dst, groups, reads=[HLS], writes=[HLD], key="halo")

    def halo_select():
        S.dma("sp", HALO, halo_dst.rearrange("(r p) (j b) -> p r j b", p=128, j=16), reads=[HLD], writes=[bHALO], key="HALO")
        for side in range(2):
            mk = o_mk + 8 + side * 4
            src_lo = 2 if side == 0 else 0
            dst = HLR[:, :, side * 2:side * 2 + 2]
            for i in range(4):
                m = SM[:, mk + i:mk + i + 1]
                src = HALO[:, i, :, src_lo:src_lo + 2]
                if i == 0:
                    S.op("dve", lambda e: e.tensor_scalar_mul(out=dst, in0=src, scalar1=m), reads=[bHALO, CONST], writes=[bHLR])
                else:
                    S.op("dve", lambda e: e.scalar_tensor_tensor(out=dst, in0=src, scalar=m, in1=dst, op0=ALU.mult, op1=ALU.add),
                         reads=[bHALO, CONST], writes=[bHLR])

    def mlstm_gates():
        w, wb = wblock(c_g, 16)
        for t in range(NTT):
            p, pb = psum()

            def mm(e):
                ins = None
                for kt in range(KT):
                    ins = e.matmul(p[:, 0:16], lhsT=XT[:, kt, t * 128:(t + 1) * 128], rhs=w[:, kt, 0:16], start=(kt == 0),
                                   stop=(kt == KT - 1))
                return ins
            S.op("pe", mm, reads=[wb, XTall[t]], writes=[pb])
            S.op("dve", lambda e: e.tensor_tensor(out=GZ[:, t, :], in0=p[:, 0:16], in1=SM[:, o_gb:o_gb + 16], op=ALU.add),
                 reads=[pb, CONST], writes=[bGZ])
        S.op("act", lambda e: e.activation(out=GZ[:, :, 8:16], in_=GZ[:, :, 8:16], func=AF.Exp, scale=-1.0), reads=[bGZ], writes=[bGZ])
        S.op("act", lambda e: e.activation(out=GZ[:, :, 8:16], in_=GZ[:, :, 8:16], func=AF.Ln, bias=1.0), reads=[bGZ], writes=[bGZ])
        S.op("dve", lambda e: e.tensor_scalar_mul(out=GZ[:, :, 8:16], in0=GZ[:, :, 8:16], scalar1=-1.0), reads=[bGZ], writes=[bGZ])
        for t in range(NTT):
            p, pb = psum()

            def mm(e):
                e.matmul(p[:, 0:4], lhsT=tri[0][:], rhs=GZ[:, t, 8:12], start=True, stop=True)
                e.matmul(p[:, 4:8], lhsT=tri[1][:], rhs=GZ[:, t, 12:16], start=True, stop=True)
                return e.matmul(p[:, 8:16], lhsT=ones[:], rhs=GZ[:, t, 8:16], start=True, stop=True)
            S.op("pe", mm, reads=[bGZ, CONST], writes=[pb])
            S.op("act", lambda e: e.activation(out=GB_[:, t, :], in_=p[:, 0:16], func=AF.Copy), reads=[pb], writes=[bGB])
        S.op("dve", lambda e: e.tensor_tensor(out=GD[:, 0], in0=GZ[:, :, 0:8], in1=GB_[:, :, 0:8], op=ALU.subtract), reads=[bGZ, bGB],
             writes=[bGD])
        S.op("act", lambda e: e.activation(out=GD[:, 1], in_=GB_[:, :, 0:8], func=AF.Exp), reads=[bGB], writes=[bGD])
        S.op("dve", lambda e: e.tensor_tensor(out=GD[:, 2], in0=GD[:, 0], in1=GB_[:, :, 8:16], op=ALU.add), reads=[bGB], writes=[bGD])
        S.op("act", lambda e: e.activation(out=GD[:, 2], in_=GD[:, 2], func=AF.Exp), reads=[bGD], writes=[bGD])
        S.op("act", lambda e: e.activation(out=GD[:, 0], in_=GD[:, 0], func=AF.Exp), reads=[bGD], writes=[bGD])
        S.op("act", lambda e: e.activation(out=GD[:, 3], in_=GB_[:, :, 8:16], func=AF.Exp), reads=[bGB], writes=[bGD])

    conv_tick = [lambda: None]

    def conv_tile(col0, j, dstT, dstb, scale):
        proj_fm(col0, lambda half, p, pb: S.op("act", lambda e: e.activation(out=ZC[:, 2 + half * 512:2 + (half + 1) * 512], in_=p[:],
                                                                              func=AF.Copy), reads=[pb], writes=[bZC]))
        S.op("dve", lambda e: e.tensor_copy(out=ZC[:, 0:2], in_=HLR[:, j, 0:2]), reads=[bHLR], writes=[bZC])
        S.op("dve", lambda e: e.tensor_copy(out=ZC[:, 1026:1028], in_=HLR[:, j, 2:4]), reads=[bHLR], writes=[bZC])
        cw = o_cw + j * 5
        S.op("dve", lambda e: e.tensor_scalar(out=ACC, in0=ZC[:, 0:1024], scalar1=SM[:, cw:cw + 1], scalar2=SM[:, o_cb + j:o_cb + j + 1],
                                              op0=ALU.mult, op1=ALU.add), reads=[bZC, CONST], writes=[bACC])
        conv_tick[0]()
        for k in range(1, 5):
            S.op("dve", lambda e, k=k: e.scalar_tensor_tensor(out=ACC, in0=ZC[:, k:k + 1024], scalar=SM[:, cw + k:cw + k + 1], in1=ACC,
                                                              op0=ALU.mult, op1=ALU.add), reads=[bZC, CONST], writes=[bACC])
            conv_tick[0]()
        if scale == 1.0:
            S.op("act", lambda e: e.activation(out=dstT, in_=ACC, func=AF.Silu), reads=[bACC], writes=[dstb])
        else:
            S.op("act", lambda e: e.activation(out=ACC, in_=ACC, func=AF.Silu), reads=[bACC], writes=[bACC])
            S.op("dve", lambda e: e.tensor_scalar_mul(out=dstT, in0=ACC, scalar1=scale), reads=[bACC], writes=[dstb])

    def mlstm_head(h, mode):
        wa, wab = wblock(c_mv + h * 256)
        wb_, wbb = wblock(c_mv + h * 256 + 128)
        S.op("dve", lambda e: e.memset(VE[:, :, 256:257], 1.0), writes=[bVE])
        vstate = {"next": 0, "pend": None}

        def v_evac():
            if vstate["pend"] is not None:
                t, p, pb = vstate["pend"]
                S.op("act", lambda e: e.activation(out=VE[:, t, 0:256], in_=p[:, 0:256], func=AF.Copy), reads=[pb], writes=[bVE])
                vstate["pend"] = None

        def tick():
            v_evac()
            t = vstate["next"]
            if t >= NTT:
                return
            vstate["next"] = t + 1
            p, pb = L["PS"][4 + t % 4], L["PSb"][4 + t % 4]

            def mm(e):
                ins = None
                for (w, c0) in ((wa, 0), (wb_, 128)):
                    for kt in range(KT):
                        ins = e.matmul(p[:, c0:c0 + 128], lhsT=XT[:, kt, t * 128:(t + 1) * 128], rhs=w[:, kt, :], start=(kt == 0),
                                       stop=(kt == KT - 1))
                return ins
            S.op("pe", mm, reads=[wab, wbb, XTall[t]], writes=[pb])
            vstate["pend"] = (t, p, pb)
        conv_tick[0] = tick
        mKT, bmKT = mKTs[h], bmKTs[h]
        if mode == 1:
            for dt in range(2):
                conv_tile(c_mk + h * 256 + dt * 128, 8 + h * 2 + dt, mKT[:, dt, :], bmKT, 1.0)
        if mode == 2:
            for dt in range(2):
                conv_tile(c_mq + h * 256 + dt * 128, h * 2 + dt, mQT[:, dt, :], bmQT, 1.0 / 16)
            while vstate["next"] < NTT or vstate["pend"] is not None:
                tick()
            run_deferred()
            for dt in range(2):
                proj_fm(c_mo + h * 256 + dt * 128,
                        lambda half, p, pb: S.op("act", lambda e: e.activation(out=OT[:, dt, half * 512:(half + 1) * 512], in_=p[:],
                                                                                func=AF.Sigmoid), reads=[pb], writes=[bOT]))
                S.op("dve", lambda e: e.tensor_scalar_mul(out=OT[:, dt, :], in0=OT[:, dt, :],
                                                          scalar1=SM[:, o_mg + h * 2 + dt:o_mg + h * 2 + dt + 1]),
                     reads=[CONST], writes=[bOT])
        while vstate["next"] < NTT or vstate["pend"] is not None:
            tick()
        conv_tick[0] = lambda: None
        for di in range(2):
            g = di * 4 + h
            sk, so = 2 + di * 2 + h // 2, (h % 2) * 516
            if mode == 1:
                S.op("dve", lambda e: e.memset(Cst, 0.0), writes=[bC])
            else:
                for dt in range(2):
                    S.dma("sp", MG, st_dst[sk].rearrange("(r p) c -> p r c", p=128)[:, :, so + dt * 258:so + dt * 258 + 258],
                          reads=[STD[sk]], writes=[bMG], key="MG")
                    combine(MG, 257, Cst[:, dt, 0:257], bC, bMG, di)
                S.op("act", lambda e: e.activation(out=Cbf[0][:, :, 0:257], in_=Cst[:, :, 0:257], func=AF.Copy), reads=[bC],
                     writes=[bCbf[0]])
            order = list(range(NTT)) if di == 0 else list(range(NTT - 1, -1, -1))

            def stA(i):
                t = order[i]
                ts = slice(t * 128, (t + 1) * 128)
                if mode == 2:
                    ps_, psb = psum()

                    def mm(e):
                        e.matmul(ps_[:, 0:128], lhsT=mKT[:, 0, ts], rhs=mQT[:, 0, ts], start=True, stop=False)
                        return e.matmul(ps_[:, 0:128], lhsT=mKT[:, 1, ts], rhs=mQT[:, 1, ts], start=False, stop=True)
                    S.op("pe", mm, reads=[bmKT, bmQT], writes=[psb])
                    S.op("dve", lambda e: e.scalar_tensor_tensor(out=SCb[i % 2], in0=ps_[:, 0:128], scalar=GD[:, 0, t, g:g + 1],
                                                                 in1=tri[di][:], op0=ALU.mult, op1=ALU.mult),
                         reads=[psb, bGD, CONST], writes=[bSCb[i % 2]])
                pt, ptb = psum()
                pT = pt[:].bitcast(BF16)

                def trn(e):
                    e.transpose(out=pT[:, 0:128], in_=mKT[:, 0, ts], identity=ident[:])
                    return e.transpose(out=pT[:, 128:256], in_=mKT[:, 1, ts], identity=ident[:])
                S.op("pe", trn, reads=[bmKT, CONST], writes=[ptb])
                S.op("act", lambda e: e.activation(out=KWb[i % 2], in_=pT[:, 0:256], func=AF.Copy, scale=GD[:, 2, t, g:g + 1]),
                     reads=[ptb, bGD], writes=[bKWb[i % 2]])
                for dt in range(2):
                    S.op("pe", lambda e: e.matmul(PCB[i % 2][dt][:, 0:257], lhsT=KWb[i % 2][:, dt * 128:(dt + 1) * 128],
                                                  rhs=VE[:, t, 0:257], start=True, stop=True), reads=[bKWb[i % 2], bVE],
                         writes=[bPCB[i % 2][dt]])

            def stB(i):
                t = order[i]
                ts = slice(t * 128, (t + 1) * 128)
                for dt in range(2):
                    S.op("dve", lambda e: e.scalar_tensor_tensor(out=Cst[:, dt, 0:257], in0=Cst[:, dt, 0:257],
                                                                 scalar=GD[:, 3, t, g:g + 1], in1=PCB[i % 2][dt][:, 0:257],
                                                                 op0=ALU.mult, op1=ALU.add), reads=[bPCB[i % 2][dt], bGD],
                         writes=[bC])
                if mode == 2:
                    if i < NTT - 1:
                        S.op("act", lambda e: e.activation(out=Cbf[(i + 1) % 2][:, :, 0:257], in_=Cst[:, :, 0:257], func=AF.Copy),
                             reads=[bC], writes=[bCbf[(i + 1) % 2]])
                    po, pob = psum()
                    cb = Cbf[i % 2]

                    def mm2(e):
                        e.matmul(po[:, 0:257], lhsT=SCb[i % 2], rhs=VE[:, t, 0:257], start=True, stop=False)
                        e.matmul(po[:, 0:257], lhsT=mQT[:, 0, ts], rhs=cb[:, 0, 0:257], start=False, stop=False)
                        return e.matmul(po[:, 0:257], lhsT=mQT[:, 1, ts], rhs=cb[:, 1, 0:257], start=False, stop=True)
                    S.op("pe", mm2, reads=[bSCb[i % 2], bVE, bmQT, bCbf[i % 2]], writes=[pob])
                    S.op("act", lambda e: e.activation(out=NUM[di][:, t, :], in_=po[:, 0:257], func=AF.Copy,
                                                       scale=GD[:, 1, t, g:g + 1]), reads=[pob, bGD], writes=[bNUM[di]])

            stA(0)
            for i in range(NTT):
                if i + 1 < NTT:
                    stA(i + 1)
                stB(i)
            if mode == 1:
                S.op("dve", lambda e: e.tensor_reduce(out=ST8[:, 48:49], in_=GB_[:, :, 8 + g], axis=AX.X, op=ALU.add), reads=[bGB],
                     writes=[ST8b_])
                for dt in range(2):
                    S.op("act", lambda e: e.activation(out=Cst[:, dt, 257:258], in_=ST8[:, 48:49], func=AF.Exp), reads=[ST8b_],
                         writes=[bC])
                S.dma("sp", st_src[sk][:, so:so + 516].rearrange("p (d n) -> p d n", d=2), Cst, reads=[bC], writes=[STS[sk]],
                      key="STS%d" % sk)
        if mode == 2:
            for di in range(2):
                c0 = di * 8
                S.op("act", lambda e: e.activation(out=RD[:, c0:c0 + 8], in_=NUM[di][:, :, 256], func=AF.Abs), reads=[bNUM[di]],
                     writes=[bRD])
                S.op("dve", lambda e: e.tensor_scalar_max(out=RD[:, c0:c0 + 8], in0=RD[:, c0:c0 + 8], scalar1=1.0), reads=[bRD],
                     writes=[bRD])
                S.op("dve", lambda e: e.reciprocal(out=RD[:, c0:c0 + 8], in_=RD[:, c0:c0 + 8]), reads=[bRD], writes=[bRD])
                S.op("dve", lambda e: e.tensor_tensor(out=NUM[di][:, :, 0:256], in0=NUM[di][:, :, 0:256],
                                                      in1=RD[:, c0:c0 + 8].unsqueeze(2).to_broadcast([128, 8, 256]), op=ALU.mult),
                     reads=[bRD], writes=[bNUM[di]])
            Hv = NUM[0][:, :, 0:256]
            S.op("dve", lambda e: e.tensor_tensor(out=Hv, in0=Hv, in1=NUM[1][:, :, 0:256], op=ALU.add), reads=[bNUM[1]],
                 writes=[bNUM[0]])
            S.op("dve", lambda e: e.tensor_reduce(out=RD[:, 16:24], in_=Hv, axis=AX.X, op=ALU.add), reads=[bNUM[0]], writes=[bRD])
            S.op("dve", lambda e: e.tensor_scalar_mul(out=RD[:, 16:24], in0=RD[:, 16:24], scalar1=-1.0 / 256), reads=[bRD],
                 writes=[bRD])
            S.op("dve", lambda e: e.tensor_tensor(out=Hv, in0=Hv, in1=RD[:, 16:24].unsqueeze(2).to_broadcast([128, 8, 256]),
                                                  op=ALU.add), reads=[bRD], writes=[bNUM[0]])
            S.op("dve", lambda e: e.tensor_tensor(out=NUM[1][:, :, 0:256], in0=Hv, in1=Hv, op=ALU.mult), reads=[bNUM[0]],
                 writes=[bNUM[1]])
            S.op("dve", lambda e: e.tensor_reduce(out=RD[:, 24:32], in_=NUM[1][:, :, 0:256], axis=AX.X, op=ALU.add), reads=[bNUM[1]],
                 writes=[bRD])
            S.op("dve", lambda e: e.tensor_scalar(out=RD[:, 24:32], in0=RD[:, 24:32], scalar1=1.0 / 256, scalar2=NORM_EPS,
                                                  op0=ALU.mult, op1=ALU.add), reads=[bRD], writes=[bRD])
            S.op("act", lambda e: e.activation(out=RD[:, 32:40], in_=RD[:, 24:32], func=AF.Sqrt), reads=[bRD], writes=[bRD])
            S.op("dve", lambda e: e.reciprocal(out=RD[:, 40:48], in_=RD[:, 32:40]), reads=[bRD], writes=[bRD])
            S.op("dve", lambda e: e.tensor_tensor(out=HNb, in0=Hv, in1=RD[:, 40:48].unsqueeze(2).to_broadcast([128, 8, 256]),
                                                  op=ALU.mult), reads=[bRD, bNUM[0]], writes=[bNUM[1]])
            def epi_pe(h=h):
                for dt in range(2):
                    p, pb = psum()
                    pT = p[:].bitcast(BF16)

                    def trn(e):
                        ins = None
                        for t in range(NTT):
                            ins = e.transpose(out=pT[:, t * 128:(t + 1) * 128], in_=HNb[:, t, dt * 128:(dt + 1) * 128],
                                              identity=ident[:])
                        return ins
                    S.op("pe", trn, reads=[bNUM[1], CONST], writes=[pb])
                    S.op("dve", lambda e: e.tensor_tensor(out=YT[:, 8 + h * 2 + dt, :], in0=pT[:, 0:1024], in1=OT[:, dt, :],
                                                          op=ALU.mult), reads=[pb, bOT], writes=[YTb])
            deferred.append(epi_pe)

    stop = L["stop"]
    if stop != "B0":
        halo_exchange()
    if stop == "B1":
        return
    hgrn_init()
    ps_pool[0] = 4
    for h in range(8):
        hgrn_head(h, 1)
    ps_pool[0] = 8
    for k in range(2):
        S.cc(st_src[k], st_dst[k], groups, reads=[STS[k]], writes=[STD[k]], key="st%d" % k)
    S.barrier()
    if stop in ("B2", "B0"):
        return
    halo_select()
    mlstm_gates()
    if stop == "B3":
        return
    ps_pool[0] = 4
    wr_n[0] = 4
    for h in range(4):
        mlstm_head(h, 1)
        if h % 2 == 1:
            for di in range(2):
                k = 2 + di * 2 + h // 2
                S.cc(st_src[k], st_dst[k], groups, reads=[STS[k]], writes=[STD[k]], key="st%d" % k)
    if stop == "B5":
        return
    for h in range(4):
        mlstm_head(h, 2)
    run_deferred()
    wr_n[0] = 3
    S.barrier()
    hgrn_init()
    ps_pool[0] = 4
    for h in range(8):
        hgrn_head(h, 2)
    run_deferred()
    ps_pool[0] = 8
    if debug:
        S.barrier()
        DBG = B[:, 0:16384]
        for half in range(2):
            S.op("dve", lambda e: e.tensor_copy(out=DBG[:, 0:8192], in_=Ab[:, half * 8192:(half + 1) * 8192]), writes=[bFK])
            S.dma("sp", dbg["yT"][:, half * 8192:(half + 1) * 8192], DBG[:, 0:8192], reads=[bFK], writes=[bFK], key="dbgy")


def _small(inputs, core):
    r = core % 4
    sm = np.zeros((128, 192), np.float32)
    lb = np.asarray(inputs["hgrn_lb"], np.float32)
    for di in range(2):
        for h in range(8):
            sm[:, 0 + di * 8 + h] = lb[di, 0, h * 128:(h + 1) * 128]
            sm[:, 16 + di * 8 + h] = lb[di, 1, h * 128:(h + 1) * 128]
    sm[:, 32:40] = np.asarray(inputs["hgrn_norm_g"], np.float32).reshape(8, 128).T
    sm[:, 40:48] = np.asarray(inputs["mlstm_norm_g"], np.float32).reshape(8, 128).T
    sm[:, 48:64] = np.asarray(inputs["mlstm_conv_b"], np.float32).reshape(16, 128).T
    cw = np.asarray(inputs["mlstm_conv_w"], np.float32).reshape(5, 16, 128)
    sm[:, 64:144] = cw.transpose(2, 1, 0).reshape(128, 80)
    ig = np.asarray(inputs["mlstm_ig_b"], np.float32).reshape(2, 4)
    fg = np.asarray(inputs["mlstm_fg_b"], np.float32).reshape(2, 4)
    sm[:, 144:160] = np.concatenate([ig[0], ig[1], fg[0], fg[1]])[None, :]
    for i in range(4):
        sm[:, 160 + i] = 1.0 if i < r else 0.0
        sm[:, 164 + i] = 1.0 if i > r else 0.0
        sm[:, 168 + i] = 1.0 if i == r - 1 else 0.0
        sm[:, 172 + i] = 1.0 if i == r + 1 else 0.0
        sm[:, 176 + i] = 0.0 if i < r else 1.0
        sm[:, 180 + i] = 0.0 if i > r else 1.0
    return sm


def _in_maps(inputs):
    f = lambda a: np.ascontiguousarray(np.asarray(a, np.float32))
    x = f(inputs["x"]).reshape(8, NT, D)
    shared = {
        "ffn1_w1": f(inputs["ffn1_w1"]).reshape(D, DFF), "ffn1_w3": f(inputs["ffn1_w3"]).reshape(D, DFF),
        "ffn1_w2": f(inputs["ffn1_w2"]).reshape(DFF, D), "ffn2_w1": f(inputs["ffn2_w1"]).reshape(D, DFF),
        "ffn2_w3": f(inputs["ffn2_w3"]).reshape(D, DFF), "ffn2_w2": f(inputs["ffn2_w2"]).reshape(DFF, D),
        "w_in": f(inputs["w_in"]).reshape(D, INC), "w_out": f(inputs["w_out"]).reshape(D, D),
    }
    lnp = {"ln1_g": inputs["ln1_g"], "ln1_b": inputs["ln1_b"], "ln2_g": inputs["ln2_g"], "ln2_b": inputs["ln2_b"],
           "ln3_g": inputs["ln3_g"], "ln3_b": inputs["ln3_b"]}
    for k, v in lnp.items():
        shared[k] = np.ascontiguousarray(np.broadcast_to(f(v).reshape(1, D), (128, D)))
    maps = []
    for c in range(8):
        m = dict(shared)
        m["x"] = x[c]
        m["small"] = _small(inputs, c)
        maps.append(m)
    return maps


_NC_CACHE = {}


def kernel(**inputs):
    if "nc" not in _NC_CACHE:
        _NC_CACHE["nc"] = build(False)
    nc = _NC_CACHE["nc"]
    res = run_bass_kernel_spmd(nc, _in_maps(inputs), core_ids=list(range(8)))
    out = np.stack([np.asarray(r["out"], np.float32) for r in res.results], axis=0)
    return out.reshape(2, 4096, D)
```
